# Optimizing a Trainium2 kernel written in Bass

```python
import numpy as np
import jax
import jax.numpy as jnp
from jax import lax

D_MODEL = 1024
BATCH = 32
SEQ = 256
DEPTH = 2
DEC_BATCH = 2
DEC_SEQ = 4096
PAST_LEN = 512

GRID_W = 64
HEAD_DIM = 64
N_BRANCH = 4
BRANCH_W = D_MODEL // N_BRANCH
A_HQ = BRANCH_W // HEAD_DIM
A_HKV = A_HQ // 2
WINDOW = 128
Q_BLOCK = 128
LRU_W = BRANCH_W
LRU_BLOCKS = 4
LRU_BW = LRU_W // LRU_BLOCKS
LRU_C = 8.0
CONV_W = 4
CONV_LEFT = CONV_W // 2
GDN_H = 4
GDN_DK = BRANCH_W // GDN_H
GDN_DV = BRANCH_W // GDN_H
CHUNK = 64
D_HQ = BRANCH_W // HEAD_DIM
D_HKV = D_HQ // 2
D_FF = 4 * D_MODEL
ROPE_BASE = 10000.0
EPS = 1e-6
NEG = -1e30
IN_WIDTHS = (A_HQ * HEAD_DIM, A_HKV * HEAD_DIM, A_HKV * HEAD_DIM,
             LRU_W, LRU_W,
             GDN_H * GDN_DK, GDN_H * GDN_DK, GDN_H * GDN_DV, GDN_H * GDN_DV, 2 * GDN_H, 2 * GDN_H,
             D_HQ * HEAD_DIM, D_HKV * HEAD_DIM, D_HKV * HEAD_DIM)
IN_COLS = sum(IN_WIDTHS)

kernel_name = "hybrid_diffusion_prefix_step"


def split_points():
    return tuple(int(s) for s in np.cumsum(IN_WIDTHS)[:-1])


def rms_norm(x, g):
    xf = x.astype(jnp.float32)
    y = xf * lax.rsqrt(jnp.mean(xf * xf, axis=-1, keepdims=True) + EPS)
    return (y * g.astype(jnp.float32)).astype(x.dtype)


def l2_normalize(x):
    return x * lax.rsqrt(jnp.sum(x * x, axis=-1, keepdims=True) + EPS)


def modulate(x, g, shift, scale):
    return rms_norm(x, g) * (1 + scale[:, None, :]) + shift[:, None, :]


def centred_dwconv(x, w):
    T = x.shape[1]
    xp = jnp.pad(x, ((0, 0), (CONV_LEFT, CONV_W - 1 - CONV_LEFT), (0, 0)))
    return sum(xp[:, j:j + T] * w[j] for j in range(CONV_W))


def axial_rope(T):
    rows = T // GRID_W
    row = jnp.repeat(jnp.arange(rows, dtype=jnp.float32), GRID_W)
    col = jnp.tile(jnp.arange(GRID_W, dtype=jnp.float32), rows)
    n_freq = HEAD_DIM // 4
    inv = ROPE_BASE ** (-jnp.arange(n_freq, dtype=jnp.float32) / n_freq)
    ang = jnp.stack([row[:, None] * inv, col[:, None] * inv], axis=1)
    return jnp.cos(ang), jnp.sin(ang)


def apply_rope(x, cos, sin):
    T = x.shape[1]
    shp = (T,) + (1,) * (x.ndim - 3) + cos.shape[1:]
    c = cos.reshape(shp)
    s = sin.reshape(shp)
    xr = x.astype(jnp.float32).reshape(x.shape[:-1] + (2, 2, HEAD_DIM // 4))
    x1 = xr[..., 0, :]
    x2 = xr[..., 1, :]
    out = jnp.stack([x1 * c - x2 * s, x2 * c + x1 * s], axis=-2)
    return out.reshape(x.shape).astype(x.dtype)


def sink_softmax(s, sink):
    if sink is None:
        return jax.nn.softmax(s, axis=-1)
    sk = sink.astype(jnp.float32)[:, :, None, None]
    m = jnp.maximum(jnp.max(s, axis=-1, keepdims=True), sk)
    e = jnp.exp(s - m)
    return e / (jnp.sum(e, axis=-1, keepdims=True) + jnp.exp(sk - m))


def attn_heads(q, k, v, qg, kg, hq, hkv):
    B, T, _ = q.shape
    q = rms_norm(q.reshape(B, T, hkv, hq // hkv, HEAD_DIM), qg)
    k = rms_norm(k.reshape(B, T, hkv, HEAD_DIM), kg)
    return q, k, v.reshape(B, T, hkv, HEAD_DIM)


def blocked_attention(q, k, v, sink):
    B, T, HKV, G, _ = q.shape
    nb = T // Q_BLOCK
    qb = jnp.moveaxis(q.reshape(B, nb, Q_BLOCK, HKV, G, HEAD_DIM), 1, 0)
    scale = HEAD_DIM ** -0.5

    def one_block(qi):
        s = jnp.einsum("bqhgd,bkhd->bhgqk", qi, k).astype(jnp.float32) * scale
        p = sink_softmax(s, sink).astype(v.dtype)
        return jnp.einsum("bhgqk,bkhd->bqhgd", p, v)

    o = lax.map(one_block, qb)
    return jnp.moveaxis(o, 0, 1).reshape(B, T, HKV * G * HEAD_DIM)


def banded_attention(q, k, v, ck, cv, sink):
    B, T, HKV, G, _ = q.shape
    nb = T // WINDOW
    qb = q.reshape(B, nb, WINDOW, HKV, G, HEAD_DIM)

    def band(t):
        tp = jnp.pad(t, ((0, 0), (WINDOW, WINDOW), (0, 0), (0, 0))).reshape(B, nb + 2, WINDOW, HKV, HEAD_DIM)
        return jnp.concatenate([tp[:, :-2], tp[:, 1:-1], tp[:, 2:]], axis=2)

    kb, vb = band(k), band(v)
    qpos = jnp.arange(nb)[:, None] * WINDOW + jnp.arange(WINDOW)[None]
    kpos = jnp.arange(nb)[:, None] * WINDOW - WINDOW + jnp.arange(3 * WINDOW)[None]
    valid = ((kpos >= 0) & (kpos < T))[:, None, :]
    mask = valid & (jnp.abs(qpos[:, :, None] - kpos[:, None, :]) <= WINDOW)
    scale = HEAD_DIM ** -0.5
    s_loc = jnp.einsum("bnqhgd,bnkhd->bnhgqk", qb, kb).astype(jnp.float32) * scale
    s_loc = jnp.where(mask[None, :, None, None], s_loc, NEG)
    s_ctx = jnp.einsum("bnqhgd,bkhd->bnhgqk", qb, ck).astype(jnp.float32) * scale
    p = sink_softmax(jnp.concatenate([s_ctx, s_loc], axis=-1), sink).astype(v.dtype)
    C = ck.shape[1]
    o = (jnp.einsum("bnhgqk,bkhd->bnqhgd", p[..., :C], cv)
         + jnp.einsum("bnhgqk,bnkhd->bnqhgd", p[..., C:], vb))
    return o.reshape(B, T, HKV * G * HEAD_DIM)


def rglru_direction(x, wr, br, wi, bi, lam, h0):
    B, T, _ = x.shape
    xb = x.reshape(B, T, LRU_BLOCKS, LRU_BW)
    r = jax.nn.sigmoid(jnp.einsum("btnc,ncd->btnd", xb, wr).reshape(B, T, LRU_W) + br)
    i = jax.nn.sigmoid(jnp.einsum("btnc,ncd->btnd", xb, wi).reshape(B, T, LRU_W) + bi)
    log_a = -LRU_C * r * jax.nn.softplus(-lam.astype(jnp.float32))
    a = jnp.exp(log_a)
    b = jnp.sqrt(-jnp.expm1(2.0 * log_a)) * (i * x)
    a_cum, h = lax.associative_scan(lambda e1, e2: (e1[0] * e2[0], e2[0] * e1[1] + e2[1]), (a, b), axis=1)
    h = h + a_cum * h0.astype(jnp.float32)[:, None, :]
    return h, h[:, -1]


def rglru_mixer(xr, gate, lp, st0):
    x = (centred_dwconv(xr, lp["lru_conv_w"]) + lp["lru_conv_b"]).astype(jnp.float32)
    hf, sf = rglru_direction(x, lp["lru_wr"][0], lp["lru_br"][0], lp["lru_wi"][0], lp["lru_bi"][0],
                             lp["lru_lam"][0], st0[:, 0])
    hb, sb = rglru_direction(jnp.flip(x, 1), lp["lru_wr"][1], lp["lru_br"][1], lp["lru_wi"][1],
                             lp["lru_bi"][1], lp["lru_lam"][1], st0[:, 1])
    y = (hf + jnp.flip(hb, 1)) * jax.nn.gelu(gate.astype(jnp.float32))
    return y.astype(xr.dtype), jnp.stack([sf, sb], axis=1)


def gdn_chunked(q, k, v, g, beta, s0):
    B, T, H, DK = q.shape
    DV = v.shape[-1]
    N = T // CHUNK

    def chunks(t):
        return jnp.moveaxis(t.reshape((B, N, CHUNK, H) + t.shape[3:]), 3, 1)

    q, k, v, g, beta = chunks(q), chunks(k), chunks(v), chunks(g), chunks(beta)
    gc = jnp.cumsum(g, axis=-1)
    idx = jnp.arange(CHUNK)
    incl = idx[:, None] >= idx[None, :]
    strict = idx[:, None] > idx[None, :]
    decay = jnp.exp(jnp.where(incl, gc[..., :, None] - gc[..., None, :], -jnp.inf))
    kk = jnp.einsum("bhnid,bhnjd->bhnij", k, k)
    lower = jnp.eye(CHUNK, dtype=jnp.float32) + jnp.where(strict, beta[..., None] * kk * decay, 0.0)
    rhs = jnp.concatenate([v * beta[..., None], k * (beta * jnp.exp(gc))[..., None]], axis=-1)
    sol = lax.linalg.triangular_solve(lower, rhs, left_side=True, lower=True)
    u, w = sol[..., :DV], sol[..., DV:]
    qk = jnp.einsum("bhnid,bhnjd->bhnij", q, k) * decay

    def step(S, xs):
        q_i, k_i, u_i, w_i, qk_i, gc_i = xs
        v_new = u_i - jnp.einsum("bhck,bhkv->bhcv", w_i, S)
        o = (jnp.einsum("bhck,bhkv->bhcv", q_i * jnp.exp(gc_i)[..., None], S)
             + jnp.einsum("bhij,bhjv->bhiv", qk_i, v_new))
        g_last = gc_i[..., -1:]
        S = (S * jnp.exp(g_last)[..., None]
             + jnp.einsum("bhck,bhcv->bhkv", k_i * jnp.exp(g_last - gc_i)[..., None], v_new))
        return S, o

    xs = tuple(jnp.moveaxis(t, 2, 0) for t in (q, k, u, w, qk, gc))
    S, o = lax.scan(step, s0.astype(jnp.float32), xs)
    o = jnp.moveaxis(jnp.moveaxis(o, 0, 2), 1, 3).reshape(B, T, H, DV)
    return o, S


def gdn_mixer(q, k, v, z, a, bb, lp, s0):
    B, T, _ = q.shape
    qkv = jax.nn.silu(centred_dwconv(jnp.concatenate([q, k, v], axis=-1), lp["gdn_conv_w"]).astype(jnp.float32))
    q, k, v = jnp.split(qkv, (GDN_H * GDN_DK, 2 * GDN_H * GDN_DK), axis=-1)
    q = l2_normalize(q.reshape(B, T, GDN_H, GDN_DK)) * (GDN_DK ** -0.5)
    k = l2_normalize(k.reshape(B, T, GDN_H, GDN_DK))
    v = v.reshape(B, T, GDN_H, GDN_DV)
    a = a.astype(jnp.float32).reshape(B, T, 2, GDN_H)
    g = -jnp.exp(lp["gdn_a_log"].astype(jnp.float32)) * jax.nn.softplus(a + lp["gdn_dt_bias"])
    beta = jax.nn.sigmoid(bb.astype(jnp.float32).reshape(B, T, 2, GDN_H))
    of, sf = gdn_chunked(q, k, v, g[:, :, 0], beta[:, :, 0], s0[:, 0])
    fl = lambda t: jnp.flip(t, 1)
    ob, sb = gdn_chunked(fl(q), fl(k), fl(v), fl(g[:, :, 1]), fl(beta[:, :, 1]), s0[:, 1])
    o = rms_norm(of + fl(ob), lp["gdn_norm_g"]) * jax.nn.silu(z.astype(jnp.float32).reshape(B, T, GDN_H, GDN_DV))
    return o.reshape(B, T, BRANCH_W).astype(z.dtype), jnp.stack([sf, sb], axis=1)


def token_mixing(h, lp, ctx, rope):
    B, T, _ = h.shape
    (aq, ak, av, lx, lg, gq, gk, gv, gz, ga, gb, dq, dk, dv) = jnp.split(h @ lp["w_in"], split_points(), axis=-1)
    aq, ak, av = attn_heads(aq, ak, av, lp["a_qn_g"], lp["a_kn_g"], A_HQ, A_HKV)
    dq, dk, dv = attn_heads(dq, dk, dv, lp["d_qn_g"], lp["d_kn_g"], D_HQ, D_HKV)
    sink = lp["a_sink"].reshape(A_HKV, A_HQ // A_HKV)
    if ctx is None:
        o_a = blocked_attention(aq, ak, av, sink)
        o_d = blocked_attention(dq, dk, dv, None)
        lru0 = jnp.zeros((B, 2, LRU_W), jnp.float32)
        gdn0 = jnp.zeros((B, 2, GDN_H, GDN_DK, GDN_DV), jnp.float32)
    else:
        cak, cav, cdk, cdv, lru0, gdn0 = ctx
        cos, sin = rope
        o_a = banded_attention(apply_rope(aq, cos, sin), apply_rope(ak, cos, sin), av, cak, cav, sink)
        o_d = blocked_attention(apply_rope(dq, cos, sin),
                                jnp.concatenate([cdk, apply_rope(dk, cos, sin)], axis=1),
                                jnp.concatenate([cdv, dv], axis=1), None)
    o_b, lru_s = rglru_mixer(lx, lg, lp, lru0)
    o_c, gdn_s = gdn_mixer(gq, gk, gv, gz, ga, gb, lp, gdn0)
    branches = jnp.stack([o_a, o_b, o_c, o_d], axis=2)
    proj = jnp.einsum("btmc,mcd->btmd", branches, lp["w_branch"])
    gates = jax.nn.sigmoid((h @ lp["w_merge"] + lp["b_merge"]).reshape(B, T, N_BRANCH, D_MODEL))
    y = jnp.sum(gates * proj, axis=2) @ lp["w_out"]
    return y, (ak, av, dk, dv, lru_s.astype(h.dtype), gdn_s.astype(h.dtype))


def trunk_layer(x, cond, lp, ctx, rope):
    mod = jax.nn.silu(cond) @ lp["mod_w"] + lp["mod_b"]
    sh1, sc1, g1, sh2, sc2, g2 = jnp.split(mod, 6, axis=-1)
    y, new_ctx = token_mixing(modulate(x, lp["norm1_g"], sh1, sc1), lp, ctx, rope)
    x = x + g1[:, None, :] * y
    h = modulate(x, lp["norm2_g"], sh2, sc2)
    x = x + g2[:, None, :] * (jnp.square(jax.nn.relu(h @ lp["mlp_w1"])) @ lp["mlp_w2"])
    return x, new_ctx


def setup_inputs(seed: int = 0) -> dict:
    key = jax.random.key(seed)
    ks = iter(jax.random.split(key, 64))
    nrm = lambda shape, s=1.0: jax.random.normal(next(ks), shape, jnp.float32) * s
    gain = lambda shape: 1.0 + nrm(shape, 0.05)
    u = jax.random.uniform(next(ks), (DEPTH, 2, LRU_W), jnp.float32, 0.9, 0.999)
    dt = jnp.exp(jax.random.uniform(next(ks), (DEPTH, 2, GDN_H), jnp.float32, np.log(1e-3), np.log(1e-1)))
    return {
        "x_prompt": nrm((BATCH, SEQ, D_MODEL)),
        "x_sample": nrm((DEC_BATCH, DEC_SEQ, D_MODEL)),
        "c": nrm((DEC_BATCH, D_MODEL)),
        "cache_a_k": nrm((DEC_BATCH, DEPTH, PAST_LEN, A_HKV, HEAD_DIM)),
        "cache_a_v": nrm((DEC_BATCH, DEPTH, PAST_LEN, A_HKV, HEAD_DIM)),
        "cache_d_k": nrm((DEC_BATCH, DEPTH, PAST_LEN, D_HKV, HEAD_DIM)),
        "cache_d_v": nrm((DEC_BATCH, DEPTH, PAST_LEN, D_HKV, HEAD_DIM)),
        "state_lru": nrm((DEC_BATCH, DEPTH, 2, LRU_W), 0.5),
        "state_gdn": nrm((DEC_BATCH, DEPTH, 2, GDN_H, GDN_DK, GDN_DV), 0.3),
        "c_ctx": nrm((D_MODEL,)),
        "mod_w": nrm((DEPTH, D_MODEL, 6 * D_MODEL), 0.5 * D_MODEL ** -0.5),
        "mod_b": nrm((DEPTH, 6 * D_MODEL), 0.01),
        "norm1_g": gain((DEPTH, D_MODEL)),
        "norm2_g": gain((DEPTH, D_MODEL)),
        "w_in": nrm((DEPTH, D_MODEL, IN_COLS), D_MODEL ** -0.5),
        "a_qn_g": gain((DEPTH, HEAD_DIM)),
        "a_kn_g": gain((DEPTH, HEAD_DIM)),
        "a_sink": nrm((DEPTH, A_HQ)),
        "lru_conv_w": nrm((DEPTH, CONV_W, LRU_W), CONV_W ** -0.5),
        "lru_conv_b": nrm((DEPTH, LRU_W), 0.01),
        "lru_wr": nrm((DEPTH, 2, LRU_BLOCKS, LRU_BW, LRU_BW), LRU_BW ** -0.5),
        "lru_br": nrm((DEPTH, 2, LRU_W), 0.01),
        "lru_wi": nrm((DEPTH, 2, LRU_BLOCKS, LRU_BW, LRU_BW), LRU_BW ** -0.5),
        "lru_bi": nrm((DEPTH, 2, LRU_W), 0.01),
        "lru_lam": jnp.log(u) - jnp.log1p(-u),
        "gdn_conv_w": nrm((DEPTH, CONV_W, 2 * GDN_H * GDN_DK + GDN_H * GDN_DV), CONV_W ** -0.5),
        "gdn_a_log": jnp.log(jax.random.uniform(next(ks), (DEPTH, 2, GDN_H), jnp.float32, 1.0, 16.0)),
        "gdn_dt_bias": dt + jnp.log(-jnp.expm1(-dt)),
        "gdn_norm_g": gain((DEPTH, GDN_DV)),
        "d_qn_g": gain((DEPTH, HEAD_DIM)),
        "d_kn_g": gain((DEPTH, HEAD_DIM)),
        "w_branch": nrm((DEPTH, N_BRANCH, BRANCH_W, D_MODEL), BRANCH_W ** -0.5),
        "w_merge": nrm((DEPTH, D_MODEL, N_BRANCH * D_MODEL), D_MODEL ** -0.5),
        "b_merge": nrm((DEPTH, N_BRANCH * D_MODEL), 0.01),
        "w_out": nrm((DEPTH, D_MODEL, D_MODEL), D_MODEL ** -0.5),
        "mlp_w1": nrm((DEPTH, D_MODEL, D_FF), D_MODEL ** -0.5),
        "mlp_w2": nrm((DEPTH, D_FF, D_MODEL), D_FF ** -0.5),
    }


def reference(x_prompt, x_sample, c, cache_a_k, cache_a_v, cache_d_k, cache_d_v, state_lru, state_gdn,
              c_ctx, mod_w, mod_b, norm1_g, norm2_g, w_in, a_qn_g, a_kn_g, a_sink, lru_conv_w, lru_conv_b,
              lru_wr, lru_br, lru_wi, lru_bi, lru_lam, gdn_conv_w, gdn_a_log, gdn_dt_bias, gdn_norm_g,
              d_qn_g, d_kn_g, w_branch, w_merge, b_merge, w_out, mlp_w1, mlp_w2):
    def layer_params(l):
        return dict(mod_w=mod_w[l], mod_b=mod_b[l], norm1_g=norm1_g[l], norm2_g=norm2_g[l], w_in=w_in[l],
                    a_qn_g=a_qn_g[l], a_kn_g=a_kn_g[l], a_sink=a_sink[l], lru_conv_w=lru_conv_w[l],
                    lru_conv_b=lru_conv_b[l], lru_wr=lru_wr[l], lru_br=lru_br[l], lru_wi=lru_wi[l],
                    lru_bi=lru_bi[l], lru_lam=lru_lam[l], gdn_conv_w=gdn_conv_w[l], gdn_a_log=gdn_a_log[l],
                    gdn_dt_bias=gdn_dt_bias[l], gdn_norm_g=gdn_norm_g[l], d_qn_g=d_qn_g[l], d_kn_g=d_kn_g[l],
                    w_branch=w_branch[l], w_merge=w_merge[l], b_merge=b_merge[l], w_out=w_out[l],
                    mlp_w1=mlp_w1[l], mlp_w2=mlp_w2[l])

    xp = x_prompt
    ctx_tensors = []
    for l in range(DEPTH):
        xp, ctx_l = trunk_layer(xp, c_ctx[None, :], layer_params(l), None, None)
        ctx_tensors.append(ctx_l)
    new_a_k, new_a_v, new_d_k, new_d_v, new_lru, new_gdn = [jnp.stack(t, axis=1) for t in zip(*ctx_tensors)]

    rope = axial_rope(x_sample.shape[1])
    xs = x_sample
    for l in range(DEPTH):
        ctx_l = (cache_a_k[:, l], cache_a_v[:, l], cache_d_k[:, l], cache_d_v[:, l], state_lru[:, l], state_gdn[:, l])
        xs, _ = trunk_layer(xs, c, layer_params(l), ctx_l, rope)

    return (xp, xs, new_a_k, new_a_v, new_d_k, new_d_v, new_lru, new_gdn)
```

```python
import numpy as np
from contextlib import ExitStack
import concourse.bass as bass
import concourse.mybir as mybir
from concourse.bass_utils import run_bass_kernel_spmd

F32 = mybir.dt.float32
BF16 = mybir.dt.bfloat16
I32 = mybir.dt.int32
AF = mybir.ActivationFunctionType
ALU = mybir.AluOpType
AX = mybir.AxisListType

GEN = 16000
N_DMA_SEM = 8
N_DMA_POOL = 2
DMA_PER_SEM = 1000


class Op:
    __slots__ = ("eng", "fn", "deps", "needs_sig", "sig", "dma", "dsem", "dval", "idx")

    def __init__(self, eng, fn, dma):
        self.eng = eng
        self.fn = fn
        self.deps = []
        self.needs_sig = False
        self.sig = None
        self.dma = dma
        self.dsem = None
        self.dval = 0
        self.idx = 0


class Sched:
    ENGS = ("pe", "act", "dve", "pool", "sp")

    def __init__(self, nc, stack):
        self.nc = nc
        self.stack = stack
        self.ops = {e: [] for e in self.ENGS}
        self.last_w = {}
        self.readers = {}
        self.dma_sems = {}
        self.dma_cnt = {}
        self.dma_rr = {}
        self.dma_last = {}
        self.nops = 0
        self.cap = None
        self.nsem_extra = 0
        self.dma_final = {"sp": [], "pool": [], "act": []}
        self.bar = []
        self.bar_applied = set(self.ENGS)
        self.dmas_since = []
        for q in ("sp", "pool", "act"):
            self.dma_sems[q] = [stack.enter_context(nc.semaphore(f"dq_{q}_{i}")) for i in range(N_DMA_SEM)]
            self.dma_cnt[q] = [0] * N_DMA_SEM
            self.dma_last[q] = [None] * N_DMA_SEM
            self.dma_rr[q] = 0

    def begin_capture(self, rename):
        self.cap = [[], [], []]
        self.cap_i = 0
        self.cap_rename = rename

    def mark(self):
        if self.cap is not None:
            self.cap_i = 1

    def end_capture(self):
        c = self.cap
        self.cap = None
        return c

    def commit(self, items):
        for it in items:
            self.add(*it[:5])

    def add(self, eng, fn, reads=(), writes=(), dma=False, glue=False):
        if self.cap is not None:
            rn = self.cap_rename
            self.cap[self.cap_i].append((eng, fn, [rn(k) for k in reads], [rn(k) for k in writes], dma, glue))
            return None
        op = Op(eng, fn, dma)
        op.idx = self.nops
        self.nops += 1
        deps = {}

        def dep(o, kind):
            if o is None or o is op:
                return
            k = deps.get(id(o))
            if k is None or (kind == "raw"):
                deps[id(o)] = (o, kind if k is None else ("raw" if "raw" in (kind, k[1]) else k[1]))

        for k in reads:
            dep(self.last_w.get(k), "raw")
        for k in writes:
            dep(self.last_w.get(k), "waw")
            rs = self.readers.get(k, ())
            if len(rs) > 2:
                last = {}
                keep = []
                for r in rs:
                    if r.dma:
                        keep.append(r)
                    else:
                        last[r.eng] = r
                rs = keep + list(last.values())
            for r in rs:
                dep(r, "war")
        if eng not in self.bar_applied:
            self.bar_applied.add(eng)
            for o in self.bar:
                if o.dma or dma or o.eng != eng:
                    dep(o, "raw")
        if dma:
            self.dmas_since.append(op)
            q = eng
            i = self.dma_rr[q]
            self.dma_rr[q] = (i + 1) % (N_DMA_POOL if q == "pool" else N_DMA_SEM)
            dep(self.dma_last[q][i], "raw")
            if self.dma_cnt[q][i] >= 16 * DMA_PER_SEM:
                self.dma_final[q].append((self.dma_sems[q][i], self.dma_cnt[q][i]))
                self.nsem_extra += 1
                self.dma_sems[q][i] = self.stack.enter_context(self.nc.semaphore(f"dq_{q}_{i}_g{self.nsem_extra}"))
                self.dma_cnt[q][i] = 0
            self.dma_cnt[q][i] += 16
            op.dsem = self.dma_sems[q][i]
            op.dval = self.dma_cnt[q][i]
            self.dma_last[q][i] = op
        for o, kind in deps.values():
            if self._need_wait(op, o, kind):
                op.deps.append(o)
                o.needs_sig = True
        for k in reads:
            self.readers.setdefault(k, []).append(op)
        for k in writes:
            self.last_w[k] = op
            self.readers[k] = []
        self.ops[eng].append(op)
        return op

    def barrier(self):
        bar = list(self.dmas_since)
        for e in self.ENGS:
            if self.ops[e]:
                bar.append(self.ops[e][-1])
        old = [o for o in self.bar] if len(self.bar_applied) < len(self.ENGS) else []
        self.bar = bar + old
        self.bar_applied = set()
        self.dmas_since = []

    @staticmethod
    def _need_wait(op, o, kind):
        if o.dma:
            return True
        if op.dma:
            return True
        if o.eng != op.eng:
            return True
        if op.eng == "pe":
            return False
        return kind == "raw"

    def emit(self):
        nc = self.nc
        sems = {}
        for e in self.ENGS:
            n = 0
            for op in self.ops[e]:
                if op.needs_sig and not op.dma:
                    op.sig = n
                    n += 1
            ngen = (n + GEN - 1) // GEN
            sems[e] = [self.stack.enter_context(nc.semaphore(f"cs_{e}_{g}")) for g in range(ngen)]
        self.csems = sems

        def run(eng_name, h):
            waited = {}
            for op in self.ops[eng_name]:
                need = {}
                for d in op.deps:
                    if d.dma:
                        s, v = d.dsem, d.dval
                    else:
                        s, v = sems[d.eng][d.sig // GEN], d.sig % GEN + 1
                    key = id(s)
                    if waited.get(key, 0) >= v:
                        continue
                    if key not in need or need[key][1] < v:
                        need[key] = (s, v)
                for key, (s, v) in need.items():
                    h.wait_ge(s, v)
                    waited[key] = v
                ins = op.fn(h)
                if op.dma:
                    ins.then_inc(op.dsem, 16)
                elif op.needs_sig:
                    ins.then_inc(sems[eng_name][op.sig // GEN], 1)
            if eng_name in self.dma_sems:
                fin = list(self.dma_final[eng_name]) + [(s, self.dma_cnt[eng_name][i])
                                                         for i, s in enumerate(self.dma_sems[eng_name])]
                for s, cnt in fin:
                    if cnt > waited.get(id(s), 0):
                        h.wait_ge(s, cnt)

        with nc.Block() as block:
            @block.tensor
            def _(h):
                run("pe", h)

            @block.scalar
            def _(h):
                run("act", h)

            @block.vector
            def _(h):
                run("dve", h)

            @block.gpsimd
            def _(h):
                run("pool", h)

            @block.sync
            def _(h):
                run("sp", h)

    def dma(self, q, out, in_, reads, writes, **kw):
        return self.add(q, lambda h: h.dma_start(out=out, in_=in_, **kw), reads, writes, dma=True)

    def mm(self, out, lhsT, rhs, reads, writes, start=True, stop=True):
        return self.add("pe", lambda h: h.matmul(out, lhsT, rhs, start=start, stop=stop), reads, writes,
                        glue=(not start))

    def tr(self, out, in_, ident, reads, writes):
        return self.add("pe", lambda h: h.transpose(out, in_, ident), reads, writes)

    def act(self, out, in_, func, reads, writes, bias=0.0, scale=1.0, accum_out=None, eng="act"):
        if accum_out is None:
            return self.add(eng, lambda h: h.activation(out, in_, func, bias=bias, scale=scale), reads, writes)
        return self.add(eng, lambda h: h.activation(out, in_, func, bias=bias, scale=scale, accum_out=accum_out),
                        reads, writes)

    def tt(self, eng, out, in0, in1, op, reads, writes):
        return self.add(eng, lambda h: h.tensor_tensor(out, in0, in1, op), reads, writes)

    def ts(self, eng, out, in0, s1, s2, op0, op1, reads, writes, accum_out=None):
        if op1 is None:
            return self.add(eng, lambda h: h.tensor_scalar(out, in0, s1, None, op0), reads, writes)
        if accum_out is None:
            return self.add(eng, lambda h: h.tensor_scalar(out, in0, s1, s2, op0, op1), reads, writes)
        return self.add(eng, lambda h: h.tensor_scalar(out, in0, s1, s2, op0, op1, accum_out=accum_out),
                        reads, writes)

    def stt(self, eng, out, in0, scalar, in1, op0, op1, reads, writes):
        return self.add(eng, lambda h: h.scalar_tensor_tensor(out, in0, scalar, in1, op0, op1), reads, writes)

    def copy(self, eng, out, in_, reads, writes):
        if eng == "act":
            return self.add(eng, lambda h: h.copy(out, in_), reads, writes)
        return self.add(eng, lambda h: h.tensor_copy(out, in_), reads, writes)

    def memset(self, eng, ap, val, writes):
        return self.add(eng, lambda h: h.memset(ap, val), (), writes)


D = 1024
NKC = 8
HD = 64
DFF = 4096
IN_COLS = 2576
C_AQ, C_AK, C_AV, C_LX, C_LG, C_GQ, C_GK, C_GV, C_GZ, C_GA, C_GB, C_DQ, C_DK, C_DV = (
    0, 256, 384, 512, 768, 1024, 1280, 1536, 1792, 2048, 2056, 2064, 2320, 2448)
EPS = 1e-6
T_P = 256
PAST = 512
GRID_W = 64
DEPTH = 2
CH = 64

WEIGHT_SPECS = [
    ("mod_w", [DEPTH, D, 6 * D]), ("mod_b", [DEPTH, 6 * D]), ("norm1_g", [DEPTH, D]), ("norm2_g", [DEPTH, D]),
    ("w_in", [DEPTH, D, IN_COLS]), ("a_qn_g", [DEPTH, HD]), ("a_kn_g", [DEPTH, HD]), ("a_sink", [DEPTH, 4]),
    ("lru_conv_w", [DEPTH, 4, 256]), ("lru_conv_b", [DEPTH, 256]), ("lru_wr", [DEPTH, 2, 4, 64, 64]),
    ("lru_br", [DEPTH, 2, 256]), ("lru_wi", [DEPTH, 2, 4, 64, 64]), ("lru_bi", [DEPTH, 2, 256]),
    ("lru_lam", [DEPTH, 2, 256]), ("gdn_conv_w", [DEPTH, 4, 768]), ("gdn_a_log", [DEPTH, 2, 4]),
    ("gdn_dt_bias", [DEPTH, 2, 4]), ("gdn_norm_g", [DEPTH, HD]), ("d_qn_g", [DEPTH, HD]), ("d_kn_g", [DEPTH, HD]),
    ("w_branch", [DEPTH, 4, 256, D]), ("w_merge", [DEPTH, D, 4 * D]), ("b_merge", [DEPTH, 4 * D]),
    ("w_out", [DEPTH, D, D]), ("mlp_w1", [DEPTH, D, DFF]), ("mlp_w2", [DEPTH, DFF, D]),
]


class Cfg:
    def __init__(self, n_pseq=4, t_s=4096, depth=DEPTH, debug=False, stop_after=None):
        self.n_pseq = n_pseq
        self.t_s = t_s
        self.depth = depth
        self.np_tok = n_pseq * T_P
        self.nt = self.np_tok + t_s
        self.debug = debug
        self.stop_after = stop_after
        self.seqs = [(j * T_P, T_P, False) for j in range(n_pseq)] + [(self.np_tok, t_s, True)]


def rope_tables(t_s):
    rows = t_s // GRID_W
    row = np.repeat(np.arange(rows, dtype=np.float32), GRID_W)
    col = np.tile(np.arange(GRID_W, dtype=np.float32), rows)
    n_freq = HD // 4
    inv = (np.float32(10000.0) ** (-np.arange(n_freq, dtype=np.float32) / np.float32(n_freq))).astype(np.float32)
    ang = np.stack([row[:, None] * inv, col[:, None] * inv], axis=1).astype(np.float32)
    return (np.cos(ang).astype(np.float32).reshape(t_s, 32), np.sin(ang).astype(np.float32).reshape(t_s, 32))


def const_inputs(cfg):
    cos, sin = rope_tables(cfg.t_s)
    ident = np.eye(128, dtype=np.float32)
    idx = np.arange(64)
    tri_f = (idx[:, None] <= idx[None, :]).astype(np.float32)
    tri_b = (idx[:, None] >= idx[None, :]).astype(np.float32)
    NEGM = -30000.0
    mb_f = np.where(idx[None, :] >= idx[:, None], 0.0, NEGM).astype(np.float32)
    mb_b = np.where(idx[None, :] <= idx[:, None], 0.0, NEGM).astype(np.float32)
    st_f = (idx[None, :] > idx[:, None]).astype(np.float32)
    st_b = (idx[None, :] < idx[:, None]).astype(np.float32)
    i128 = np.arange(128)
    bm_prev = np.where(i128[:, None] >= i128[None, :], 0.0, NEGM).astype(np.float32)
    bm_next = np.where(i128[:, None] <= i128[None, :], 0.0, NEGM).astype(np.float32)
    return dict(rope_cos=cos, rope_sin=sin, c_ident=ident, c_tri_f=tri_f, c_tri_b=tri_b,
                c_mb_f=mb_f, c_mb_b=mb_b, c_st_f=st_f, c_st_b=st_b, c_bm_prev=bm_prev, c_bm_next=bm_next)


def V(ap, off, dims):
    return bass.AP(ap.tensor, ap.offset + off, [list(ap.ap[0])] + [list(d) for d in dims])


class MK:
    def __init__(self, cfg):
        self.cfg = cfg
        self.nc = bass.Bass("TRN2", target_bir_lowering=False)
        self.top = ExitStack()
        self.S = Sched(self.nc, self.top)
        self.dr = {}
        self.uid = 0

    def din(self, name, shape, dt=F32):
        self.dr[name] = self.nc.dram_tensor(name, list(shape), dt, kind="ExternalInput").ap()
        return self.dr[name]

    def dout(self, name, shape, dt=F32):
        self.dr[name] = self.nc.dram_tensor(name, list(shape), dt, kind="ExternalOutput").ap()
        return self.dr[name]

    def dscr(self, name, shape, dt=F32):
        kind = "ExternalOutput" if (self.cfg.debug and name in self.cfg.debug) else "Internal"
        self.dr[name] = self.nc.dram_tensor(name, list(shape), dt, kind=kind).ap()
        return self.dr[name]

    def sb(self, ph, name, shape, dt=F32):
        self.uid += 1
        return ph.enter_context(self.nc.sbuf_tensor(f"{name}_{self.uid}", list(shape), dt))

    def declare(self):
        cfg = self.cfg
        L = cfg.depth
        self.din("x_prompt", [cfg.np_tok, D])
        self.din("x_sample", [cfg.t_s, D])
        self.din("c", [1, D])
        self.din("c_ctx", [1, D])
        for n in ("cache_a_k", "cache_a_v", "cache_d_k", "cache_d_v"):
            self.din(n, [DEPTH, PAST, 128])
        self.din("state_lru", [DEPTH, 2, 256])
        self.din("state_gdn", [DEPTH, 2, 4, 64, 64])
        for n, shp in WEIGHT_SPECS:
            self.din(n, shp)
        for n, a in const_inputs(cfg).items():
            self.din(n, a.shape)
        self.dout("y_prompt", [cfg.np_tok, D])
        self.dout("y_sample", [cfg.t_s, D])
        for n in ("new_a_k", "new_a_v", "new_d_k", "new_d_v"):
            self.dout(n, [cfg.n_pseq, DEPTH, T_P, 128])
        self.dout("new_lru", [cfg.n_pseq, DEPTH, 2, 256])
        self.dout("new_gdn", [cfg.n_pseq, DEPTH, 2, 4, 64, 64])
        NT = cfg.nt
        self.dscr("MODS", [DEPTH, 2, 6, D])
        self.dscr("XS", [NT, D])
        self.dscr("X1", [NT, D])
        self.dscr("HT", [NKC, 128, NT], BF16)
        self.dscr("MT", [NKC, 128, NT], BF16)
        for m in ("A", "D"):
            self.dscr("QT_" + m, [2, 128, NT], BF16)
            self.dscr("KT_" + m, [2, 128, NT], BF16)
            self.dscr("V_" + m, [NT, 128], BF16)
            self.dscr("Y" + m, [2, 128, NT], BF16)
        self.dscr("YB", [2, 128, NT], BF16)
        self.dscr("FM", [10, 128, NT])
        self.dscr("TMZ", [NT, 256])
        self.dscr("TMG", [NT, 16])
        self.dscr("GN", [6, 128, NT])
        self.dscr("OF", [NT, 256])
        self.dscr("OB", [NT, 256])
        nc = self.nc
        self.ps = [self.top.enter_context(nc.psum_tensor(f"psb{i}", [128, 512], F32)) for i in range(8)]
        self.ident_f = self.top.enter_context(nc.sbuf_tensor("ident_f", [128, 128], F32))
        self.ident_b = self.top.enter_context(nc.sbuf_tensor("ident_b", [128, 128], BF16))
        S = self.S
        self.eps_t = self.top.enter_context(nc.sbuf_tensor("eps_t", [128, 1], F32))
        self.one_t = self.top.enter_context(nc.sbuf_tensor("one_t", [128, 1], F32))
        S.memset("pool", self.eps_t[:], EPS, ["eps_t"])
        S.memset("pool", self.one_t[:], 1.0, ["one_t"])
        S.dma("sp", self.ident_f[:], self.dr["c_ident"], [], ["ident_f"])
        S.copy("dve", self.ident_b[:], self.ident_f[:], ["ident_f"], ["ident_b"])

    def group_of(self, t0):
        return 1 if t0 >= self.cfg.np_tok else 0

    def phase_mod(self):
        S, dr = self.S, self.dr
        with ExitStack() as ph:
            condT = self.sb(ph, "condT", [128, NKC, 2])
            scond = self.sb(ph, "scond", [128, NKC, 2])
            S.dma("sp", condT[:, :, 0], dr["c_ctx"].rearrange("o (kc p) -> p (o kc)", p=128), [], ["condT"],
                  allow_slow_non_contiguous=True)
            S.dma("sp", condT[:, :, 1], dr["c"].rearrange("o (kc p) -> p (o kc)", p=128), [], ["condT"],
                  allow_slow_non_contiguous=True)
            S.act(scond[:], condT[:], AF.Silu, ["condT"], ["scond"])
            sch = self.sb(ph, "sch", [128, NKC, 2], BF16)
            scl = self.sb(ph, "scl", [128, NKC, 2], BF16)
            S.copy("act", sch[:], scond[:], ["scond"], ["sch"])
            S.tt("dve", scond[:], scond[:], sch[:], ALU.subtract, ["scond", "sch"], ["scond"])
            S.copy("act", scl[:], scond[:], ["scond"], ["scl"])
            wb = [self.sb(ph, f"modw{i}", [128, NKC, 512], BF16) for i in range(2)]
            modb = self.sb(ph, "modb", [2, 6 * D])
            mods = self.sb(ph, "mods", [2, 6 * D])
            ng = self.sb(ph, "ng", [2, 2, D])
            it = 0
            for l in range(self.cfg.depth):
                S.dma("sp", modb[:], dr["mod_b"][l:l + 1, :].partition_broadcast(2), [], ["modb"])
                S.dma("sp", ng[:, 0, :], dr["norm1_g"][l:l + 1, :].partition_broadcast(2), [], ["ng"])
                S.dma("sp", ng[:, 1, :], dr["norm2_g"][l:l + 1, :].partition_broadcast(2), [], ["ng"])
                for nb in range(12):
                    w = wb[it % 2]
                    wk = f"modw{it % 2}"
                    pk = f"ps{it % 2}"
                    pst = self.ps[it % 2]
                    S.dma("pool", w[:],
                          dr["mod_w"][l, :, nb * 512:(nb + 1) * 512].rearrange("(kc p) n -> p kc n", p=128),
                          [], [wk])
                    for pi_, (sc_, sck) in enumerate(((sch, "sch"), (scl, "scl"))):
                        for kc in range(NKC):
                            S.mm(pst[0:2, :], sc_[:, kc, :], w[:, kc, :], [sck, wk], [pk],
                                 start=(pi_ == 0 and kc == 0), stop=(pi_ == 1 and kc == NKC - 1))
                    S.tt("dve", mods[:, nb * 512:(nb + 1) * 512], pst[0:2, :], modb[:, nb * 512:(nb + 1) * 512],
                         ALU.add, [pk, "modb"], ["mods"])
                    it += 1
                S.stt("dve", mods[:, D:2 * D], mods[:, D:2 * D], 1.0, ng[:, 0, :], ALU.add, ALU.mult,
                      ["mods", "ng"], ["mods"])
                S.stt("dve", mods[:, 4 * D:5 * D], mods[:, 4 * D:5 * D], 1.0, ng[:, 1, :], ALU.add, ALU.mult,
                      ["mods", "ng"], ["mods"])
                S.dma("sp", dr["MODS"][l].rearrange("g s d -> g (s d)"), mods[:], ["mods"], [("MODS", l)])
        S.barrier()

    def load_mod(self, ph, l, g, idx, name):
        t = self.sb(ph, name, [128, D])
        self.S.dma("sp", t[:], self.dr["MODS"][l, g, idx:idx + 1, :].partition_broadcast(128),
                   [("MODS", l)], [t.name])
        return t

    def emit_norm_mod(self, x, xk, Gt, SHt, out, outk, scr, l):
        S = self.S
        ss, lnm, rstd, junk, tmp = scr
        S.memset("pool", ss[:], 0.0, [ss.name])
        S.act(junk[:], x, AF.Square, [xk, ss.name], [junk.name, ss.name], accum_out=ss[:])
        S.act(lnm[:], ss[:], AF.Ln, [ss.name], [lnm.name], scale=1.0 / D, bias=self.eps_t[:, 0:1])
        S.act(rstd[:], lnm[:], AF.Exp, [lnm.name], [rstd.name], scale=-0.5)
        S.stt("dve", tmp[:], x, rstd[:, 0:1], Gt[:], ALU.mult, ALU.mult, [xk, rstd.name, Gt.name], [tmp.name])
        S.tt("dve", out, tmp[:], SHt[:], ALU.add, [tmp.name, SHt.name], [outk])

    def supertiles(self):
        cfg = self.cfg
        out = []
        t = 0
        while t < cfg.np_tok:
            n = min(512, cfg.np_tok - t)
            out.append((t, n))
            t += n
        while t < cfg.nt:
            n = min(512, cfg.nt - t)
            out.append((t, n))
            t += n
        return out

    def xsrc(self, l, t0, n):
        cfg = self.cfg
        if l == 0:
            if t0 < cfg.np_tok:
                return self.dr["x_prompt"][t0:t0 + n, :]
            return self.dr["x_sample"][t0 - cfg.np_tok:t0 - cfg.np_tok + n, :]
        return self.dr["XS"][t0:t0 + n, :]

    def phase_A(self, l):
        S, dr, cfg = self.S, self.dr, self.cfg
        ps = self.ps
        with ExitStack() as ph:
            sb = lambda name, shape, dt=F32: self.sb(ph, name, shape, dt)
            w_in = sb("w_in", [128, NKC, IN_COLS], BF16)
            for kc in range(NKC):
                S.dma("pool", w_in[:, kc, :], dr["w_in"][l, kc * 128:(kc + 1) * 128, :], [], [("w_in", kc)])
            wk = [("w_in", kc) for kc in range(NKC)]
            G1 = [self.load_mod(ph, l, g, 1, f"G1_{g}") for g in range(2)]
            SH1 = [self.load_mod(ph, l, g, 0, f"SH1_{g}") for g in range(2)]
            xt = [sb(f"xt{i}", [128, D]) for i in range(2)]
            hb = [sb(f"hb{i}", [128, D], BF16) for i in range(2)]
            hT = [sb(f"hT{i}", [128, NKC, 512], BF16) for i in range(2)]
            scr = (sb("ss", [128, 1]), sb("lnm", [128, 1]), sb("rstd", [128, 1]), sb("junk", [128, D], BF16),
                   sb("tmp", [128, D]))
            nst_s = cfg.t_s // 128
            cos_t = sb("cos_t", [128, nst_s, 32])
            sin_t = sb("sin_t", [128, nst_s, 32])
            S.dma("sp", cos_t[:], dr["rope_cos"].rearrange("(s p) f -> p s f", p=128), [], ["cos_t"])
            S.dma("sp", sin_t[:], dr["rope_sin"].rearrange("(s p) f -> p s f", p=128), [], ["sin_t"])
            GQK = {}
            for mix, qn, kn in (("A", "a_qn_g", "a_kn_g"), ("D", "d_qn_g", "d_kn_g")):
                t = sb("GQK" + mix, [128, 6, 64])
                for hh in range(6):
                    src = dr[qn if hh < 4 else kn][l:l + 1, :].partition_broadcast(128)
                    S.dma("sp", t[:, hh, :], src, [], [t.name])
                GQK[mix] = t
            dtb = sb("dtb", [128, 8])
            nexpa = sb("nexpa", [128, 8])
            S.dma("sp", dtb[:], dr["gdn_dt_bias"][l:l + 1].rearrange("o a b -> o (a b)").partition_broadcast(128),
                  [], ["dtb"])
            S.dma("sp", nexpa[:], dr["gdn_a_log"][l:l + 1].rearrange("o a b -> o (a b)").partition_broadcast(128),
                  [], ["nexpa"])
            S.act(nexpa[:], nexpa[:], AF.Exp, ["nexpa"], ["nexpa"])
            S.ts("dve", nexpa[:], nexpa[:], -1.0, None, ALU.mult, None, ["nexpa"], ["nexpa"])
            sq = sb("sq", [128, 384])
            ss6 = sb("ss6", [128, 6])
            r6 = sb("r6", [128, 6])
            qkn = sb("qkn", [128, 384])
            qkr = sb("qkr", [128, 384])
            ta = sb("ta", [128, 192])
            tb = sb("tb", [128, 192])
            vbuf = sb("vbuf", [128, 128])
            qb = sb("qb", [128, 256], BF16)
            kd = sb("kd", [128, 256], BF16)
            stage = {m: [sb(f"stg{m}{i}", [128, 4, 512], BF16) for i in range(2)] for m in ("A", "D")}
            vstage = {m: [sb(f"vst{m}{i}", [128, 4, 128], BF16) for i in range(2)] for m in ("A", "D")}
            gzb = [sb(f"gzb{i}", [128, 256]) for i in range(2)]
            gbb = [sb(f"gbb{i}", [128, 16]) for i in range(2)]
            gtmp = sb("gtmp", [128, 8])
            fms = [sb(f"fms{i}", [128, 512]) for i in range(3)]
            fmi = 0

            def att_post(mix, P, pk, t0, st, sti, nsup, is_sample):
                S.act(sq[:], P[:, 0:384], AF.Square, [pk], ["sq"])
                S.add("dve", lambda h: h.reduce_sum(ss6[:], V(sq[:], 0, [[64, 6], [1, 64]]), AX.X), ["sq"], ["ss6"])
                S.act(ss6[:], ss6[:], AF.Ln, ["ss6"], ["ss6"], scale=1.0 / HD, bias=self.eps_t[:, 0:1])
                S.act(r6[:], ss6[:], AF.Exp, ["ss6"], ["r6"], scale=-0.5)
                S.tt("dve", V(qkn[:], 0, [[64, 6], [1, 64]]), V(P[:, 0:384], 0, [[64, 6], [1, 64]]),
                     V(r6[:], 0, [[1, 6], [0, 64]]), ALU.mult, [pk, "r6"], ["qkn"])
                S.tt("dve", qkn[:], qkn[:], V(GQK[mix][:], 0, [[1, 384]]), ALU.mult, ["qkn", GQK[mix].name], ["qkn"])
                src = qkn
                srck = "qkn"
                if is_sample:
                    ti = (t0 - cfg.np_tok) // 128
                    pat = [[64, 6], [32, 2], [1, 16]]
                    x1 = V(qkn[:], 0, pat)
                    x2 = V(qkn[:], 16, pat)
                    cpat = [[0, 6], [16, 2], [1, 16]]
                    cc = V(cos_t[:], ti * 32, cpat)
                    sn = V(sin_t[:], ti * 32, cpat)
                    o1 = V(qkr[:], 0, pat)
                    o2 = V(qkr[:], 16, pat)
                    tpat = [[32, 6], [16, 2], [1, 16]]
                    S.tt("dve", V(ta[:], 0, tpat), x1, cc, ALU.mult, ["qkn", "cos_t"], ["ta"])
                    S.tt("dve", V(tb[:], 0, tpat), x2, sn, ALU.mult, ["qkn", "sin_t"], ["tb"])
                    S.tt("dve", o1, V(ta[:], 0, tpat), V(tb[:], 0, tpat), ALU.subtract, ["ta", "tb"], ["qkr"])
                    S.tt("dve", V(ta[:], 0, tpat), x2, cc, ALU.mult, ["qkn", "cos_t"], ["ta"])
                    S.tt("dve", V(tb[:], 0, tpat), x1, sn, ALU.mult, ["qkn", "sin_t"], ["tb"])
                    S.tt("dve", o2, V(ta[:], 0, tpat), V(tb[:], 0, tpat), ALU.add, ["ta", "tb"], ["qkr"])
                    src = qkr
                    srck = "qkr"
                else:
                    b = t0 // T_P
                    tt0 = t0 % T_P
                    lo = mix.lower()
                    S.copy("act", vbuf[:], P[:, 384:512], [pk], ["vbuf"])
                    S.dma("act", dr[f"new_{lo}_k"][b, l, tt0:tt0 + 128, :], qkn[:, 256:384], ["qkn"], [])
                    S.dma("act", dr[f"new_{lo}_v"][b, l, tt0:tt0 + 128, :], vbuf[:], ["vbuf"], [])
                S.copy("act", qb[:], src[:, 0:256], [srck], ["qb"])
                S.copy("dve", V(kd[:], 0, [[128, 2], [64, 2], [1, 64]]), V(src[:], 256, [[64, 2], [0, 2], [1, 64]]),
                       [srck], ["kd"])
                pq = ps[4 if mix == "A" else 5]
                pqk = f"ps{4 if mix == 'A' else 5}"
                pqb = pq[:].bitcast(BF16)
                S.tr(pqb[:, 0:128], qb[:, 0:128], self.ident_b[:], ["qb", "ident_b"], [pqk])
                S.tr(pqb[:, 128:256], qb[:, 128:256], self.ident_b[:], ["qb", "ident_b"], [pqk])
                S.tr(pqb[:, 256:384], kd[:, 0:128], self.ident_b[:], ["kd", "ident_b"], [pqk])
                S.tr(pqb[:, 384:512], kd[:, 128:256], self.ident_b[:], ["kd", "ident_b"], [pqk])
                stg = stage[mix][sti % 2]
                S.copy("act", V(stg[:], st * 128, [[512, 4], [1, 128]]), V(pqb, 0, [[128, 4], [1, 128]]),
                       [pqk], [stg.name])
                vst = vstage[mix][sti % 2]
                S.copy("dve", vst[:, st, :], P[:, 384:512], [pk], [vst.name])

            for sti, (T0, ntok) in enumerate(self.supertiles()):
                g = self.group_of(T0)
                is_sample = g == 1
                nsub = ntok // 128
                hTt = hT[sti % 2]
                for st in range(nsub):
                    t0 = T0 + st * 128
                    it = sti * 4 + st
                    x = xt[it % 2]
                    S.dma("sp", x[:], self.xsrc(l, t0, 128), [("X", l, t0)], [x.name])
                    h = hb[it % 2]
                    self.emit_norm_mod(x[:], x.name, G1[g], SH1[g], h[:], h.name, scr, l)
                    ptb = ps[3][:].bitcast(BF16)
                    for kc in range(NKC):
                        S.tr(ptb[:, kc * 128:(kc + 1) * 128], h[:, kc * 128:(kc + 1) * 128], self.ident_b[:],
                             [h.name, "ident_b"], ["ps3"])
                    S.copy("act", V(hTt[:], st * 128, [[512, NKC], [1, 128]]), V(ptb, 0, [[128, NKC], [1, 128]]),
                           ["ps3"], [hTt.name])
                    for (pi, c0, ncol) in ((0, C_AQ, 512), (1, C_DQ, 512), (2, C_GZ, 272)):
                        for kc in range(NKC):
                            S.mm(ps[pi][:, 0:ncol], hTt[:, kc, st * 128:(st + 1) * 128], w_in[:, kc, c0:c0 + ncol],
                                 [hTt.name, wk[kc]], [f"ps{pi}"], start=(kc == 0), stop=(kc == NKC - 1))
                    att_post("A", ps[0], "ps0", t0, st, sti, nsub, is_sample)
                    att_post("D", ps[1], "ps1", t0, st, sti, nsub, is_sample)
                    gz = gzb[it % 2]
                    gb = gbb[it % 2]
                    S.copy("act", gz[:], ps[2][:, 0:256], ["ps2"], [gz.name])
                    S.dma("act", dr["TMZ"][t0:t0 + 128, :], gz[:], [gz.name], [("TMZ", t0)])
                    S.tt("dve", gtmp[:], ps[2][:, 256:264], dtb[:], ALU.add, ["ps2", "dtb"], ["gtmp"])
                    S.act(gtmp[:], gtmp[:], AF.Exp, ["gtmp"], ["gtmp"])
                    S.act(gtmp[:], gtmp[:], AF.Ln, ["gtmp"], ["gtmp"], bias=self.one_t[:, 0:1])
                    S.tt("dve", gb[:, 0:8], gtmp[:], nexpa[:], ALU.mult, ["gtmp", "nexpa"], [gb.name])
                    S.act(gb[:, 8:16], ps[2][:, 264:272], AF.Sigmoid, ["ps2"], [gb.name])
                    S.dma("act", dr["TMG"][t0:t0 + 128, :], gb[:], [gb.name], [("TMG", t0)])
                for mix in ("A", "D"):
                    stg = stage[mix][sti % 2]
                    S.dma("act", dr["QT_" + mix][:, :, T0:T0 + ntok].rearrange("a p t -> p a t"),
                          stg[:, 0:2, 0:ntok], [stg.name], [("QT_" + mix, T0)])
                    S.dma("act", dr["KT_" + mix][:, :, T0:T0 + ntok].rearrange("a p t -> p a t"),
                          stg[:, 2:4, 0:ntok], [stg.name], [("KT_" + mix, T0)])
                    vst = vstage[mix][sti % 2]
                    S.dma("act", dr["V_" + mix][T0:T0 + ntok, :].rearrange("(s p) f -> p s f", p=128),
                          vst[:, 0:nsub, :], [vst.name], [("V_" + mix, T0)])
                S.dma("act", dr["HT"][:, :, T0:T0 + ntok].rearrange("k p t -> p k t"), hTt[:, :, 0:ntok],
                      [hTt.name], [("HT", T0)])
                for mt in range(10):
                    c0 = C_LX + mt * 128
                    pi = 6 + (mt % 2)
                    for kc in range(NKC):
                        S.mm(ps[pi][:, 0:ntok], w_in[:, kc, c0:c0 + 128], hTt[:, kc, 0:ntok],
                             [hTt.name, wk[kc]], [f"ps{pi}"], start=(kc == 0), stop=(kc == NKC - 1))
                    f = fms[fmi % 3]
                    fmi += 1
                    if mt % 2 == 0:
                        S.copy("act", f[:, 0:ntok], ps[pi][:, 0:ntok], [f"ps{pi}"], [f.name])
                    else:
                        S.copy("dve", f[:, 0:ntok], ps[pi][:, 0:ntok], [f"ps{pi}"], [f.name])
                    S.dma("act", dr["FM"][mt, :, T0:T0 + ntok], f[:, 0:ntok], [f.name], [("FM", T0)])
        S.barrier()


def make_in_maps(inp, cfg, n_cores, consts):
    maps = []
    f32 = lambda a: np.ascontiguousarray(np.asarray(a, dtype=np.float32))
    nps = cfg.n_pseq
    ngrp = max(1, n_cores // inp["x_sample"].shape[0])
    for i in range(n_cores):
        s = min(i // ngrp, inp["x_sample"].shape[0] - 1)
        m = {}
        m["x_prompt"] = f32(inp["x_prompt"][i * nps:(i + 1) * nps]).reshape(nps * T_P, D)
        m["x_sample"] = f32(inp["x_sample"][s])
        m["c"] = f32(inp["c"][s:s + 1])
        m["c_ctx"] = f32(inp["c_ctx"]).reshape(1, D)
        for n in ("cache_a_k", "cache_a_v", "cache_d_k", "cache_d_v"):
            m[n] = f32(inp[n][s]).reshape(DEPTH, PAST, 128)
        m["state_lru"] = f32(inp["state_lru"][s])
        m["state_gdn"] = f32(inp["state_gdn"][s])
        for n, shp in WEIGHT_SPECS:
            m[n] = f32(inp[n]).reshape(shp)
        m.update(consts)
        maps.append(m)
    return maps


def _phase_att(self, l, mix):
    S, dr, cfg, ps = self.S, self.dr, self.cfg, self.ps
    NEGM = -30000.0
    with ExitStack() as ph:
        sb = lambda name, shape, dt=F32: self.sb(ph, name, shape, dt)
        Tmax = max(T_P, cfg.t_s)
        Smax = PAST + cfg.t_s
        kT = sb("kT", [128, 2, Smax], BF16)
        vA = sb("vA", [128, Smax // 128, 2, 128], BF16)
        qT = sb("qT", [128, 2, Tmax], BF16)
        yT = sb("yT", [128, 2, Tmax], BF16)
        vld = sb("vld", [128, Smax // 128, 128], BF16)
        S.memset("pool", vA[:], 1.0, ["vA"])
        pT = [sb(f"pT{i}", [128, 512], BF16) for i in range(3)]
        sc = [sb(f"sc{i}", [128, 512]) for i in range(2)]
        rec = sb("rec", [64, 512])
        ck = sb("ck", [128, PAST // 128, 128])
        kdup = sb("kdup", [128, PAST // 128, 256], BF16)
        sinkE = sb("sinkE", [128, 4])
        if mix == "A":
            S.dma("sp", sinkE[:], dr["a_sink"][l:l + 1, :].partition_broadcast(128), [], ["sinkE"])
            S.act(sinkE[:], sinkE[:], AF.Exp, ["sinkE"], ["sinkE"])
            bmp = sb("bmp", [128, 128])
            bmn = sb("bmn", [128, 128])
            S.dma("sp", bmp[:], dr["c_bm_prev"], [], ["bmp"])
            S.dma("sp", bmn[:], dr["c_bm_next"], [], ["bmn"])
            masks = sb("masks", [128, 6, 512])
            S.memset("pool", masks[:], NEGM, ["masks"])
            for dd in range(6):
                for j in range(4):
                    diff = (dd - 1) - j
                    if diff == -1:
                        S.copy("dve", masks[:, dd, j * 128:(j + 1) * 128], bmp[:], ["bmp", "masks"], ["masks"])
                    elif diff == 0:
                        S.memset("dve", masks[:, dd, j * 128:(j + 1) * 128], 0.0, ["masks"])
                    elif diff == 1:
                        S.copy("dve", masks[:, dd, j * 128:(j + 1) * 128], bmn[:], ["bmn", "masks"], ["masks"])
        lo = mix.lower()
        nsc = 0
        npt = 0
        nacc = 0
        for (start, T, is_sample) in cfg.seqs:
            nkc = PAST // 128 if is_sample else 0
            if is_sample:
                S.dma("sp", ck[:], dr[f"cache_{lo}_k"][l].rearrange("(s p) f -> p s f", p=128), [], ["ck"])
                for s_ in range(nkc):
                    S.copy("dve", V(kdup[:], s_ * 256, [[128, 2], [64, 2], [1, 64]]),
                           V(ck[:], s_ * 128, [[64, 2], [0, 2], [1, 64]]), ["ck"], ["kdup"])
                for s_ in range(nkc):
                    pqb = ps[3][:].bitcast(BF16)
                    for hh in range(2):
                        S.tr(pqb[:, hh * 128:(hh + 1) * 128], kdup[:, s_, hh * 128:(hh + 1) * 128], self.ident_b[:],
                             ["kdup", "ident_b"], ["ps3"])
                    S.copy("act", V(kT[:], s_ * 128, [[Smax, 2], [1, 128]]), V(pqb, 0, [[128, 2], [1, 128]]),
                           ["ps3"], ["kT"])
                S.dma("sp", ck[:], dr[f"cache_{lo}_v"][l].rearrange("(s p) f -> p s f", p=128), ["ck"], ["ck"])
                S.copy("dve", V(vA[:], 0, [[256, nkc], [128, 2], [1, 64]]), V(ck[:], 0, [[128, nkc], [64, 2], [1, 64]]),
                       ["ck"], ["vA"])
            nkt = T // 128
            S.dma("sp", kT[:, :, nkc * 128:nkc * 128 + T],
                  dr["KT_" + mix][:, :, start:start + T].rearrange("a p t -> p a t"), [("KT_" + mix,)], ["kT"])
            for k0 in range(0, nkt, 4):
                k1 = min(nkt, k0 + 4)
                S.dma("sp", vld[:, k0:k1, :],
                      dr["V_" + mix][start + k0 * 128:start + k1 * 128, :].rearrange("(s p) f -> p s f", p=128),
                      [("V_" + mix,)], ["vld"])
            S.copy("dve", V(vA[:], nkc * 256, [[256, nkt], [128, 2], [1, 64]]),
                   V(vld[:], 0, [[128, nkt], [64, 2], [1, 64]]), ["vld"], ["vA"])
            S.dma("sp", qT[:, :, 0:T], dr["QT_" + mix][:, :, start:start + T].rearrange("a p t -> p a t"),
                  [("QT_" + mix,)], ["qT"])
            G = min(512, T)
            for q0 in range(0, T, G):
                qi0 = q0 // 128
                if mix == "A" and is_sample:
                    kts = [(k_, None) for k_ in range(nkc)]
                    for kl in range(qi0 - 1, qi0 + 5):
                        if 0 <= kl < nkt:
                            kts.append((nkc + kl, kl - qi0 + 1))
                else:
                    kts = [(k_, None) for k_ in range(nkc + nkt)]
                for hq in range(4):
                    pair, g2 = hq // 2, hq % 2
                    pr = slice(g2 * 64, (g2 + 1) * 64)
                    acc = ps[4 + nacc % 2]
                    acck = f"ps{4 + nacc % 2}"
                    nacc += 1
                    pend = []
                    nk = len(kts)

                    def emit_pv(ki, kt, p):
                        S.mm(acc[:, 0:G], vA[:, kt, pair, :], p[:, 0:G], ["vA", p.name], [acck],
                             start=(ki == 0), stop=(ki == nk - 1))

                    for ki, (kt, mi) in enumerate(kts):
                        pss = ps[nsc % 3]
                        psk = f"ps{nsc % 3}"
                        nsc += 1
                        S.mm(pss[:, 0:G], kT[pr, pair, kt * 128:(kt + 1) * 128], qT[pr, pair, q0:q0 + G],
                             ["kT", "qT"], [psk])
                        p = pT[npt % 3]
                        npt += 1
                        if mi is None:
                            S.act(p[:, 0:G], pss[:, 0:G], AF.Exp, [psk], [p.name], scale=0.125)
                        else:
                            s2 = sc[npt % 2]
                            S.tt("dve", s2[:, 0:G], pss[:, 0:G], masks[:, mi, 0:G], ALU.add, [psk, "masks"], [s2.name])
                            S.act(p[:, 0:G], s2[:, 0:G], AF.Exp, [s2.name], [p.name], scale=0.125)
                        pend.append((ki, kt, p))
                        if len(pend) > 2:
                            emit_pv(*pend.pop(0))
                    for it_ in pend:
                        emit_pv(*it_)
                    if mix == "A":
                        S.ts("dve", rec[:, 0:G], acc[64:128, 0:G], sinkE[64:128, hq:hq + 1], None, ALU.add, None,
                             [acck, "sinkE"], ["rec"])
                        S.add("dve", lambda h, G=G: h.reciprocal(rec[:, 0:G], rec[:, 0:G]), ["rec"], ["rec"])
                    else:
                        S.add("dve", lambda h, G=G, acc=acc: h.reciprocal(rec[:, 0:G], acc[64:128, 0:G]),
                              [acck], ["rec"])
                    S.tt("dve", yT[pr, pair, q0:q0 + G], acc[0:64, 0:G], rec[:, 0:G], ALU.mult, [acck, "rec"], ["yT"])
            S.dma("act", dr["Y" + mix][:, :, start:start + T].rearrange("a p t -> p a t"), yT[:, :, 0:T],
                  ["yT"], [("Y" + mix,)])
    S.barrier()


MK.phase_att = _phase_att


def _phase_lru(self, l):
    S, dr, cfg, ps = self.S, self.dr, self.cfg, self.ps
    with ExitStack() as ph:
        sb = lambda name, shape, dt=F32: self.sb(ph, name, shape, dt)
        Tm = max(T_P, cfg.t_s)
        xp = sb("xp", [128, Tm + 3])
        x = sb("x", [128, Tm])
        gate = sb("gate", [128, Tm])
        ra = sb("ra", [128, Tm])
        ib = sb("ib", [128, Tm])
        tmp = sb("tmp", [128, Tm])
        hf = sb("hf", [128, Tm])
        hb = sb("hb", [128, Tm])
        y = sb("y", [128, Tm], BF16)
        cw = sb("cw", [128, 2, 4])
        cb = sb("cb", [128, 2])
        prm = sb("prm", [128, 2, 2, 3])
        cl = sb("cl", [128, 2, 2])
        h0 = sb("h0", [128, 2, 2])
        zero = sb("zero", [128, 1])
        fin = sb("fin", [128, 2])
        wbd = sb("wbd", [128, 2, 2, 2, 128], BF16)
        xbf = sb("xbf", [128, Tm], BF16)
        S.memset("pool", zero[:], 0.0, ["zero"])
        S.memset("pool", wbd[:], 0.0, ["wbd"])
        S.memset("pool", xp[:], 0.0, ["xp"])
        for ct in range(2):
            S.dma("sp", cw[:, ct, :], dr["lru_conv_w"][l, :, ct * 128:(ct + 1) * 128].rearrange("j c -> c j"),
                  [], ["cw"], allow_slow_non_contiguous=True)
            S.dma("sp", cb[:, ct:ct + 1], dr["lru_conv_b"][l:l + 1, ct * 128:(ct + 1) * 128].rearrange("o c -> c o"),
                  [], ["cb"], allow_slow_non_contiguous=True)
            for d in range(2):
                for pi, pn in enumerate(("lru_br", "lru_bi", "lru_lam")):
                    S.dma("sp", prm[:, ct, d, pi:pi + 1],
                          dr[pn][l, d:d + 1, ct * 128:(ct + 1) * 128].rearrange("o c -> c o"), [], ["prm"],
                          allow_slow_non_contiguous=True)
                S.dma("sp", h0[:, ct, d:d + 1],
                      dr["state_lru"][l, d:d + 1, ct * 128:(ct + 1) * 128].rearrange("o c -> c o"), [], ["h0"],
                      allow_slow_non_contiguous=True)
                for wi, wn in enumerate(("lru_wr", "lru_wi")):
                    for bl in range(2):
                        S.dma("pool", wbd[bl * 64:(bl + 1) * 64, ct, d, wi, bl * 64:(bl + 1) * 64],
                              dr[wn][l, d, 2 * ct + bl], ["wbd"], ["wbd"])
        S.act(cl[:], V(prm[:], 2, [[6, 2], [3, 2]]), AF.Exp, ["prm"], ["cl"], scale=-1.0)
        S.act(cl[:], cl[:], AF.Ln, ["cl"], ["cl"], bias=self.one_t[:, 0:1])
        S.ts("dve", cl[:], cl[:], -8.0, None, ALU.mult, None, ["cl"], ["cl"])
        npsum = 0
        stg = cfg.stop_after or 99
        for (start, T, is_sample) in (cfg.seqs if stg > 1 else []):
            for ct in range(2):
                S.dma("sp", xp[:, 2:2 + T], dr["FM"][ct, :, start:start + T], [("FM", start)], ["xp"])
                S.dma("sp", gate[:, 0:T], dr["FM"][2 + ct, :, start:start + T], [("FM", start)], ["gate"])
                if T < Tm:
                    S.memset("pool", xp[:, 2 + T:3 + T], 0.0, ["xp"])
                S.ts("dve", x[:, 0:T], xp[:, 0:T], cw[:, ct, 0:1], cb[:, ct:ct + 1], ALU.mult, ALU.add,
                     ["xp", "cw", "cb"], ["x"])
                for j in range(1, 4):
                    S.stt("dve", x[:, 0:T], xp[:, j:j + T], cw[:, ct, j:j + 1], x[:, 0:T], ALU.mult, ALU.add,
                          ["xp", "cw", "x"], ["x"])
                if stg <= 2:
                    continue
                S.copy("act", xbf[:, 0:T], x[:, 0:T], ["x"], ["xbf"])
                for d in range(2):
                    nch = (T + 511) // 512
                    for c_ in range(nch):
                        c0 = c_ * 512
                        n = min(512, T - c0)
                        for wi, dst, dk in ((0, ra, "ra"), (1, ib, "ib")):
                            pb = ps[npsum % 4]
                            pk = f"ps{npsum % 4}"
                            npsum += 1
                            S.mm(pb[:, 0:n], wbd[:, ct, d, wi, :], xbf[:, c0:c0 + n], ["wbd", "xbf"], [pk])
                            S.act(dst[:, c0:c0 + n], pb[:, 0:n], AF.Sigmoid, [pk, "prm"], [dk],
                                  bias=prm[:, ct, d, wi:wi + 1])
                    if stg <= 3:
                        continue
                    S.act(ra[:, 0:T], ra[:, 0:T], AF.Exp, ["ra", "cl"], ["ra"], scale=cl[:, ct, d:d + 1])
                    S.tt("dve", tmp[:, 0:T], ra[:, 0:T], ra[:, 0:T], ALU.mult, ["ra"], ["tmp"])
                    S.act(tmp[:, 0:T], tmp[:, 0:T], AF.Sqrt, ["tmp"], ["tmp"], scale=-1.0, bias=self.one_t[:, 0:1])
                    S.tt("dve", ib[:, 0:T], ib[:, 0:T], x[:, 0:T], ALU.mult, ["ib", "x"], ["ib"])
                    S.tt("dve", ib[:, 0:T], ib[:, 0:T], tmp[:, 0:T], ALU.mult, ["ib", "tmp"], ["ib"])
                    if stg <= 4:
                        continue
                    init = h0[:, ct, d:d + 1] if is_sample else zero[:, 0:1]
                    hh = hf if d == 0 else hb
                    hk = "hf" if d == 0 else "hb"
                    if d == 0:
                        S.add("dve", lambda h, T=T, init=init: h.tensor_tensor_scan(
                            hf[:, 0:T], ra[:, 0:T], ib[:, 0:T], init, ALU.mult, ALU.add), ["ra", "ib", "h0", "zero"], [hk])
                    else:
                        rv = lambda t, T=T: V(t[:], T - 1, [[-1, T]])
                        S.add("dve", lambda h, T=T, init=init, rv=rv: h.tensor_tensor_scan(
                            rv(hb), rv(ra), rv(ib), init, ALU.mult, ALU.add), ["ra", "ib", "h0", "zero"], [hk])
                    if not is_sample:
                        b_ = start // T_P
                        col = hh[:, T - 1:T] if d == 0 else hh[:, 0:1]
                        S.copy("dve", fin[:, d:d + 1], col, [hk], ["fin"])
                        S.dma("act", dr["new_lru"][b_, l, d:d + 1, ct * 128:(ct + 1) * 128].rearrange("o c -> c o"),
                              fin[:, d:d + 1], ["fin"], [], allow_slow_non_contiguous=True)
                if stg <= 5:
                    continue
                S.tt("dve", tmp[:, 0:T], gate[:, 0:T], gate[:, 0:T], ALU.mult, ["gate"], ["tmp"])
                S.ts("dve", tmp[:, 0:T], tmp[:, 0:T], 0.044715, 1.0, ALU.mult, ALU.add, ["tmp"], ["tmp"])
                S.tt("dve", tmp[:, 0:T], tmp[:, 0:T], gate[:, 0:T], ALU.mult, ["tmp", "gate"], ["tmp"])
                S.act(tmp[:, 0:T], tmp[:, 0:T], AF.Sigmoid, ["tmp"], ["tmp"], scale=1.5957691216057308)
                S.tt("dve", tmp[:, 0:T], tmp[:, 0:T], gate[:, 0:T], ALU.mult, ["tmp", "gate"], ["tmp"])
                S.tt("dve", hf[:, 0:T], hf[:, 0:T], hb[:, 0:T], ALU.add, ["hf", "hb"], ["hf"])
                S.tt("dve", y[:, 0:T], hf[:, 0:T], tmp[:, 0:T], ALU.mult, ["hf", "tmp"], ["y"])
                if stg <= 6:
                    continue
                S.dma("act", dr["YB"][ct, :, start:start + T], y[:, 0:T], ["y"], [("YB",)])
    S.barrier()


MK.phase_lru = _phase_lru


def _phase_gdn(self, l):
    S, dr, cfg, ps = self.S, self.dr, self.cfg, self.ps
    import os
    dbg_sel = os.environ.get("GDN_DBG", "")
    with ExitStack() as ph:
        sb = lambda name, shape, dt=F32: self.sb(ph, name, shape, dt)
        BT = 512
        cwg = sb("cwg", [128, 6, 4])
        for i in range(6):
            S.dma("sp", cwg[:, i, :], dr["gdn_conv_w"][l, :, i * 128:(i + 1) * 128].rearrange("j c -> c j"),
                  [], ["cwg"], allow_slow_non_contiguous=True)
        bd1 = sb("bd1", [128, 128], BF16)
        S.memset("pool", bd1[:], 0.0, ["bd1"])
        S.memset("pool", bd1[0:64, 0:64], 1.0, ["bd1"])
        S.memset("pool", bd1[64:128, 64:128], 1.0, ["bd1"])
        xp = [sb(f"gxp{i}", [128, BT + 3]) for i in range(2)]
        cx = [sb(f"gcx{i}", [128, BT]) for i in range(2)]
        sq = [sb(f"gsq{i}", [128, BT], BF16) for i in range(2)]
        rn = [sb(f"grn{i}", [128, BT]) for i in range(2)]
        ob = [sb(f"gob{i}", [128, BT]) for i in range(2)]
        it = 0
        for (s0, T, is_sample) in cfg.seqs:
            for t0 in range(s0, s0 + T, BT):
                n = min(BT, s0 + T - t0)
                for i in range(6):
                    b = it % 2
                    it += 1
                    x_, c_, q_, r_, o_ = xp[b], cx[b], sq[b], rn[b], ob[b]
                    lo = max(t0 - 2, s0)
                    hi = min(t0 + n + 1, s0 + T)
                    S.memset("pool", x_[:, 0:2], 0.0, [x_.name])
                    S.memset("pool", x_[:, n + 2:n + 3], 0.0, [x_.name])
                    S.dma("sp", x_[:, lo - (t0 - 2):hi - (t0 - 2)], dr["FM"][4 + i, :, lo:hi], [("FM", s0)], [x_.name])
                    S.ts("dve", c_[:, 0:n], x_[:, 0:n], cwg[:, i, 0:1], None, ALU.mult, None, [x_.name, "cwg"], [c_.name])
                    for j in range(1, 4):
                        S.stt("dve", c_[:, 0:n], x_[:, j:j + n], cwg[:, i, j:j + 1], c_[:, 0:n], ALU.mult, ALU.add,
                              [x_.name, "cwg", c_.name], [c_.name])
                    S.act(c_[:, 0:n], c_[:, 0:n], AF.Silu, [c_.name], [c_.name])
                    if i < 4:
                        S.tt("dve", q_[:, 0:n], c_[:, 0:n], c_[:, 0:n], ALU.mult, [c_.name], [q_.name])
                        pb, pk = ps[it % 2], f"ps{it % 2}"
                        S.mm(pb[:, 0:n], bd1[:], q_[:, 0:n], ["bd1", q_.name], [pk])
                        S.act(r_[:, 0:n], pb[:, 0:n], AF.Ln, [pk], [r_.name], bias=self.eps_t[:, 0:1])
                        S.act(r_[:, 0:n], r_[:, 0:n], AF.Exp, [r_.name], [r_.name], scale=-0.5)
                        S.stt("dve", o_[:, 0:n], c_[:, 0:n], (0.125 if i < 2 else 1.0), r_[:, 0:n], ALU.mult, ALU.mult,
                              [c_.name, r_.name], [o_.name])
                        S.dma("act", dr["GN"][i, :, t0:t0 + n], o_[:, 0:n], [o_.name], [("GN", s0)])
                    else:
                        S.dma("act", dr["GN"][i, :, t0:t0 + n], c_[:, 0:n], [c_.name], [("GN", s0)])
    S.barrier()
    with ExitStack() as ph:
        sb = lambda name, shape, dt=F32: self.sb(ph, name, shape, dt)
        GDT = BF16
        HL = GDT != F32
        BL = 256
        NCB = BL // CH
        ctmp = sb("ctmp", [64, 8, 64])
        tri = sb("tri", [64, 2, 64], GDT)
        mb = sb("mb", [64, 2, 64], GDT)
        TRI8 = sb("TRI8", [64, 8, 64])
        ST8 = sb("ST8", [64, 8, 64])
        ID8 = sb("ID8", [64, 8, 64])
        ones64 = sb("ones64", [64, 64], GDT)
        S.memset("pool", ones64[:], 1.0, ["ones64"])
        for ch in range(8):
            d = ch // 4
            S.dma("sp", TRI8[:, ch, :], dr["c_tri_f" if d == 0 else "c_tri_b"], [], ["TRI8"])
            S.dma("sp", ST8[:, ch, :], dr["c_st_f" if d == 0 else "c_st_b"], [], ["ST8"])
            S.dma("sp", ID8[:, ch, :], dr["c_ident"][0:64, 0:64], [], ["ID8"])
        S.copy("dve", tri[:, 0, :], TRI8[:, 0, :], ["TRI8"], ["tri"])
        S.copy("dve", tri[:, 1, :], TRI8[:, 4, :], ["TRI8"], ["tri"])
        S.dma("sp", ctmp[:, 0, :], dr["c_mb_f"], [], ["ctmp"])
        S.dma("sp", ctmp[:, 1, :], dr["c_mb_b"], [], ["ctmp"])
        S.copy("dve", mb[:], ctmp[:, 0:2, :], ["ctmp"], ["mb"])
        idn = (self.ident_b if HL else self.ident_f)[0:64, 0:64]
        QB = [[sb(f"QB{d}{i}", [64, 4, BL], GDT) for i in range(2)] for d in range(2)]
        KB = [[sb(f"KB{d}{i}", [64, 4, BL], GDT) for i in range(2)] for d in range(2)]
        VB = [[sb(f"VB{d}{i}", [64, 4, BL], GDT) for i in range(2)] for d in range(2)]
        KBl = [[sb(f"KBl{d}{i}", [64, 4, BL], GDT) for i in range(2)] for d in range(2)]
        VBl = [[sb(f"VBl{d}{i}", [64, 4, BL], GDT) for i in range(2)] for d in range(2)]
        KVf = [sb(f"KVf{i}", [64, 4, BL]) for i in range(2)]
        Sl = sb("Sl", [64, 8, 64], GDT)
        GBk = [[sb(f"GBk{d}{i}", [64, NCB, 16]) for i in range(2)] for d in range(2)]
        St = sb("St", [64, 8, 64])
        Sb = sb("Sb", [64, 8, 64], GDT)
        names = ["G8", "B8", "GG", "EE", "DL", "EGD", "GLf"]
        smbn = ("G8h", "G8l", "NGh", "NGl")
        f32n = ("DT", "BS", "N0", "Nn", "Y", "U", "T1", "O1", "Oo", "S1")
        b16n = ("GBh", "GBl", "NTh", "NTl", "Nb", "NTt", "QKM", "RK", "KD", "RV", "Yb", "Pa", "PTa", "Pb", "PTb",
                "WT", "VN", "KD2")
        NSLOT = 2
        sm2 = [{n_: sb(f"{n_}s{t}", [64, 16]) for n_ in names} for t in range(NSLOT)]
        smb2 = [{n_: sb(f"{n_}s{t}", [64, 16], GDT) for n_ in smbn} for t in range(NSLOT)]
        big2 = []
        for t in range(NSLOT):
            b_ = {n_: sb(f"{n_}s{t}", [64, 8, 64]) for n_ in f32n}
            b_.update({n_: sb(f"{n_}s{t}", [64, 8, 64], GDT) for n_ in b16n})
            big2.append(b_)
        local_names = set(names) | set(smbn) | set(f32n) | set(b16n)
        Stmp = sb("Stmp", [64, 8, 64])
        all_ps = self.ps

        def make_rename(slot):
            def rn(k):
                if isinstance(k, str):
                    if k in local_names:
                        return f"{k}@{slot}"
                    if k.startswith("ps") and k[2:].isdigit():
                        return f"ps{(int(k[2:]) % 4) + 4 * slot}"
                return k
            return rn

        ILG = int(os.environ.get("GDN_ILG", "4"))

        def chunks(a):
            out, cur_ = [], []
            for it_ in a:
                if len(cur_) >= ILG and not it_[5]:
                    out.append(cur_)
                    cur_ = []
                cur_.append(it_)
            if cur_:
                out.append(cur_)
            return out

        def interleave(a, b):
            ca, cb = chunks(a), chunks(b)
            out = []
            for i in range(max(len(ca), len(cb))):
                if i < len(ca):
                    out.extend(ca[i])
                if i < len(cb):
                    out.extend(cb[i])
            return out

        def bcs(t, c0):
            return V(t[:], c0, [[1, 8], [0, 64]])

        def mm8(out_bank, outk, lhs, lhsk, rhs, rhsk, lhs_fn=None, rhs_fn=None):
            for ch in range(8):
                a_ = lhs_fn(ch) if lhs_fn else lhs[:, ch, :]
                b_ = rhs_fn(ch) if rhs_fn else rhs[:, ch, :]
                ks_ = (list(lhsk) if isinstance(lhsk, (list, tuple)) else [lhsk]) + \
                      (list(rhsk) if isinstance(rhsk, (list, tuple)) else [rhsk])
                S.mm(out_bank[0:64, ch * 64:(ch + 1) * 64], a_, b_, ks_, [outk])

        def P3(bank):
            return V(bank[0:64, :], 0, [[64, 8], [1, 64]])

        for (s0, T, is_sample) in cfg.seqs:
            N = T // CH
            if dbg_sel == "nomain" or (dbg_sel == "p" and is_sample) or (dbg_sel == "s" and not is_sample):
                continue
            if is_sample:
                S.dma("sp", St[:], dr["state_gdn"][l].rearrange("d h k v -> k (d h) v"), [], ["St"])
            else:
                S.memset("pool", St[:], 0.0, ["St"])
            S.copy("act", Sb[:], St[:], ["St"], ["Sb"])
            S.tt("dve", Stmp[:], St[:], Sb[:], ALU.subtract, ["St", "Sb"], ["Stmp"])
            S.copy("pool", Sl[:], Stmp[:], ["Stmp"], ["Sl"])
            cur = [None, None]
            held = []
            for s in range(N):
                slot = s % NSLOT
                sm, smb, big = sm2[slot], smb2[slot], big2[slot]
                ps = [all_ps[(i % 4) + 4 * slot] for i in range(8)]
                S.begin_capture(make_rename(slot))
                chunk = (s, N - 1 - s)
                bi = (s // NCB) % 2
                S.cap_i = 2
                for d in range(2):
                    blk = chunk[d] // NCB
                    if cur[d] != blk:
                        cur[d] = blk
                        tb0 = s0 + blk * BL
                        S.dma("pool", QB[d][bi][:],
                              dr["GN"][0:2, :, tb0:tb0 + BL].rearrange("a (b p) t -> p (a b) t", p=64),
                              [("GN", s0)], [QB[d][bi].name])
                        for ki, (dst, dstl, base) in enumerate(((KB, KBl, 2), (VB, VBl, 4))):
                            f_ = KVf[ki]
                            S.dma("sp", f_[:],
                                  dr["GN"][base:base + 2, :, tb0:tb0 + BL].rearrange("a (b p) t -> p (a b) t", p=64),
                                  [("GN", s0)], [f_.name])
                            S.copy("act", dst[d][bi][:], f_[:], [f_.name], [dst[d][bi].name])
                            S.tt("dve", f_[:], f_[:], dst[d][bi][:], ALU.subtract, [f_.name, dst[d][bi].name], [f_.name])
                            S.copy("pool", dstl[d][bi][:], f_[:], [f_.name], [dstl[d][bi].name])
                        S.dma("sp", GBk[d][bi][:], dr["TMG"][tb0:tb0 + BL, :].rearrange("(n c) f -> c n f", c=CH),
                              [("TMG", tb0 - tb0 % 128), ("TMG", tb0 - tb0 % 128 + 128)], [GBk[d][bi].name])
                S.cap_i = 0
                qb = [QB[d][bi] for d in range(2)]
                kb = [KB[d][bi] for d in range(2)]
                vb = [VB[d][bi] for d in range(2)]
                gbk = [GBk[d][bi] for d in range(2)]
                co = [(chunk[d] % NCB) * CH for d in range(2)]
                cn = [chunk[d] % NCB for d in range(2)]
                kT = lambda ch: kb[ch // 4][:, ch % 4, co[ch // 4]:co[ch // 4] + CH]
                qT = lambda ch: qb[ch // 4][:, ch % 4, co[ch // 4]:co[ch // 4] + CH]
                vT = lambda ch: vb[ch // 4][:, ch % 4, co[ch // 4]:co[ch // 4] + CH]
                kbl = [KBl[d][bi] for d in range(2)]
                vbl = [VBl[d][bi] for d in range(2)]
                kTl = lambda ch: kbl[ch // 4][:, ch % 4, co[ch // 4]:co[ch // 4] + CH]
                vTl = lambda ch: vbl[ch // 4][:, ch % 4, co[ch // 4]:co[ch // 4] + CH]
                kbk = [kb[0].name, kb[1].name]
                G8, B8, GG, EE, DL, EGD, GLf = (sm[n_] for n_ in names)
                G8h, G8l, NGh, NGl = (smb[n_] for n_ in ("G8h", "G8l", "NGh", "NGl"))
                for d in range(2):
                    S.copy("pool", G8[:, d * 4:(d + 1) * 4], gbk[d][:, cn[d], d * 4:(d + 1) * 4], [gbk[d].name], ["G8"])
                    S.copy("pool", B8[:, d * 4:(d + 1) * 4], gbk[d][:, cn[d], 8 + d * 4:8 + (d + 1) * 4],
                           [gbk[d].name], ["B8"])
                S.copy("act", G8h[:, 0:8], G8[:, 0:8], ["G8"], ["G8h"])
                S.tt("dve", GLf[:, 0:8], G8[:, 0:8], G8h[:, 0:8], ALU.subtract, ["G8", "G8h"], ["GLf"])
                S.copy("act", G8l[:, 0:8], GLf[:, 0:8], ["GLf"], ["G8l"])
                S.ts("pool", NGh[:, 0:8], G8h[:, 0:8], -1.0, None, ALU.mult, None, ["G8h"], ["NGh"])
                S.ts("pool", NGl[:, 0:8], G8l[:, 0:8], -1.0, None, ALU.mult, None, ["G8l"], ["NGl"])
                pg, pgk = ps[0], "ps0"
                gl_ = ((G8h, "G8h"), (G8l, "G8l")) if HL else ((G8h, "G8h"),)
                ng_ = len(gl_) - 1
                for gi, (gt, gk_) in enumerate(gl_):
                    S.mm(pg[0:64, 0:4], tri[:, 0, :], gt[:, 0:4], ["tri", gk_], [pgk], start=(gi == 0), stop=(gi == ng_))
                for gi, (gt, gk_) in enumerate(gl_):
                    S.mm(pg[0:64, 4:8], tri[:, 1, :], gt[:, 4:8], ["tri", gk_], [pgk], start=(gi == 0), stop=(gi == ng_))
                for gi, (gt, gk_) in enumerate(gl_):
                    S.mm(pg[0:64, 8:16], ones64[:], gt[:, 0:8], ["ones64", gk_], [pgk], start=(gi == 0), stop=(gi == ng_))
                S.copy("act", GG[:], pg[0:64, 0:16], [pgk], ["GG"])
                S.act(EE[:], GG[:], AF.Exp, ["GG"], ["EE"])
                S.tt("dve", DL[:, 0:8], GG[:, 8:16], GG[:, 0:8], ALU.subtract, ["GG"], ["DL"])
                S.act(EGD[:, 0:8], DL[:, 0:8], AF.Exp, ["DL"], ["EGD"])
                GBh, GBl, NTh, NTl, DT = big["GBh"], big["GBl"], big["NTh"], big["NTl"], big["DT"]
                S.copy("dve", GBh[:], bcs(G8h, 0), ["G8h"], ["GBh"])
                S.copy("pool", GBl[:], bcs(G8l, 0), ["G8l"], ["GBl"])
                S.tt("dve", NTh[:], TRI8[:], bcs(NGh, 0), ALU.mult, ["TRI8", "NGh"], ["NTh"])
                S.tt("pool", NTl[:], TRI8[:], bcs(NGl, 0), ALU.mult, ["TRI8", "NGl"], ["NTl"])
                pd, pdk = ps[1], "ps1"
                for ch in range(8):
                    d = ch // 4
                    o_ = pd[0:64, ch * 64:(ch + 1) * 64]
                    S.mm(o_, GBh[:, ch, :], tri[:, d, :], ["GBh", "tri"], [pdk], start=True, stop=False)
                    if HL:
                        S.mm(o_, GBl[:, ch, :], tri[:, d, :], ["GBl", "tri"], [pdk], start=False, stop=False)
                    S.mm(o_, NTh[:, ch, :], ones64[:], ["NTh", "ones64"], [pdk], start=False, stop=False)
                    if HL:
                        S.mm(o_, NTl[:, ch, :], ones64[:], ["NTl", "ones64"], [pdk], start=False, stop=False)
                    S.mm(o_, idn, mb[:, d, :], ["ident_b", "ident_f", "mb"], [pdk], start=False, stop=True)
                S.act(DT[:], P3(pd), AF.Exp, [pdk], ["DT"])
                pkk, pkkk = ps[2], "ps2"
                pqk, pqkk = ps[3], "ps3"
                mm8(pkk, pkkk, None, kbk, None, kbk, lhs_fn=kT, rhs_fn=kT)
                mm8(pqk, pqkk, None, kbk, None, [qb[0].name, qb[1].name], lhs_fn=kT, rhs_fn=qT)
                for d in range(2):
                    pass
                BS, N0, Nn, Nb, QKM = big["BS"], big["N0"], big["Nn"], big["Nb"], big["QKM"]
                S.tt("pool", BS[:], ST8[:], bcs(B8, 0), ALU.mult, ["ST8", "B8"], ["BS"])
                S.tt("dve", N0[:], P3(pkk), DT[:], ALU.mult, [pkkk, "DT"], ["N0"])
                S.tt("dve", Nn[:], N0[:], BS[:], ALU.mult, ["N0", "BS"], ["Nn"])
                S.copy("act", Nb[:], Nn[:], ["Nn"], ["Nb"])
                S.tt("dve", QKM[:], P3(pqk), DT[:], ALU.mult, [pqkk, "DT"], ["QKM"])
                pkt, pktk = ps[4], "ps4"
                pvt, pvtk = ps[5], "ps5"
                mm8(pkt, pktk, None, kbk, None, "ident_b", lhs_fn=kT, rhs_fn=lambda ch: idn)
                for ch in range(8):
                    o_ = pvt[0:64, ch * 64:(ch + 1) * 64]
                    S.mm(o_, vT(ch), idn, [vb[0].name, vb[1].name, "ident_b", "ident_f"], [pvtk], start=True, stop=not HL)
                    if HL:
                        S.mm(o_, vTl(ch), idn, [vbl[0].name, vbl[1].name, "ident_b", "ident_f"], [pvtk], start=False, stop=True)
                KD, RV = big["KD"], big["U"]
                S.tt("dve", KD[:], P3(pkt), bcs(EGD, 0), ALU.mult, [pktk, "EGD"], ["KD"])
                S.copy("act", RV[:], P3(pvt), [pvtk], ["U"])
                Y, Yb, NTt = big["Y"], big["Yb"], big["NTt"]
                Yl = big["RK"]
                sp_i = [0]

                def split(src_ap, srck, hi, hik, lo, lok):
                    tmpf = big["O1"] if sp_i[0] % 2 == 0 else big["Oo"]
                    tk = "O1" if sp_i[0] % 2 == 0 else "Oo"
                    sp_i[0] += 1
                    S.copy("act", hi[:], src_ap, [srck], [hik])
                    S.tt("dve", lo[:], src_ap, hi[:], ALU.subtract, [srck, hik], [lok])

                def mm3(bank, bk, ah, ahk, al, alk, bh, bhk, bl, blk, full=True):
                    for ch in range(8):
                        o_ = bank[0:64, ch * 64:(ch + 1) * 64]
                        S.mm(o_, ah[:, ch, :], bh[:, ch, :], [ahk, bhk], [bk], start=True, stop=not full)
                        if full:
                            S.mm(o_, ah[:, ch, :], bl[:, ch, :], [ahk, blk], [bk], start=False, stop=False)
                            S.mm(o_, al[:, ch, :], bh[:, ch, :], [alk, bhk], [bk], start=False, stop=True)

                Nl, NTl_ = big["GBh"], big["GBl"]
                split(Nn[:], "Nn", Nb, "Nb", Nl, "GBh")
                pn, pnk = ps[6], "ps6"
                for ch in range(8):
                    o_ = pn[0:64, ch * 64:(ch + 1) * 64]
                    S.mm(o_, Nb[:, ch, :], idn, ["Nb", "ident_b"], [pnk], start=True, stop=False)
                    S.mm(o_, Nl[:, ch, :], idn, ["GBh", "ident_b"], [pnk], start=False, stop=True)
                split(P3(pn), pnk, NTt, "NTt", NTl_, "GBl")
                S.tt("dve", Y[:], ID8[:], Nn[:], ALU.subtract, ["ID8", "Nn"], ["Y"])
                split(Y[:], "Y", Yb, "Yb", Yl, "RK")
                cur_ = (Nb, "Nb", Nl, "GBh", NTt, "NTt", NTl_, "GBl")
                alt = [(big["Pa"], "Pa", big["NTh"], "NTh", big["PTa"], "PTa", big["NTl"], "NTl"),
                       (big["Pb"], "Pb", big["KD2"], "KD2", big["PTb"], "PTb", big["RV"], "RV")]
                for m_ in range(5):
                    Ph, Phk, Pl, Plk, PTh, PThk, PTl, PTlk = cur_
                    nPh, nPhk, nPl, nPlk, nPTh, nPThk, nPTl, nPTlk = alt[m_ % 2]
                    fullp = m_ < 4
                    fulln = m_ < 3
                    p2t, p2tk = ps[7], "ps7"
                    mm3(p2t, p2tk, Ph, Phk, Pl, Plk, PTh, PThk, PTl, PTlk, full=fullp)
                    if fulln:
                        split(P3(p2t), p2tk, nPTh, nPThk, nPTl, nPTlk)
                    else:
                        S.copy("act", nPTh[:], P3(p2t), [p2tk], [nPThk])
                    if m_ < 4:
                        p2, p2k = ps[6], "ps6"
                        mm3(p2, p2k, PTh, PThk, PTl, PTlk, Ph, Phk, Pl, Plk, full=fullp)
                        if fulln:
                            split(P3(p2), p2k, nPh, nPhk, nPl, nPlk)
                        else:
                            S.copy("dve", nPh[:], P3(p2), [p2k], [nPhk])
                    py, pyk = ps[0], "ps0"
                    mm3(py, pyk, nPTh, nPThk, nPTl, nPTlk, Yb, "Yb", Yl, "RK", full=fulln)
                    S.tt("dve", Y[:], Y[:], P3(py), ALU.add, ["Y", pyk], ["Y"])
                    if fulln and m_ < 4:
                        split(Y[:], "Y", Yb, "Yb", Yl, "RK")
                    else:
                        S.copy("act", Yb[:], Y[:], ["Y"], ["Yb"])
                    cur_ = alt[m_ % 2]
                S.mark()
                T1, VN, O1, Oo, S1 = big["T1"], big["VN"], big["O1"], big["Oo"], big["S1"]
                Rb = big["WT"]
                p1, p1k = ps[3], "ps3"
                for ch in range(8):
                    o_ = p1[0:64, ch * 64:(ch + 1) * 64]
                    S.mm(o_, kT(ch), Sb[:, ch, :], [kbk[0], kbk[1], "Sb"], [p1k], start=True, stop=not HL)
                    if HL:
                        S.mm(o_, kT(ch), Sl[:, ch, :], [kbk[0], kbk[1], "Sl"], [p1k], start=False, stop=False)
                        S.mm(o_, kTl(ch), Sb[:, ch, :], [kbl[0].name, kbl[1].name, "Sb"], [p1k], start=False, stop=True)
                S.tt("dve", T1[:], P3(p1), bcs(EE, 0), ALU.mult, [p1k, "EE"], ["T1"])
                S.tt("dve", Rb[:], RV[:], T1[:], ALU.subtract, ["U", "T1"], ["WT"])
                px, pxk = ps[1], "ps1"
                mm8(px, pxk, Yb, "Yb", Rb, "WT")
                S.tt("dve", VN[:], P3(px), bcs(B8, 0), ALU.mult, [pxk, "B8"], ["VN"])
                po1, po1k = ps[4], "ps4"
                po2, po2k = ps[5], "ps5"
                psu, psuk = ps[6], "ps6"
                mm8(po1, po1k, None, [qb[0].name, qb[1].name], Sb, "Sb", lhs_fn=qT)
                mm8(po2, po2k, QKM, "QKM", VN, "VN")
                mm8(psu, psuk, KD, "KD", VN, "VN")
                S.tt("dve", O1[:], P3(po1), bcs(EE, 0), ALU.mult, [po1k, "EE"], ["O1"])
                S.tt("dve", Oo[:], O1[:], P3(po2), ALU.add, ["O1", po2k], ["Oo"])
                S.tt("pool", S1[:], St[:], bcs(EE, 8), ALU.mult, ["St", "EE"], ["S1"])
                S.tt("dve", St[:], S1[:], P3(psu), ALU.add, ["S1", psuk], ["St"])
                S.copy("act", Sb[:], St[:], ["St"], ["Sb"])
                S.tt("dve", S1[:], St[:], Sb[:], ALU.subtract, ["St", "Sb"], ["S1"])
                S.copy("pool", Sl[:], S1[:], ["S1"], ["Sl"])
                for d in range(2):
                    tk0 = s0 + chunk[d] * CH
                    S.dma("sp", dr["OF" if d == 0 else "OB"][tk0:tk0 + CH, :],
                          V(Oo[:], d * 256, [[1, 256]]), ["Oo"], [("O", d, tk0 - tk0 % 128)])
                held.append(S.end_capture())
                if len(held) == NSLOT or s == N - 1:
                    for h_ in held:
                        S.commit(h_[2])
                    preps = held[0][0]
                    for h_ in held[1:]:
                        preps = (preps + h_[0]) if os.environ.get("GDN_NOIL") else interleave(preps, h_[0])
                    S.commit(preps)
                    for h_ in held:
                        S.commit(h_[1])
                    held = []
            if not is_sample:
                b_ = s0 // T_P
                S.dma("sp", dr["new_gdn"][b_, l].rearrange("d h k v -> k (d h) v"), St[:], ["St"], [])
    S.barrier()


MK.phase_gdn = _phase_gdn


def _phase_C1(self, l):
    S, dr, cfg, ps = self.S, self.dr, self.cfg, self.ps
    with ExitStack() as ph:
        sb = lambda name, shape, dt=F32: self.sb(ph, name, shape, dt)
        wbr = sb("wbr", [128, 4, 2, D], BF16)
        wmg = sb("wmg", [128, NKC, 4 * D], BF16)
        wout = sb("wout", [128, NKC, D], BF16)
        for m in range(4):
            S.dma("pool", wbr[:, m, :, :], dr["w_branch"][l, m].rearrange("(kc p) n -> p kc n", p=128), [], [("wbr", m)])
        for kc in range(NKC):
            S.dma("pool", wmg[:, kc, :], dr["w_merge"][l, kc * 128:(kc + 1) * 128, :], [], [("wmg", kc)])
        S.dma("pool", wout[:], dr["w_out"][l].rearrange("(kc p) n -> p kc n", p=128), [], ["wout"])
        wbrk = [("wbr", m) for m in range(4)]
        wmgk = [("wmg", kc) for kc in range(NKC)]
        bm = sb("bm", [1, 4 * D])
        S.dma("sp", bm[:], dr["b_merge"][l:l + 1, :], [], ["bm"])
        onesr = sb("onesr", [1, 128], BF16)
        S.memset("pool", onesr[:], 1.0, ["onesr"])
        bmh = sb("bmh", [1, 4 * D], BF16)
        bml = sb("bml", [1, 4 * D], BF16)
        S.copy("act", bmh[:], bm[:], ["bm"], ["bmh"])
        S.tt("dve", bm[:], bm[:], bmh[:], ALU.subtract, ["bm", "bmh"], ["bm"])
        S.copy("act", bml[:], bm[:], ["bm"], ["bml"])
        GA1 = [self.load_mod(ph, l, g, 2, f"GA1_{g}") for g in range(2)]
        gng = sb("gng", [128, 4, 64])
        for hh in range(4):
            S.dma("sp", gng[:, hh, :], dr["gdn_norm_g"][l:l + 1, :].partition_broadcast(128), [], ["gng"])
        xt = [sb(f"cxt{i}", [128, D]) for i in range(2)]
        hT = [sb(f"chT{i}", [128, NKC, 128], BF16) for i in range(2)]
        brT = [sb(f"brT{i}", [128, 4, 2, 128], BF16) for i in range(2)]
        oft = [sb(f"oft{i}", [128, 256]) for i in range(2)]
        obt = [sb(f"obt{i}", [128, 256]) for i in range(2)]
        gzt = [sb(f"gzt{i}", [128, 256]) for i in range(2)]
        osq = sb("osq", [128, 256])
        ss4 = sb("ss4", [128, 4])
        ycb = sb("ycb", [128, 256], BF16)
        gsb = sb("gsb", [128, D])
        mergeds = [sb(f"merged{i}", [128, D]) for i in range(2)]
        tmp = sb("ctmp", [128, D])
        tmpt = sb("ctmpt", [128, D])
        held_tail = None
        mbf = sb("mbf", [128, D], BF16)
        mT = sb("mT", [128, NKC, 128], BF16)
        x1 = [sb(f"x1_{i}", [128, D]) for i in range(2)]
        ntile = cfg.nt // 128
        for ti in range(ntile):
            t0 = ti * 128
            g = self.group_of(t0)
            b = ti % 2
            x, h, br, of_, ob_, gz = xt[b], hT[b], brT[b], oft[b], obt[b], gzt[b]
            merged = mergeds[b]
            S.dma("sp", x[:], self.xsrc(l, t0, 128), [("X", l, t0)], [x.name])
            S.dma("sp", h[:], dr["HT"][:, :, t0:t0 + 128].rearrange("k p t -> p k t"), [("HT", t0 - t0 % 512), ("HT", t0 - t0 % 256)], [h.name])
            for m, nm in ((0, "YA"), (1, "YB"), (3, "YD")):
                S.dma("sp", br[:, m, :, :], dr[nm][:, :, t0:t0 + 128].rearrange("a p t -> p a t"), [(nm,)], [br.name])
            S.dma("sp", of_[:], dr["OF"][t0:t0 + 128, :], [("O", 0, t0)], [of_.name])
            S.dma("sp", ob_[:], dr["OB"][t0:t0 + 128, :], [("O", 1, t0)], [ob_.name])
            S.dma("sp", gz[:], dr["TMZ"][t0:t0 + 128, :], [("TMZ", t0)], [gz.name])
            S.tt("dve", of_[:], of_[:], ob_[:], ALU.add, [of_.name, ob_.name], [of_.name])
            S.act(osq[:], of_[:], AF.Square, [of_.name], ["osq"])
            S.add("dve", lambda h_: h_.reduce_sum(ss4[:], V(osq[:], 0, [[64, 4], [1, 64]]), AX.X), ["osq"], ["ss4"])
            S.act(ss4[:], ss4[:], AF.Ln, ["ss4"], ["ss4"], scale=1.0 / HD, bias=self.eps_t[:, 0:1])
            S.act(ss4[:], ss4[:], AF.Exp, ["ss4"], ["ss4"], scale=-0.5)
            S.tt("dve", V(of_[:], 0, [[64, 4], [1, 64]]), V(of_[:], 0, [[64, 4], [1, 64]]), V(ss4[:], 0, [[1, 4], [0, 64]]),
                 ALU.mult, [of_.name, "ss4"], [of_.name])
            S.tt("dve", of_[:], of_[:], V(gng[:], 0, [[1, 256]]), ALU.mult, [of_.name, "gng"], [of_.name])
            S.act(gz[:], gz[:], AF.Silu, [gz.name], [gz.name])
            S.tt("dve", ycb[:], of_[:], gz[:], ALU.mult, [of_.name, gz.name], ["ycb"])
            ptb = ps[0][:].bitcast(BF16)
            for c2 in range(2):
                S.tr(ptb[:, c2 * 128:(c2 + 1) * 128], ycb[:, c2 * 128:(c2 + 1) * 128], self.ident_b[:],
                     ["ycb", "ident_b"], ["ps0"])
            S.copy("act", br[:, 2, :, :], V(ptb, 0, [[128, 2], [1, 128]]), ["ps0"], [br.name])
            for m in range(4):
                pp = (0, 1) if m % 2 == 0 else (4, 5)
                pg = (2, 3) if m % 2 == 0 else (6, 7)
                for hf_ in range(2):
                    n0 = hf_ * 512
                    for kc in range(2):
                        S.mm(ps[pp[hf_]][:], br[:, m, kc, :], wbr[:, m, kc, n0:n0 + 512], [br.name, wbrk[m]],
                             [f"ps{pp[hf_]}"], start=(kc == 0), stop=(kc == 1))
                    for kc in range(NKC):
                        S.mm(ps[pg[hf_]][:], h[:, kc, :], wmg[:, kc, m * D + n0:m * D + n0 + 512], [h.name, wmgk[kc]],
                             [f"ps{pg[hf_]}"], start=(kc == 0), stop=False)
                    S.mm(ps[pg[hf_]][:], onesr[:], bmh[:, m * D + n0:m * D + n0 + 512], ["onesr", "bmh"],
                         [f"ps{pg[hf_]}"], start=False, stop=False)
                    S.mm(ps[pg[hf_]][:], onesr[:], bml[:, m * D + n0:m * D + n0 + 512], ["onesr", "bml"],
                         [f"ps{pg[hf_]}"], start=False, stop=True)
                    S.act(gsb[:, n0:n0 + 512], ps[pg[hf_]][:], AF.Sigmoid, [f"ps{pg[hf_]}"], [("gsb", hf_)])
                    if m == 0:
                        S.tt("dve", merged[:, n0:n0 + 512], gsb[:, n0:n0 + 512], ps[pp[hf_]][:], ALU.mult,
                             [("gsb", hf_), f"ps{pp[hf_]}"], [("merged", b, hf_)])
                    else:
                        S.tt("dve", tmp[:, n0:n0 + 512], gsb[:, n0:n0 + 512], ps[pp[hf_]][:], ALU.mult,
                             [("gsb", hf_), f"ps{pp[hf_]}"], [("ctmp", hf_)])
                        S.tt("pool", merged[:, n0:n0 + 512], merged[:, n0:n0 + 512], tmp[:, n0:n0 + 512], ALU.add,
                             [("merged", b, hf_), ("ctmp", hf_)], [("merged", b, hf_)])
            if held_tail is not None:
                S.commit(held_tail)
                held_tail = None
            S.begin_capture(lambda k_: k_)
            S.copy("act", mbf[:], merged[:], [("merged", b, 0), ("merged", b, 1)], ["mbf"])
            ptb = ps[0][:].bitcast(BF16)
            for kc in range(NKC):
                S.tr(ptb[:, kc * 128:(kc + 1) * 128], mbf[:, kc * 128:(kc + 1) * 128], self.ident_b[:],
                     ["mbf", "ident_b"], ["ps0"])
            S.copy("act", mT[:], V(ptb, 0, [[128, NKC], [1, 128]]), ["ps0"], ["mT"])
            if cfg.debug and "MT" in cfg.debug:
                S.dma("act", dr["MT"][:, :, t0:t0 + 128].rearrange("k p t -> p k t"), mT[:], ["mT"], [])
            xo = x1[b]
            for hf_ in range(2):
                n0 = hf_ * 512
                pb, pk = ps[1 + hf_], f"ps{1 + hf_}"
                for kc in range(NKC):
                    S.mm(pb[:], mT[:, kc, :], wout[:, kc, n0:n0 + 512], ["mT", "wout"], [pk],
                         start=(kc == 0), stop=(kc == NKC - 1))
                S.tt("dve", tmpt[:, n0:n0 + 512], pb[:], GA1[g][:, n0:n0 + 512], ALU.mult, [pk, GA1[g].name],
                     [("ctmpt", hf_)])
                S.tt("pool", xo[:, n0:n0 + 512], x[:, n0:n0 + 512], tmpt[:, n0:n0 + 512], ALU.add,
                     [x.name, ("ctmpt", hf_)], [xo.name])
            S.dma("act", dr["X1"][t0:t0 + 128, :], xo[:], [xo.name], [("X1", t0)])
            held_tail = S.end_capture()[0]
        if held_tail is not None:
            S.commit(held_tail)
    S.barrier()


def _phase_C2(self, l):
    S, dr, cfg, ps = self.S, self.dr, self.cfg, self.ps
    last = (l == cfg.depth - 1)
    with ExitStack() as ph:
        sb = lambda name, shape, dt=F32: self.sb(ph, name, shape, dt)
        w1 = sb("w1", [128, NKC, DFF], BF16)
        w2 = sb("w2", [128, DFF // 128, D], BF16)
        for kc in range(NKC):
            S.dma("pool", w1[:, kc, :], dr["mlp_w1"][l, kc * 128:(kc + 1) * 128, :], [], [("w1", kc)])
        for q4 in range(4):
            S.dma("pool", w2[:, q4 * 8:(q4 + 1) * 8, :],
                  dr["mlp_w2"][l, q4 * 1024:(q4 + 1) * 1024, :].rearrange("(f p) n -> p f n", p=128), [], [("w2", q4)])
        w1k = [("w1", kc) for kc in range(NKC)]
        G2 = sb("G2", [128, D])
        SH2 = sb("SH2", [128, D])
        GA2 = sb("GA2", [128, D])
        x1 = sb("x1s", [128, 2, D])
        hb = sb("h2b", [128, D], BF16)
        hT = sb("h2T", [128, NKC, 256], BF16)
        hid = [sb(f"hid{i}", [128, 256], BF16) for i in range(3)]
        rl = [sb(f"rl{i}", [128, 256]) for i in range(3)]
        scr = (sb("ss2", [128, 1]), sb("lnm2", [128, 1]), sb("rstd2", [128, 1]), sb("junk2", [128, D], BF16),
               sb("tmp2", [128, D]))
        xo = [sb(f"xo{i}", [128, D]) for i in range(2)]
        curg = None
        nst = cfg.nt // 256
        nh = 0
        for si in range(nst):
            T0 = si * 256
            g = self.group_of(T0)
            if g != curg:
                curg = g
                for t, idx in ((G2, 4), (SH2, 3), (GA2, 5)):
                    S.dma("sp", t[:], dr["MODS"][l, g, idx:idx + 1, :].partition_broadcast(128), [("MODS", l)], [t.name])
            for st in range(2):
                t0 = T0 + st * 128
                S.dma("sp", x1[:, st, :], dr["X1"][t0:t0 + 128, :], [("X1", t0)], [("x1s", st)])
                self.emit_norm_mod(x1[:, st, :], ("x1s", st), G2, SH2, hb[:], "h2b", scr, l)
                ptb = ps[7][:].bitcast(BF16)
                for kc in range(NKC):
                    S.tr(ptb[:, kc * 128:(kc + 1) * 128], hb[:, kc * 128:(kc + 1) * 128], self.ident_b[:],
                         ["h2b", "ident_b"], ["ps7"])
                S.copy("act", V(hT[:], st * 128, [[256, NKC], [1, 128]]), V(ptb, 0, [[128, NKC], [1, 128]]),
                       ["ps7"], ["h2T"])
            NF = DFF // 128

            def emit_w1(fc):
                pb, pk = ps[4 + fc % 3], f"ps{4 + fc % 3}"
                for kc in range(NKC):
                    S.mm(pb[:, 0:256], w1[:, kc, fc * 128:(fc + 1) * 128], hT[:, kc, :], ["h2T", w1k[kc]], [pk],
                         start=(kc == 0), stop=(kc == NKC - 1))
                hd = hid[fc % 3]
                r32 = rl[fc % 3]
                S.act(r32[:], pb[:, 0:256], AF.Relu, [pk], [r32.name])
                S.tt("dve" if fc % 2 == 0 else "pool", hd[:], r32[:], r32[:], ALU.mult, [r32.name], [hd.name])

            def emit_w2(fc):
                hd = hid[fc % 3]
                for st in range(2):
                    for hf_ in range(2):
                        pi = st * 2 + hf_
                        S.mm(ps[pi][:], hd[:, st * 128:(st + 1) * 128], w2[:, fc, hf_ * 512:(hf_ + 1) * 512],
                             [hd.name, ("w2", fc // 8)], [f"ps{pi}"], start=(fc == 0), stop=(fc == NF - 1))

            emit_w1(0)
            for fc in range(NF):
                if fc + 1 < NF:
                    emit_w1(fc + 1)
                emit_w2(fc)
            for st in range(2):
                t0 = T0 + st * 128
                o = xo[st]
                for hf_ in range(2):
                    n0 = hf_ * 512
                    pi = st * 2 + hf_
                    S.tt("dve", scr[4][:, n0:n0 + 512], ps[pi][:], GA2[:, n0:n0 + 512], ALU.mult, [f"ps{pi}", GA2.name],
                         [("tmp2h", hf_)])
                    S.tt("pool", o[:, n0:n0 + 512], x1[:, st, n0:n0 + 512], scr[4][:, n0:n0 + 512], ALU.add,
                         [("x1s", st), ("tmp2h", hf_)], [o.name])
                if not last:
                    S.dma("act", dr["XS"][t0:t0 + 128, :], o[:], [o.name], [("X", l + 1, t0)])
                elif t0 < cfg.np_tok:
                    S.dma("act", dr["y_prompt"][t0:t0 + 128, :], o[:], [o.name], [])
                else:
                    ts_ = t0 - cfg.np_tok
                    S.dma("act", dr["y_sample"][ts_:ts_ + 128, :], o[:], [o.name], [])
    S.barrier()


MK.phase_C1 = _phase_C1
MK.phase_C2 = _phase_C2


def build_program(cfg):
    mk = MK(cfg)
    mk.declare()
    mk.phase_mod()
    for l in range(cfg.depth):
        mk.phase_A(l)
        mk.phase_att(l, "A")
        mk.phase_att(l, "D")
        mk.phase_lru(l)
        mk.phase_gdn(l)
        mk.phase_C1(l)
        mk.phase_C2(l)
    mk.S.emit()
    return mk


_CACHE = {}


def kernel(**inputs):
    n_cores = 8
    B = inputs["x_prompt"].shape[0]
    cfg = Cfg(n_pseq=B // n_cores, t_s=inputs["x_sample"].shape[1], depth=DEPTH)
    if "mk" not in _CACHE:
        _CACHE["mk"] = build_program(cfg)
    mk = _CACHE["mk"]
    consts = const_inputs(cfg)
    maps = make_in_maps(inputs, cfg, n_cores, consts)
    res = run_bass_kernel_spmd(mk.nc, maps, core_ids=list(range(n_cores)))
    R = res.results
    nps = cfg.n_pseq
    nsamp = inputs["x_sample"].shape[0]
    grp = n_cores // nsamp
    y_prompt = np.concatenate([np.asarray(r["y_prompt"]).reshape(nps, T_P, D) for r in R], axis=0)
    y_sample = np.stack([np.asarray(R[s * grp]["y_sample"]) for s in range(nsamp)], axis=0)
    outs = [y_prompt.astype(np.float32), y_sample.astype(np.float32)]
    for n in ("new_a_k", "new_a_v", "new_d_k", "new_d_v"):
        outs.append(np.concatenate([np.asarray(r[n]).reshape(nps, DEPTH, T_P, 2, HD) for r in R], axis=0).astype(np.float32))
    outs.append(np.concatenate([np.asarray(r["new_lru"]) for r in R], axis=0).astype(np.float32))
    outs.append(np.concatenate([np.asarray(r["new_gdn"]) for r in R], axis=0).astype(np.float32))
    return tuple(outs)
```

```python
import numpy as np
from contextlib import ExitStack
import concourse.bass as bass
import concourse.mybir as mybir
from concourse.bass_utils import run_bass_kernel_spmd

F32 = mybir.dt.float32
BF16 = mybir.dt.bfloat16
I32 = mybir.dt.int32
AF = mybir.ActivationFunctionType
ALU = mybir.AluOpType
AX = mybir.AxisListType

GEN = 16000
N_DMA_SEM = 8
N_DMA_POOL = 2
DMA_PER_SEM = 1000


class Op:
    __slots__ = ("eng", "fn", "deps", "needs_sig", "sig", "dma", "dsem", "dval", "idx")

    def __init__(self, eng, fn, dma):
        self.eng = eng
        self.fn = fn
        self.deps = []
        self.needs_sig = False
        self.sig = None
        self.dma = dma
        self.dsem = None
        self.dval = 0
        self.idx = 0


class Sched:
    ENGS = ("pe", "act", "dve", "pool", "sp")

    def __init__(self, nc, stack):
        self.nc = nc
        self.stack = stack
        self.ops = {e: [] for e in self.ENGS}
        self.last_w = {}
        self.readers = {}
        self.dma_sems = {}
        self.dma_cnt = {}
        self.dma_rr = {}
        self.dma_last = {}
        self.nops = 0
        self.cap = None
        self.nsem_extra = 0
        self.dma_final = {"sp": [], "pool": [], "act": []}
        self.bar = []
        self.bar_applied = set(self.ENGS)
        self.dmas_since = []
        for q in ("sp", "pool", "act"):
            self.dma_sems[q] = [stack.enter_context(nc.semaphore(f"dq_{q}_{i}")) for i in range(N_DMA_SEM)]
            self.dma_cnt[q] = [0] * N_DMA_SEM
            self.dma_last[q] = [None] * N_DMA_SEM
            self.dma_rr[q] = 0

    def begin_capture(self, rename):
        self.cap = [[], [], []]
        self.cap_i = 0
        self.cap_rename = rename

    def mark(self):
        if self.cap is not None:
            self.cap_i = 1

    def end_capture(self):
        c = self.cap
        self.cap = None
        return c

    def commit(self, items):
        for it in items:
            self.add(*it[:5])

    def add(self, eng, fn, reads=(), writes=(), dma=False, glue=False):
        if self.cap is not None:
            rn = self.cap_rename
            self.cap[self.cap_i].append((eng, fn, [rn(k) for k in reads], [rn(k) for k in writes], dma, glue))
            return None
        op = Op(eng, fn, dma)
        op.idx = self.nops
        self.nops += 1
        deps = {}

        def dep(o, kind):
            if o is None or o is op:
                return
            k = deps.get(id(o))
            if k is None or (kind == "raw"):
                deps[id(o)] = (o, kind if k is None else ("raw" if "raw" in (kind, k[1]) else k[1]))

        for k in reads:
            dep(self.last_w.get(k), "raw")
        for k in writes:
            dep(self.last_w.get(k), "waw")
            rs = self.readers.get(k, ())
            if len(rs) > 2:
                last = {}
                keep = []
                for r in rs:
                    if r.dma:
                        keep.append(r)
                    else:
                        last[r.eng] = r
                rs = keep + list(last.values())
            for r in rs:
                dep(r, "war")
        if eng not in self.bar_applied:
            self.bar_applied.add(eng)
            for o in self.bar:
                if o.dma or dma or o.eng != eng:
                    dep(o, "raw")
        if dma:
            self.dmas_since.append(op)
            q = eng
            i = self.dma_rr[q]
            self.dma_rr[q] = (i + 1) % (N_DMA_POOL if q == "pool" else N_DMA_SEM)
            dep(self.dma_last[q][i], "raw")
            if self.dma_cnt[q][i] >= 16 * DMA_PER_SEM:
                self.dma_final[q].append((self.dma_sems[q][i], self.dma_cnt[q][i]))
                self.nsem_extra += 1
                self.dma_sems[q][i] = self.stack.enter_context(self.nc.semaphore(f"dq_{q}_{i}_g{self.nsem_extra}"))
                self.dma_cnt[q][i] = 0
            self.dma_cnt[q][i] += 16
            op.dsem = self.dma_sems[q][i]
            op.dval = self.dma_cnt[q][i]
            self.dma_last[q][i] = op
        for o, kind in deps.values():
            if self._need_wait(op, o, kind):
                op.deps.append(o)
                o.needs_sig = True
        for k in reads:
            self.readers.setdefault(k, []).append(op)
        for k in writes:
            self.last_w[k] = op
            self.readers[k] = []
        self.ops[eng].append(op)
        return op

    def barrier(self):
        bar = list(self.dmas_since)
        for e in self.ENGS:
            if self.ops[e]:
                bar.append(self.ops[e][-1])
        old = [o for o in self.bar] if len(self.bar_applied) < len(self.ENGS) else []
        self.bar = bar + old
        self.bar_applied = set()
        self.dmas_since = []

    @staticmethod
    def _need_wait(op, o, kind):
        if o.dma:
            return True
        if op.dma:
            return True
        if o.eng != op.eng:
            return True
        if op.eng == "pe":
            return False
        return kind == "raw"

    def emit(self):
        nc = self.nc
        sems = {}
        for e in self.ENGS:
            n = 0
            for op in self.ops[e]:
                if op.needs_sig and not op.dma:
                    op.sig = n
                    n += 1
            ngen = (n + GEN - 1) // GEN
            sems[e] = [self.stack.enter_context(nc.semaphore(f"cs_{e}_{g}")) for g in range(ngen)]
        self.csems = sems

        def run(eng_name, h):
            waited = {}
            for op in self.ops[eng_name]:
                need = {}
                for d in op.deps:
                    if d.dma:
                        s, v = d.dsem, d.dval
                    else:
                        s, v = sems[d.eng][d.sig // GEN], d.sig % GEN + 1
                    key = id(s)
                    if waited.get(key, 0) >= v:
                        continue
                    if key not in need or need[key][1] < v:
                        need[key] = (s, v)
                for key, (s, v) in need.items():
                    h.wait_ge(s, v)
                    waited[key] = v
                ins = op.fn(h)
                if op.dma:
                    ins.then_inc(op.dsem, 16)
                elif op.needs_sig:
                    ins.then_inc(sems[eng_name][op.sig // GEN], 1)
            if eng_name in self.dma_sems:
                fin = list(self.dma_final[eng_name]) + [(s, self.dma_cnt[eng_name][i])
                                                         for i, s in enumerate(self.dma_sems[eng_name])]
                for s, cnt in fin:
                    if cnt > waited.get(id(s), 0):
                        h.wait_ge(s, cnt)

        with nc.Block() as block:
            @block.tensor
            def _(h):
                run("pe", h)

            @block.scalar
            def _(h):
                run("act", h)

            @block.vector
            def _(h):
                run("dve", h)

            @block.gpsimd
            def _(h):
                run("pool", h)

            @block.sync
            def _(h):
                run("sp", h)

    def dma(self, q, out, in_, reads, writes, **kw):
        return self.add(q, lambda h: h.dma_start(out=out, in_=in_, **kw), reads, writes, dma=True)

    def mm(self, out, lhsT, rhs, reads, writes, start=True, stop=True):
        return self.add("pe", lambda h: h.matmul(out, lhsT, rhs, start=start, stop=stop), reads, writes,
                        glue=(not start))

    def tr(self, out, in_, ident, reads, writes):
        return self.add("pe", lambda h: h.transpose(out, in_, ident), reads, writes)

    def act(self, out, in_, func, reads, writes, bias=0.0, scale=1.0, accum_out=None, eng="act"):
        if accum_out is None:
            return self.add(eng, lambda h: h.activation(out, in_, func, bias=bias, scale=scale), reads, writes)
        return self.add(eng, lambda h: h.activation(out, in_, func, bias=bias, scale=scale, accum_out=accum_out),
                        reads, writes)

    def tt(self, eng, out, in0, in1, op, reads, writes):
        return self.add(eng, lambda h: h.tensor_tensor(out, in0, in1, op), reads, writes)

    def ts(self, eng, out, in0, s1, s2, op0, op1, reads, writes, accum_out=None):
        if op1 is None:
            return self.add(eng, lambda h: h.tensor_scalar(out, in0, s1, None, op0), reads, writes)
        if accum_out is None:
            return self.add(eng, lambda h: h.tensor_scalar(out, in0, s1, s2, op0, op1), reads, writes)
        return self.add(eng, lambda h: h.tensor_scalar(out, in0, s1, s2, op0, op1, accum_out=accum_out),
                        reads, writes)

    def stt(self, eng, out, in0, scalar, in1, op0, op1, reads, writes):
        return self.add(eng, lambda h: h.scalar_tensor_tensor(out, in0, scalar, in1, op0, op1), reads, writes)

    def copy(self, eng, out, in_, reads, writes):
        if eng == "act":
            return self.add(eng, lambda h: h.copy(out, in_), reads, writes)
        return self.add(eng, lambda h: h.tensor_copy(out, in_), reads, writes)

    def memset(self, eng, ap, val, writes):
        return self.add(eng, lambda h: h.memset(ap, val), (), writes)


D = 1024
NKC = 8
HD = 64
DFF = 4096
IN_COLS = 2576
C_AQ, C_AK, C_AV, C_LX, C_LG, C_GQ, C_GK, C_GV, C_GZ, C_GA, C_GB, C_DQ, C_DK, C_DV = (
    0, 256, 384, 512, 768, 1024, 1280, 1536, 1792, 2048, 2056, 2064, 2320, 2448)
EPS = 1e-6
T_P = 256
PAST = 512
GRID_W = 64
DEPTH = 2
CH = 64

WEIGHT_SPECS = [
    ("mod_w", [DEPTH, D, 6 * D]), ("mod_b", [DEPTH, 6 * D]), ("norm1_g", [DEPTH, D]), ("norm2_g", [DEPTH, D]),
    ("w_in", [DEPTH, D, IN_COLS]), ("a_qn_g", [DEPTH, HD]), ("a_kn_g", [DEPTH, HD]), ("a_sink", [DEPTH, 4]),
    ("lru_conv_w", [DEPTH, 4, 256]), ("lru_conv_b", [DEPTH, 256]), ("lru_wr", [DEPTH, 2, 4, 64, 64]),
    ("lru_br", [DEPTH, 2, 256]), ("lru_wi", [DEPTH, 2, 4, 64, 64]), ("lru_bi", [DEPTH, 2, 256]),
    ("lru_lam", [DEPTH, 2, 256]), ("gdn_conv_w", [DEPTH, 4, 768]), ("gdn_a_log", [DEPTH, 2, 4]),
    ("gdn_dt_bias", [DEPTH, 2, 4]), ("gdn_norm_g", [DEPTH, HD]), ("d_qn_g", [DEPTH, HD]), ("d_kn_g", [DEPTH, HD]),
    ("w_branch", [DEPTH, 4, 256, D]), ("w_merge", [DEPTH, D, 4 * D]), ("b_merge", [DEPTH, 4 * D]),
    ("w_out", [DEPTH, D, D]), ("mlp_w1", [DEPTH, D, DFF]), ("mlp_w2", [DEPTH, DFF, D]),
]


class Cfg:
    def __init__(self, n_pseq=4, t_s=4096, depth=DEPTH, debug=False, stop_after=None):
        self.n_pseq = n_pseq
        self.t_s = t_s
        self.depth = depth
        self.np_tok = n_pseq * T_P
        self.nt = self.np_tok + t_s
        self.debug = debug
        self.stop_after = stop_after
        self.seqs = [(j * T_P, T_P, False) for j in range(n_pseq)] + [(self.np_tok, t_s, True)]


def rope_tables(t_s):
    rows = t_s // GRID_W
    row = np.repeat(np.arange(rows, dtype=np.float32), GRID_W)
    col = np.tile(np.arange(GRID_W, dtype=np.float32), rows)
    n_freq = HD // 4
    inv = (np.float32(10000.0) ** (-np.arange(n_freq, dtype=np.float32) / np.float32(n_freq))).astype(np.float32)
    ang = np.stack([row[:, None] * inv, col[:, None] * inv], axis=1).astype(np.float32)
    return (np.cos(ang).astype(np.float32).reshape(t_s, 32), np.sin(ang).astype(np.float32).reshape(t_s, 32))


def const_inputs(cfg):
    cos, sin = rope_tables(cfg.t_s)
    ident = np.eye(128, dtype=np.float32)
    idx = np.arange(64)
    tri_f = (idx[:, None] <= idx[None, :]).astype(np.float32)
    tri_b = (idx[:, None] >= idx[None, :]).astype(np.float32)
    NEGM = -30000.0
    mb_f = np.where(idx[None, :] >= idx[:, None], 0.0, NEGM).astype(np.float32)
    mb_b = np.where(idx[None, :] <= idx[:, None], 0.0, NEGM).astype(np.float32)
    st_f = (idx[None, :] > idx[:, None]).astype(np.float32)
    st_b = (idx[None, :] < idx[:, None]).astype(np.float32)
    i128 = np.arange(128)
    bm_prev = np.where(i128[:, None] >= i128[None, :], 0.0, NEGM).astype(np.float32)
    bm_next = np.where(i128[:, None] <= i128[None, :], 0.0, NEGM).astype(np.float32)
    return dict(rope_cos=cos, rope_sin=sin, c_ident=ident, c_tri_f=tri_f, c_tri_b=tri_b,
                c_mb_f=mb_f, c_mb_b=mb_b, c_st_f=st_f, c_st_b=st_b, c_bm_prev=bm_prev, c_bm_next=bm_next)


def V(ap, off, dims):
    return bass.AP(ap.tensor, ap.offset + off, [list(ap.ap[0])] + [list(d) for d in dims])


class MK:
    def __init__(self, cfg):
        self.cfg = cfg
        self.nc = bass.Bass("TRN2", target_bir_lowering=False)
        self.top = ExitStack()
        self.S = Sched(self.nc, self.top)
        self.dr = {}
        self.uid = 0

    def din(self, name, shape, dt=F32):
        self.dr[name] = self.nc.dram_tensor(name, list(shape), dt, kind="ExternalInput").ap()
        return self.dr[name]

    def dout(self, name, shape, dt=F32):
        self.dr[name] = self.nc.dram_tensor(name, list(shape), dt, kind="ExternalOutput").ap()
        return self.dr[name]

    def dscr(self, name, shape, dt=F32):
        kind = "ExternalOutput" if (self.cfg.debug and name in self.cfg.debug) else "Internal"
        self.dr[name] = self.nc.dram_tensor(name, list(shape), dt, kind=kind).ap()
        return self.dr[name]

    def sb(self, ph, name, shape, dt=F32):
        self.uid += 1
        return ph.enter_context(self.nc.sbuf_tensor(f"{name}_{self.uid}", list(shape), dt))

    def declare(self):
        cfg = self.cfg
        L = cfg.depth
        self.din("x_prompt", [cfg.np_tok, D])
        self.din("x_sample", [cfg.t_s, D])
        self.din("c", [1, D])
        self.din("c_ctx", [1, D])
        for n in ("cache_a_k", "cache_a_v", "cache_d_k", "cache_d_v"):
            self.din(n, [DEPTH, PAST, 128])
        self.din("state_lru", [DEPTH, 2, 256])
        self.din("state_gdn", [DEPTH, 2, 4, 64, 64])
        for n, shp in WEIGHT_SPECS:
            self.din(n, shp)
        for n, a in const_inputs(cfg).items():
            self.din(n, a.shape)
        self.dout("y_prompt", [cfg.np_tok, D])
        self.dout("y_sample", [cfg.t_s, D])
        for n in ("new_a_k", "new_a_v", "new_d_k", "new_d_v"):
            self.dout(n, [cfg.n_pseq, DEPTH, T_P, 128])
        self.dout("new_lru", [cfg.n_pseq, DEPTH, 2, 256])
        self.dout("new_gdn", [cfg.n_pseq, DEPTH, 2, 4, 64, 64])
        NT = cfg.nt
        self.dscr("MODS", [DEPTH, 2, 6, D])
        self.dscr("XS", [NT, D])
        self.dscr("X1", [NT, D])
        self.dscr("HT", [NKC, 128, NT], BF16)
        self.dscr("MT", [NKC, 128, NT], BF16)
        for m in ("A", "D"):
            self.dscr("QT_" + m, [2, 128, NT], BF16)
            self.dscr("KT_" + m, [2, 128, NT], BF16)
            self.dscr("V_" + m, [NT, 128], BF16)
            self.dscr("Y" + m, [2, 128, NT], BF16)
        self.dscr("YB", [2, 128, NT], BF16)
        self.dscr("FM", [10, 128, NT])
        self.dscr("TMZ", [NT, 256])
        self.dscr("TMG", [NT, 16])
        self.dscr("GN", [6, 128, NT])
        self.dscr("OF", [NT, 256])
        self.dscr("OB", [NT, 256])
        nc = self.nc
        self.ps = [self.top.enter_context(nc.psum_tensor(f"psb{i}", [128, 512], F32)) for i in range(8)]
        self.ident_f = self.top.enter_context(nc.sbuf_tensor("ident_f", [128, 128], F32))
        self.ident_b = self.top.enter_context(nc.sbuf_tensor("ident_b", [128, 128], BF16))
        S = self.S
        self.eps_t = self.top.enter_context(nc.sbuf_tensor("eps_t", [128, 1], F32))
        self.one_t = self.top.enter_context(nc.sbuf_tensor("one_t", [128, 1], F32))
        S.memset("pool", self.eps_t[:], EPS, ["eps_t"])
        S.memset("pool", self.one_t[:], 1.0, ["one_t"])
        S.dma("sp", self.ident_f[:], self.dr["c_ident"], [], ["ident_f"])
        S.copy("dve", self.ident_b[:], self.ident_f[:], ["ident_f"], ["ident_b"])

    def group_of(self, t0):
        return 1 if t0 >= self.cfg.np_tok else 0

    def phase_mod(self):
        S, dr = self.S, self.dr
        with ExitStack() as ph:
            condT = self.sb(ph, "condT", [128, NKC, 2])
            scond = self.sb(ph, "scond", [128, NKC, 2])
            S.dma("sp", condT[:, :, 0], dr["c_ctx"].rearrange("o (kc p) -> p (o kc)", p=128), [], ["condT"],
                  allow_slow_non_contiguous=True)
            S.dma("sp", condT[:, :, 1], dr["c"].rearrange("o (kc p) -> p (o kc)", p=128), [], ["condT"],
                  allow_slow_non_contiguous=True)
            S.act(scond[:], condT[:], AF.Silu, ["condT"], ["scond"])
            sch = self.sb(ph, "sch", [128, NKC, 2], BF16)
            scl = self.sb(ph, "scl", [128, NKC, 2], BF16)
            S.copy("act", sch[:], scond[:], ["scond"], ["sch"])
            S.tt("dve", scond[:], scond[:], sch[:], ALU.subtract, ["scond", "sch"], ["scond"])
            S.copy("act", scl[:], scond[:], ["scond"], ["scl"])
            wb = [self.sb(ph, f"modw{i}", [128, NKC, 512], BF16) for i in range(2)]
            modb = self.sb(ph, "modb", [2, 6 * D])
            mods = self.sb(ph, "mods", [2, 6 * D])
            ng = self.sb(ph, "ng", [2, 2, D])
            it = 0
            for l in range(self.cfg.depth):
                S.dma("sp", modb[:], dr["mod_b"][l:l + 1, :].partition_broadcast(2), [], ["modb"])
                S.dma("sp", ng[:, 0, :], dr["norm1_g"][l:l + 1, :].partition_broadcast(2), [], ["ng"])
                S.dma("sp", ng[:, 1, :], dr["norm2_g"][l:l + 1, :].partition_broadcast(2), [], ["ng"])
                for nb in range(12):
                    w = wb[it % 2]
                    wk = f"modw{it % 2}"
                    pk = f"ps{it % 2}"
                    pst = self.ps[it % 2]
                    S.dma("pool", w[:],
                          dr["mod_w"][l, :, nb * 512:(nb + 1) * 512].rearrange("(kc p) n -> p kc n", p=128),
                          [], [wk])
                    for pi_, (sc_, sck) in enumerate(((sch, "sch"), (scl, "scl"))):
                        for kc in range(NKC):
                            S.mm(pst[0:2, :], sc_[:, kc, :], w[:, kc, :], [sck, wk], [pk],
                                 start=(pi_ == 0 and kc == 0), stop=(pi_ == 1 and kc == NKC - 1))
                    S.tt("dve", mods[:, nb * 512:(nb + 1) * 512], pst[0:2, :], modb[:, nb * 512:(nb + 1) * 512],
                         ALU.add, [pk, "modb"], ["mods"])
                    it += 1
                S.stt("dve", mods[:, D:2 * D], mods[:, D:2 * D], 1.0, ng[:, 0, :], ALU.add, ALU.mult,
                      ["mods", "ng"], ["mods"])
                S.stt("dve", mods[:, 4 * D:5 * D], mods[:, 4 * D:5 * D], 1.0, ng[:, 1, :], ALU.add, ALU.mult,
                      ["mods", "ng"], ["mods"])
                S.dma("sp", dr["MODS"][l].rearrange("g s d -> g (s d)"), mods[:], ["mods"], [("MODS", l)])
        S.barrier()

    def load_mod(self, ph, l, g, idx, name):
        t = self.sb(ph, name, [128, D])
        self.S.dma("sp", t[:], self.dr["MODS"][l, g, idx:idx + 1, :].partition_broadcast(128),
                   [("MODS", l)], [t.name])
        return t

    def emit_norm_mod(self, x, xk, Gt, SHt, out, outk, scr, l):
        S = self.S
        ss, lnm, rstd, junk, tmp = scr
        S.memset("pool", ss[:], 0.0, [ss.name])
        S.act(junk[:], x, AF.Square, [xk, ss.name], [junk.name, ss.name], accum_out=ss[:])
        S.act(lnm[:], ss[:], AF.Ln, [ss.name], [lnm.name], scale=1.0 / D, bias=self.eps_t[:, 0:1])
        S.act(rstd[:], lnm[:], AF.Exp, [lnm.name], [rstd.name], scale=-0.5)
        S.stt("dve", tmp[:], x, rstd[:, 0:1], Gt[:], ALU.mult, ALU.mult, [xk, rstd.name, Gt.name], [tmp.name])
        S.tt("dve", out, tmp[:], SHt[:], ALU.add, [tmp.name, SHt.name], [outk])

    def supertiles(self):
        cfg = self.cfg
        out = []
        t = 0
        while t < cfg.np_tok:
            n = min(512, cfg.np_tok - t)
            out.append((t, n))
            t += n
        while t < cfg.nt:
            n = min(512, cfg.nt - t)
            out.append((t, n))
            t += n
        return out

    def xsrc(self, l, t0, n):
        cfg = self.cfg
        if l == 0:
            if t0 < cfg.np_tok:
                return self.dr["x_prompt"][t0:t0 + n, :]
            return self.dr["x_sample"][t0 - cfg.np_tok:t0 - cfg.np_tok + n, :]
        return self.dr["XS"][t0:t0 + n, :]

    def phase_A(self, l):
        S, dr, cfg = self.S, self.dr, self.cfg
        ps = self.ps
        with ExitStack() as ph:
            sb = lambda name, shape, dt=F32: self.sb(ph, name, shape, dt)
            w_in = sb("w_in", [128, NKC, IN_COLS], BF16)
            for kc in range(NKC):
                S.dma("pool", w_in[:, kc, :], dr["w_in"][l, kc * 128:(kc + 1) * 128, :], [], [("w_in", kc)])
            wk = [("w_in", kc) for kc in range(NKC)]
            G1 = [self.load_mod(ph, l, g, 1, f"G1_{g}") for g in range(2)]
            SH1 = [self.load_mod(ph, l, g, 0, f"SH1_{g}") for g in range(2)]
            xt = [sb(f"xt{i}", [128, D]) for i in range(2)]
            hb = [sb(f"hb{i}", [128, D], BF16) for i in range(2)]
            hT = [sb(f"hT{i}", [128, NKC, 512], BF16) for i in range(2)]
            scr = (sb("ss", [128, 1]), sb("lnm", [128, 1]), sb("rstd", [128, 1]), sb("junk", [128, D], BF16),
                   sb("tmp", [128, D]))
            nst_s = cfg.t_s // 128
            cos_t = sb("cos_t", [128, nst_s, 32])
            sin_t = sb("sin_t", [128, nst_s, 32])
            S.dma("sp", cos_t[:], dr["rope_cos"].rearrange("(s p) f -> p s f", p=128), [], ["cos_t"])
            S.dma("sp", sin_t[:], dr["rope_sin"].rearrange("(s p) f -> p s f", p=128), [], ["sin_t"])
            GQK = {}
            for mix, qn, kn in (("A", "a_qn_g", "a_kn_g"), ("D", "d_qn_g", "d_kn_g")):
                t = sb("GQK" + mix, [128, 6, 64])
                for hh in range(6):
                    src = dr[qn if hh < 4 else kn][l:l + 1, :].partition_broadcast(128)
                    S.dma("sp", t[:, hh, :], src, [], [t.name])
                GQK[mix] = t
            dtb = sb("dtb", [128, 8])
            nexpa = sb("nexpa", [128, 8])
            S.dma("sp", dtb[:], dr["gdn_dt_bias"][l:l + 1].rearrange("o a b -> o (a b)").partition_broadcast(128),
                  [], ["dtb"])
            S.dma("sp", nexpa[:], dr["gdn_a_log"][l:l + 1].rearrange("o a b -> o (a b)").partition_broadcast(128),
                  [], ["nexpa"])
            S.act(nexpa[:], nexpa[:], AF.Exp, ["nexpa"], ["nexpa"])
            S.ts("dve", nexpa[:], nexpa[:], -1.0, None, ALU.mult, None, ["nexpa"], ["nexpa"])
            sq = sb("sq", [128, 384])
            ss6 = sb("ss6", [128, 6])
            r6 = sb("r6", [128, 6])
            qkn = sb("qkn", [128, 384])
            qkr = sb("qkr", [128, 384])
            ta = sb("ta", [128, 192])
            tb = sb("tb", [128, 192])
            vbuf = sb("vbuf", [128, 128])
            qb = sb("qb", [128, 256], BF16)
            kd = sb("kd", [128, 256], BF16)
            stage = {m: [sb(f"stg{m}{i}", [128, 4, 512], BF16) for i in range(2)] for m in ("A", "D")}
            vstage = {m: [sb(f"vst{m}{i}", [128, 4, 128], BF16) for i in range(2)] for m in ("A", "D")}
            gzb = [sb(f"gzb{i}", [128, 256]) for i in range(2)]
            gbb = [sb(f"gbb{i}", [128, 16]) for i in range(2)]
            gtmp = sb("gtmp", [128, 8])
            fms = [sb(f"fms{i}", [128, 512]) for i in range(3)]
            fmi = 0

            def att_post(mix, P, pk, t0, st, sti, nsup, is_sample):
                S.act(sq[:], P[:, 0:384], AF.Square, [pk], ["sq"])
                S.add("dve", lambda h: h.reduce_sum(ss6[:], V(sq[:], 0, [[64, 6], [1, 64]]), AX.X), ["sq"], ["ss6"])
                S.act(ss6[:], ss6[:], AF.Ln, ["ss6"], ["ss6"], scale=1.0 / HD, bias=self.eps_t[:, 0:1])
                S.act(r6[:], ss6[:], AF.Exp, ["ss6"], ["r6"], scale=-0.5)
                S.tt("dve", V(qkn[:], 0, [[64, 6], [1, 64]]), V(P[:, 0:384], 0, [[64, 6], [1, 64]]),
                     V(r6[:], 0, [[1, 6], [0, 64]]), ALU.mult, [pk, "r6"], ["qkn"])
                S.tt("dve", qkn[:], qkn[:], V(GQK[mix][:], 0, [[1, 384]]), ALU.mult, ["qkn", GQK[mix].name], ["qkn"])
                src = qkn
                srck = "qkn"
                if is_sample:
                    ti = (t0 - cfg.np_tok) // 128
                    pat = [[64, 6], [32, 2], [1, 16]]
                    x1 = V(qkn[:], 0, pat)
                    x2 = V(qkn[:], 16, pat)
                    cpat = [[0, 6], [16, 2], [1, 16]]
                    cc = V(cos_t[:], ti * 32, cpat)
                    sn = V(sin_t[:], ti * 32, cpat)
                    o1 = V(qkr[:], 0, pat)
                    o2 = V(qkr[:], 16, pat)
                    tpat = [[32, 6], [16, 2], [1, 16]]
                    S.tt("dve", V(ta[:], 0, tpat), x1, cc, ALU.mult, ["qkn", "cos_t"], ["ta"])
                    S.tt("dve", V(tb[:], 0, tpat), x2, sn, ALU.mult, ["qkn", "sin_t"], ["tb"])
                    S.tt("dve", o1, V(ta[:], 0, tpat), V(tb[:], 0, tpat), ALU.subtract, ["ta", "tb"], ["qkr"])
                    S.tt("dve", V(ta[:], 0, tpat), x2, cc, ALU.mult, ["qkn", "cos_t"], ["ta"])
                    S.tt("dve", V(tb[:], 0, tpat), x1, sn, ALU.mult, ["qkn", "sin_t"], ["tb"])
                    S.tt("dve", o2, V(ta[:], 0, tpat), V(tb[:], 0, tpat), ALU.add, ["ta", "tb"], ["qkr"])
                    src = qkr
                    srck = "qkr"
                else:
                    b = t0 // T_P
                    tt0 = t0 % T_P
                    lo = mix.lower()
                    S.copy("act", vbuf[:], P[:, 384:512], [pk], ["vbuf"])
                    S.dma("act", dr[f"new_{lo}_k"][b, l, tt0:tt0 + 128, :], qkn[:, 256:384], ["qkn"], [])
                    S.dma("act", dr[f"new_{lo}_v"][b, l, tt0:tt0 + 128, :], vbuf[:], ["vbuf"], [])
                S.copy("act", qb[:], src[:, 0:256], [srck], ["qb"])
                S.copy("dve", V(kd[:], 0, [[128, 2], [64, 2], [1, 64]]), V(src[:], 256, [[64, 2], [0, 2], [1, 64]]),
                       [srck], ["kd"])
                pq = ps[4 if mix == "A" else 5]
                pqk = f"ps{4 if mix == 'A' else 5}"
                pqb = pq[:].bitcast(BF16)
                S.tr(pqb[:, 0:128], qb[:, 0:128], self.ident_b[:], ["qb", "ident_b"], [pqk])
                S.tr(pqb[:, 128:256], qb[:, 128:256], self.ident_b[:], ["qb", "ident_b"], [pqk])
                S.tr(pqb[:, 256:384], kd[:, 0:128], self.ident_b[:], ["kd", "ident_b"], [pqk])
                S.tr(pqb[:, 384:512], kd[:, 128:256], self.ident_b[:], ["kd", "ident_b"], [pqk])
                stg = stage[mix][sti % 2]
                S.copy("act", V(stg[:], st * 128, [[512, 4], [1, 128]]), V(pqb, 0, [[128, 4], [1, 128]]),
                       [pqk], [stg.name])
                vst = vstage[mix][sti % 2]
                S.copy("dve", vst[:, st, :], P[:, 384:512], [pk], [vst.name])

            for sti, (T0, ntok) in enumerate(self.supertiles()):
                g = self.group_of(T0)
                is_sample = g == 1
                nsub = ntok // 128
                hTt = hT[sti % 2]
                for st in range(nsub):
                    t0 = T0 + st * 128
                    it = sti * 4 + st
                    x = xt[it % 2]
                    S.dma("sp", x[:], self.xsrc(l, t0, 128), [("X", l, t0)], [x.name])
                    h = hb[it % 2]
                    self.emit_norm_mod(x[:], x.name, G1[g], SH1[g], h[:], h.name, scr, l)
                    ptb = ps[3][:].bitcast(BF16)
                    for kc in range(NKC):
                        S.tr(ptb[:, kc * 128:(kc + 1) * 128], h[:, kc * 128:(kc + 1) * 128], self.ident_b[:],
                             [h.name, "ident_b"], ["ps3"])
                    S.copy("act", V(hTt[:], st * 128, [[512, NKC], [1, 128]]), V(ptb, 0, [[128, NKC], [1, 128]]),
                           ["ps3"], [hTt.name])
                    for (pi, c0, ncol) in ((0, C_AQ, 512), (1, C_DQ, 512), (2, C_GZ, 272)):
                        for kc in range(NKC):
                            S.mm(ps[pi][:, 0:ncol], hTt[:, kc, st * 128:(st + 1) * 128], w_in[:, kc, c0:c0 + ncol],
                                 [hTt.name, wk[kc]], [f"ps{pi}"], start=(kc == 0), stop=(kc == NKC - 1))
                    att_post("A", ps[0], "ps0", t0, st, sti, nsub, is_sample)
                    att_post("D", ps[1], "ps1", t0, st, sti, nsub, is_sample)
                    gz = gzb[it % 2]
                    gb = gbb[it % 2]
                    S.copy("act", gz[:], ps[2][:, 0:256], ["ps2"], [gz.name])
                    S.dma("act", dr["TMZ"][t0:t0 + 128, :], gz[:], [gz.name], [("TMZ", t0)])
                    S.tt("dve", gtmp[:], ps[2][:, 256:264], dtb[:], ALU.add, ["ps2", "dtb"], ["gtmp"])
                    S.act(gtmp[:], gtmp[:], AF.Exp, ["gtmp"], ["gtmp"])
                    S.act(gtmp[:], gtmp[:], AF.Ln, ["gtmp"], ["gtmp"], bias=self.one_t[:, 0:1])
                    S.tt("dve", gb[:, 0:8], gtmp[:], nexpa[:], ALU.mult, ["gtmp", "nexpa"], [gb.name])
                    S.act(gb[:, 8:16], ps[2][:, 264:272], AF.Sigmoid, ["ps2"], [gb.name])
                    S.dma("act", dr["TMG"][t0:t0 + 128, :], gb[:], [gb.name], [("TMG", t0)])
                for mix in ("A", "D"):
                    stg = stage[mix][sti % 2]
                    S.dma("act", dr["QT_" + mix][:, :, T0:T0 + ntok].rearrange("a p t -> p a t"),
                          stg[:, 0:2, 0:ntok], [stg.name], [("QT_" + mix, T0)])
                    S.dma("act", dr["KT_" + mix][:, :, T0:T0 + ntok].rearrange("a p t -> p a t"),
                          stg[:, 2:4, 0:ntok], [stg.name], [("KT_" + mix, T0)])
                    vst = vstage[mix][sti % 2]
                    S.dma("act", dr["V_" + mix][T0:T0 + ntok, :].rearrange("(s p) f -> p s f", p=128),
                          vst[:, 0:nsub, :], [vst.name], [("V_" + mix, T0)])
                S.dma("act", dr["HT"][:, :, T0:T0 + ntok].rearrange("k p t -> p k t"), hTt[:, :, 0:ntok],
                      [hTt.name], [("HT", T0)])
                for mt in range(10):
                    c0 = C_LX + mt * 128
                    pi = 6 + (mt % 2)
                    for kc in range(NKC):
                        S.mm(ps[pi][:, 0:ntok], w_in[:, kc, c0:c0 + 128], hTt[:, kc, 0:ntok],
                             [hTt.name, wk[kc]], [f"ps{pi}"], start=(kc == 0), stop=(kc == NKC - 1))
                    f = fms[fmi % 3]
                    fmi += 1
                    if mt % 2 == 0:
                        S.copy("act", f[:, 0:ntok], ps[pi][:, 0:ntok], [f"ps{pi}"], [f.name])
                    else:
                        S.copy("dve", f[:, 0:ntok], ps[pi][:, 0:ntok], [f"ps{pi}"], [f.name])
                    S.dma("act", dr["FM"][mt, :, T0:T0 + ntok], f[:, 0:ntok], [f.name], [("FM", T0)])
        S.barrier()


def make_in_maps(inp, cfg, n_cores, consts):
    maps = []
    f32 = lambda a: np.ascontiguousarray(np.asarray(a, dtype=np.float32))
    nps = cfg.n_pseq
    ngrp = max(1, n_cores // inp["x_sample"].shape[0])
    for i in range(n_cores):
        s = min(i // ngrp, inp["x_sample"].shape[0] - 1)
        m = {}
        m["x_prompt"] = f32(inp["x_prompt"][i * nps:(i + 1) * nps]).reshape(nps * T_P, D)
        m["x_sample"] = f32(inp["x_sample"][s])
        m["c"] = f32(inp["c"][s:s + 1])
        m["c_ctx"] = f32(inp["c_ctx"]).reshape(1, D)
        for n in ("cache_a_k", "cache_a_v", "cache_d_k", "cache_d_v"):
            m[n] = f32(inp[n][s]).reshape(DEPTH, PAST, 128)
        m["state_lru"] = f32(inp["state_lru"][s])
        m["state_gdn"] = f32(inp["state_gdn"][s])
        for n, shp in WEIGHT_SPECS:
            m[n] = f32(inp[n]).reshape(shp)
        m.update(consts)
        maps.append(m)
    return maps


def _phase_att(self, l, mix):
    S, dr, cfg, ps = self.S, self.dr, self.cfg, self.ps
    NEGM = -30000.0
    with ExitStack() as ph:
        sb = lambda name, shape, dt=F32: self.sb(ph, name, shape, dt)
        Tmax = max(T_P, cfg.t_s)
        Smax = PAST + cfg.t_s
        kT = sb("kT", [128, 2, Smax], BF16)
        vA = sb("vA", [128, Smax // 128, 2, 128], BF16)
        qT = sb("qT", [128, 2, Tmax], BF16)
        yT = sb("yT", [128, 2, Tmax], BF16)
        vld = sb("vld", [128, Smax // 128, 128], BF16)
        S.memset("pool", vA[:], 1.0, ["vA"])
        pT = [sb(f"pT{i}", [128, 512], BF16) for i in range(4)]
        sc = [sb(f"sc{i}", [128, 512]) for i in range(2)]
        rec = sb("rec", [64, 512])
        ck = sb("ck", [128, PAST // 128, 128])
        kdup = sb("kdup", [128, PAST // 128, 256], BF16)
        sinkE = sb("sinkE", [128, 4])
        if mix == "A":
            S.dma("sp", sinkE[:], dr["a_sink"][l:l + 1, :].partition_broadcast(128), [], ["sinkE"])
            S.act(sinkE[:], sinkE[:], AF.Exp, ["sinkE"], ["sinkE"])
            bmp = sb("bmp", [128, 128])
            bmn = sb("bmn", [128, 128])
            S.dma("sp", bmp[:], dr["c_bm_prev"], [], ["bmp"])
            S.dma("sp", bmn[:], dr["c_bm_next"], [], ["bmn"])
            masks = sb("masks", [128, 6, 512])
            S.memset("pool", masks[:], NEGM, ["masks"])
            for dd in range(6):
                for j in range(4):
                    diff = (dd - 1) - j
                    if diff == -1:
                        S.copy("dve", masks[:, dd, j * 128:(j + 1) * 128], bmp[:], ["bmp", "masks"], ["masks"])
                    elif diff == 0:
                        S.memset("dve", masks[:, dd, j * 128:(j + 1) * 128], 0.0, ["masks"])
                    elif diff == 1:
                        S.copy("dve", masks[:, dd, j * 128:(j + 1) * 128], bmn[:], ["bmn", "masks"], ["masks"])
        lo = mix.lower()
        nsc = 0
        npt = 0
        nacc = 0
        for (start, T, is_sample) in cfg.seqs:
            nkc = PAST // 128 if is_sample else 0
            if is_sample:
                S.dma("sp", ck[:], dr[f"cache_{lo}_k"][l].rearrange("(s p) f -> p s f", p=128), [], ["ck"])
                for s_ in range(nkc):
                    S.copy("dve", V(kdup[:], s_ * 256, [[128, 2], [64, 2], [1, 64]]),
                           V(ck[:], s_ * 128, [[64, 2], [0, 2], [1, 64]]), ["ck"], ["kdup"])
                for s_ in range(nkc):
                    pqb = ps[3][:].bitcast(BF16)
                    for hh in range(2):
                        S.tr(pqb[:, hh * 128:(hh + 1) * 128], kdup[:, s_, hh * 128:(hh + 1) * 128], self.ident_b[:],
                             ["kdup", "ident_b"], ["ps3"])
                    S.copy("act", V(kT[:], s_ * 128, [[Smax, 2], [1, 128]]), V(pqb, 0, [[128, 2], [1, 128]]),
                           ["ps3"], ["kT"])
                S.dma("sp", ck[:], dr[f"cache_{lo}_v"][l].rearrange("(s p) f -> p s f", p=128), ["ck"], ["ck"])
                S.copy("dve", V(vA[:], 0, [[256, nkc], [128, 2], [1, 64]]), V(ck[:], 0, [[128, nkc], [64, 2], [1, 64]]),
                       ["ck"], ["vA"])
            nkt = T // 128
            S.dma("sp", kT[:, :, nkc * 128:nkc * 128 + T],
                  dr["KT_" + mix][:, :, start:start + T].rearrange("a p t -> p a t"), [("KT_" + mix,)], ["kT"])
            for k0 in range(0, nkt, 4):
                k1 = min(nkt, k0 + 4)
                S.dma("sp", vld[:, k0:k1, :],
                      dr["V_" + mix][start + k0 * 128:start + k1 * 128, :].rearrange("(s p) f -> p s f", p=128),
                      [("V_" + mix,)], ["vld"])
            S.copy("dve", V(vA[:], nkc * 256, [[256, nkt], [128, 2], [1, 64]]),
                   V(vld[:], 0, [[128, nkt], [64, 2], [1, 64]]), ["vld"], ["vA"])
            S.dma("sp", qT[:, :, 0:T], dr["QT_" + mix][:, :, start:start + T].rearrange("a p t -> p a t"),
                  [("QT_" + mix,)], ["qT"])
            G = min(512, T)
            for q0 in range(0, T, G):
                qi0 = q0 // 128
                if mix == "A" and is_sample:
                    kts = [(k_, None) for k_ in range(nkc)]
                    for kl in range(qi0 - 1, qi0 + 5):
                        if 0 <= kl < nkt:
                            kts.append((nkc + kl, kl - qi0 + 1))
                else:
                    kts = [(k_, None) for k_ in range(nkc + nkt)]
                for hq in range(4):
                    pair, g2 = hq // 2, hq % 2
                    pr = slice(g2 * 64, (g2 + 1) * 64)
                    acc = ps[4 + nacc % 2]
                    acck = f"ps{4 + nacc % 2}"
                    nacc += 1
                    pend = []
                    nk = len(kts)

                    def emit_pv(ki, kt, p):
                        S.mm(acc[:, 0:G], vA[:, kt, pair, :], p[:, 0:G], ["vA", p.name], [acck],
                             start=(ki == 0), stop=(ki == nk - 1))

                    for ki, (kt, mi) in enumerate(kts):
                        sbk_ = (0, 1, 2, 3)[nsc % 4]
                        pss = ps[sbk_]
                        psk = f"ps{sbk_}"
                        nsc += 1
                        S.mm(pss[:, 0:G], kT[pr, pair, kt * 128:(kt + 1) * 128], qT[pr, pair, q0:q0 + G],
                             ["kT", "qT"], [psk])
                        p = pT[npt % 4]
                        npt += 1
                        if mi is None:
                            S.act(p[:, 0:G], pss[:, 0:G], AF.Exp, [psk], [p.name], scale=0.125)
                        else:
                            s2 = sc[npt % 2]
                            S.tt("dve", s2[:, 0:G], pss[:, 0:G], masks[:, mi, 0:G], ALU.add, [psk, "masks"], [s2.name])
                            S.act(p[:, 0:G], s2[:, 0:G], AF.Exp, [s2.name], [p.name], scale=0.125)
                        pend.append((ki, kt, p))
                        if len(pend) > 3:
                            emit_pv(*pend.pop(0))
                    for it_ in pend:
                        emit_pv(*it_)
                    if mix == "A":
                        S.ts("dve", rec[:, 0:G], acc[64:128, 0:G], sinkE[64:128, hq:hq + 1], None, ALU.add, None,
                             [acck, "sinkE"], ["rec"])
                        S.add("dve", lambda h, G=G: h.reciprocal(rec[:, 0:G], rec[:, 0:G]), ["rec"], ["rec"])
                    else:
                        S.add("dve", lambda h, G=G, acc=acc: h.reciprocal(rec[:, 0:G], acc[64:128, 0:G]),
                              [acck], ["rec"])
                    S.tt("dve", yT[pr, pair, q0:q0 + G], acc[0:64, 0:G], rec[:, 0:G], ALU.mult, [acck, "rec"], ["yT"])
            S.dma("act", dr["Y" + mix][:, :, start:start + T].rearrange("a p t -> p a t"), yT[:, :, 0:T],
                  ["yT"], [("Y" + mix,)])
    S.barrier()


MK.phase_att = _phase_att


def _phase_lru(self, l):
    S, dr, cfg, ps = self.S, self.dr, self.cfg, self.ps
    with ExitStack() as ph:
        sb = lambda name, shape, dt=F32: self.sb(ph, name, shape, dt)
        Tm = max(T_P, cfg.t_s)
        xp = sb("xp", [128, Tm + 3])
        x = sb("x", [128, Tm])
        gate = sb("gate", [128, Tm])
        ra = sb("ra", [128, Tm])
        ib = sb("ib", [128, Tm])
        tmp = sb("tmp", [128, Tm])
        hf = sb("hf", [128, Tm])
        hb = sb("hb", [128, Tm])
        y = sb("y", [128, Tm], BF16)
        cw = sb("cw", [128, 2, 4])
        cb = sb("cb", [128, 2])
        prm = sb("prm", [128, 2, 2, 3])
        cl = sb("cl", [128, 2, 2])
        h0 = sb("h0", [128, 2, 2])
        zero = sb("zero", [128, 1])
        fin = sb("fin", [128, 2])
        wbd = sb("wbd", [128, 2, 2, 2, 128], BF16)
        xbf = sb("xbf", [128, Tm], BF16)
        S.memset("pool", zero[:], 0.0, ["zero"])
        S.memset("pool", wbd[:], 0.0, ["wbd"])
        S.memset("pool", xp[:], 0.0, ["xp"])
        for ct in range(2):
            S.dma("sp", cw[:, ct, :], dr["lru_conv_w"][l, :, ct * 128:(ct + 1) * 128].rearrange("j c -> c j"),
                  [], ["cw"], allow_slow_non_contiguous=True)
            S.dma("sp", cb[:, ct:ct + 1], dr["lru_conv_b"][l:l + 1, ct * 128:(ct + 1) * 128].rearrange("o c -> c o"),
                  [], ["cb"], allow_slow_non_contiguous=True)
            for d in range(2):
                for pi, pn in enumerate(("lru_br", "lru_bi", "lru_lam")):
                    S.dma("sp", prm[:, ct, d, pi:pi + 1],
                          dr[pn][l, d:d + 1, ct * 128:(ct + 1) * 128].rearrange("o c -> c o"), [], ["prm"],
                          allow_slow_non_contiguous=True)
                S.dma("sp", h0[:, ct, d:d + 1],
                      dr["state_lru"][l, d:d + 1, ct * 128:(ct + 1) * 128].rearrange("o c -> c o"), [], ["h0"],
                      allow_slow_non_contiguous=True)
                for wi, wn in enumerate(("lru_wr", "lru_wi")):
                    for bl in range(2):
                        S.dma("pool", wbd[bl * 64:(bl + 1) * 64, ct, d, wi, bl * 64:(bl + 1) * 64],
                              dr[wn][l, d, 2 * ct + bl], ["wbd"], ["wbd"])
        S.act(cl[:], V(prm[:], 2, [[6, 2], [3, 2]]), AF.Exp, ["prm"], ["cl"], scale=-1.0)
        S.act(cl[:], cl[:], AF.Ln, ["cl"], ["cl"], bias=self.one_t[:, 0:1])
        S.ts("dve", cl[:], cl[:], -8.0, None, ALU.mult, None, ["cl"], ["cl"])
        npsum = 0
        stg = cfg.stop_after or 99
        for (start, T, is_sample) in (cfg.seqs if stg > 1 else []):
            for ct in range(2):
                S.dma("sp", xp[:, 2:2 + T], dr["FM"][ct, :, start:start + T], [("FM", start)], ["xp"])
                S.dma("sp", gate[:, 0:T], dr["FM"][2 + ct, :, start:start + T], [("FM", start)], ["gate"])
                if T < Tm:
                    S.memset("pool", xp[:, 2 + T:3 + T], 0.0, ["xp"])
                S.ts("dve", x[:, 0:T], xp[:, 0:T], cw[:, ct, 0:1], cb[:, ct:ct + 1], ALU.mult, ALU.add,
                     ["xp", "cw", "cb"], ["x"])
                for j in range(1, 4):
                    S.stt("dve", x[:, 0:T], xp[:, j:j + T], cw[:, ct, j:j + 1], x[:, 0:T], ALU.mult, ALU.add,
                          ["xp", "cw", "x"], ["x"])
                if stg <= 2:
                    continue
                S.copy("act", xbf[:, 0:T], x[:, 0:T], ["x"], ["xbf"])
                for d in range(2):
                    nch = (T + 511) // 512
                    for c_ in range(nch):
                        c0 = c_ * 512
                        n = min(512, T - c0)
                        for wi, dst, dk in ((0, ra, "ra"), (1, ib, "ib")):
                            pb = ps[npsum % 4]
                            pk = f"ps{npsum % 4}"
                            npsum += 1
                            S.mm(pb[:, 0:n], wbd[:, ct, d, wi, :], xbf[:, c0:c0 + n], ["wbd", "xbf"], [pk])
                            S.act(dst[:, c0:c0 + n], pb[:, 0:n], AF.Sigmoid, [pk, "prm"], [dk],
                                  bias=prm[:, ct, d, wi:wi + 1])
                    if stg <= 3:
                        continue
                    S.act(ra[:, 0:T], ra[:, 0:T], AF.Exp, ["ra", "cl"], ["ra"], scale=cl[:, ct, d:d + 1])
                    S.tt("dve", tmp[:, 0:T], ra[:, 0:T], ra[:, 0:T], ALU.mult, ["ra"], ["tmp"])
                    S.act(tmp[:, 0:T], tmp[:, 0:T], AF.Sqrt, ["tmp"], ["tmp"], scale=-1.0, bias=self.one_t[:, 0:1])
                    S.tt("dve", ib[:, 0:T], ib[:, 0:T], x[:, 0:T], ALU.mult, ["ib", "x"], ["ib"])
                    S.tt("dve", ib[:, 0:T], ib[:, 0:T], tmp[:, 0:T], ALU.mult, ["ib", "tmp"], ["ib"])
                    if stg <= 4:
                        continue
                    init = h0[:, ct, d:d + 1] if is_sample else zero[:, 0:1]
                    hh = hf if d == 0 else hb
                    hk = "hf" if d == 0 else "hb"
                    if d == 0:
                        S.add("dve", lambda h, T=T, init=init: h.tensor_tensor_scan(
                            hf[:, 0:T], ra[:, 0:T], ib[:, 0:T], init, ALU.mult, ALU.add), ["ra", "ib", "h0", "zero"], [hk])
                    else:
                        rv = lambda t, T=T: V(t[:], T - 1, [[-1, T]])
                        S.add("dve", lambda h, T=T, init=init, rv=rv: h.tensor_tensor_scan(
                            rv(hb), rv(ra), rv(ib), init, ALU.mult, ALU.add), ["ra", "ib", "h0", "zero"], [hk])
                    if not is_sample:
                        b_ = start // T_P
                        col = hh[:, T - 1:T] if d == 0 else hh[:, 0:1]
                        S.copy("dve", fin[:, d:d + 1], col, [hk], ["fin"])
                        S.dma("act", dr["new_lru"][b_, l, d:d + 1, ct * 128:(ct + 1) * 128].rearrange("o c -> c o"),
                              fin[:, d:d + 1], ["fin"], [], allow_slow_non_contiguous=True)
                if stg <= 5:
                    continue
                S.tt("dve", tmp[:, 0:T], gate[:, 0:T], gate[:, 0:T], ALU.mult, ["gate"], ["tmp"])
                S.ts("dve", tmp[:, 0:T], tmp[:, 0:T], 0.044715, 1.0, ALU.mult, ALU.add, ["tmp"], ["tmp"])
                S.tt("dve", tmp[:, 0:T], tmp[:, 0:T], gate[:, 0:T], ALU.mult, ["tmp", "gate"], ["tmp"])
                S.act(tmp[:, 0:T], tmp[:, 0:T], AF.Sigmoid, ["tmp"], ["tmp"], scale=1.5957691216057308)
                S.tt("dve", tmp[:, 0:T], tmp[:, 0:T], gate[:, 0:T], ALU.mult, ["tmp", "gate"], ["tmp"])
                S.tt("dve", hf[:, 0:T], hf[:, 0:T], hb[:, 0:T], ALU.add, ["hf", "hb"], ["hf"])
                S.tt("dve", y[:, 0:T], hf[:, 0:T], tmp[:, 0:T], ALU.mult, ["hf", "tmp"], ["y"])
                if stg <= 6:
                    continue
                S.dma("act", dr["YB"][ct, :, start:start + T], y[:, 0:T], ["y"], [("YB",)])
    S.barrier()


MK.phase_lru = _phase_lru


def _phase_gdn(self, l):
    S, dr, cfg, ps = self.S, self.dr, self.cfg, self.ps
    import os
    dbg_sel = os.environ.get("GDN_DBG", "")
    with ExitStack() as ph:
        sb = lambda name, shape, dt=F32: self.sb(ph, name, shape, dt)
        BT = 512
        cwg = sb("cwg", [128, 6, 4])
        for i in range(6):
            S.dma("sp", cwg[:, i, :], dr["gdn_conv_w"][l, :, i * 128:(i + 1) * 128].rearrange("j c -> c j"),
                  [], ["cwg"], allow_slow_non_contiguous=True)
        bd1 = sb("bd1", [128, 128], BF16)
        S.memset("pool", bd1[:], 0.0, ["bd1"])
        S.memset("pool", bd1[0:64, 0:64], 1.0, ["bd1"])
        S.memset("pool", bd1[64:128, 64:128], 1.0, ["bd1"])
        xp = [sb(f"gxp{i}", [128, BT + 3]) for i in range(2)]
        cx = [sb(f"gcx{i}", [128, BT]) for i in range(2)]
        sq = [sb(f"gsq{i}", [128, BT], BF16) for i in range(2)]
        rn = [sb(f"grn{i}", [128, BT]) for i in range(2)]
        ob = [sb(f"gob{i}", [128, BT]) for i in range(2)]
        it = 0
        for (s0, T, is_sample) in cfg.seqs:
            for t0 in range(s0, s0 + T, BT):
                n = min(BT, s0 + T - t0)
                for i in range(6):
                    b = it % 2
                    it += 1
                    x_, c_, q_, r_, o_ = xp[b], cx[b], sq[b], rn[b], ob[b]
                    lo = max(t0 - 2, s0)
                    hi = min(t0 + n + 1, s0 + T)
                    S.memset("pool", x_[:, 0:2], 0.0, [x_.name])
                    S.memset("pool", x_[:, n + 2:n + 3], 0.0, [x_.name])
                    S.dma("sp", x_[:, lo - (t0 - 2):hi - (t0 - 2)], dr["FM"][4 + i, :, lo:hi], [("FM", s0)], [x_.name])
                    S.ts("dve", c_[:, 0:n], x_[:, 0:n], cwg[:, i, 0:1], None, ALU.mult, None, [x_.name, "cwg"], [c_.name])
                    for j in range(1, 4):
                        S.stt("dve", c_[:, 0:n], x_[:, j:j + n], cwg[:, i, j:j + 1], c_[:, 0:n], ALU.mult, ALU.add,
                              [x_.name, "cwg", c_.name], [c_.name])
                    S.act(c_[:, 0:n], c_[:, 0:n], AF.Silu, [c_.name], [c_.name])
                    if i < 4:
                        S.tt("dve", q_[:, 0:n], c_[:, 0:n], c_[:, 0:n], ALU.mult, [c_.name], [q_.name])
                        pb, pk = ps[it % 2], f"ps{it % 2}"
                        S.mm(pb[:, 0:n], bd1[:], q_[:, 0:n], ["bd1", q_.name], [pk])
                        S.act(r_[:, 0:n], pb[:, 0:n], AF.Ln, [pk], [r_.name], bias=self.eps_t[:, 0:1])
                        S.act(r_[:, 0:n], r_[:, 0:n], AF.Exp, [r_.name], [r_.name], scale=-0.5)
                        S.stt("dve", o_[:, 0:n], c_[:, 0:n], (0.125 if i < 2 else 1.0), r_[:, 0:n], ALU.mult, ALU.mult,
                              [c_.name, r_.name], [o_.name])
                        S.dma("act", dr["GN"][i, :, t0:t0 + n], o_[:, 0:n], [o_.name], [("GN", s0)])
                    else:
                        S.dma("act", dr["GN"][i, :, t0:t0 + n], c_[:, 0:n], [c_.name], [("GN", s0)])
    S.barrier()
    with ExitStack() as ph:
        sb = lambda name, shape, dt=F32: self.sb(ph, name, shape, dt)
        GDT = BF16
        HL = GDT != F32
        BL = 256
        NCB = BL // CH
        ctmp = sb("ctmp", [64, 8, 64])
        tri = sb("tri", [64, 2, 64], GDT)
        mb = sb("mb", [64, 2, 64], GDT)
        TRI8 = sb("TRI8", [64, 8, 64])
        ST8 = sb("ST8", [64, 8, 64])
        ID8 = sb("ID8", [64, 8, 64])
        ones64 = sb("ones64", [64, 64], GDT)
        S.memset("pool", ones64[:], 1.0, ["ones64"])
        for ch in range(8):
            d = ch // 4
            S.dma("sp", TRI8[:, ch, :], dr["c_tri_f" if d == 0 else "c_tri_b"], [], ["TRI8"])
            S.dma("sp", ST8[:, ch, :], dr["c_st_f" if d == 0 else "c_st_b"], [], ["ST8"])
            S.dma("sp", ID8[:, ch, :], dr["c_ident"][0:64, 0:64], [], ["ID8"])
        S.copy("dve", tri[:, 0, :], TRI8[:, 0, :], ["TRI8"], ["tri"])
        S.copy("dve", tri[:, 1, :], TRI8[:, 4, :], ["TRI8"], ["tri"])
        S.dma("sp", ctmp[:, 0, :], dr["c_mb_f"], [], ["ctmp"])
        S.dma("sp", ctmp[:, 1, :], dr["c_mb_b"], [], ["ctmp"])
        S.copy("dve", mb[:], ctmp[:, 0:2, :], ["ctmp"], ["mb"])
        idn = (self.ident_b if HL else self.ident_f)[0:64, 0:64]
        QB = [[sb(f"QB{d}{i}", [64, 4, BL], GDT) for i in range(2)] for d in range(2)]
        KB = [[sb(f"KB{d}{i}", [64, 4, BL], GDT) for i in range(2)] for d in range(2)]
        VB = [[sb(f"VB{d}{i}", [64, 4, BL], GDT) for i in range(2)] for d in range(2)]
        KBl = [[sb(f"KBl{d}{i}", [64, 4, BL], GDT) for i in range(2)] for d in range(2)]
        VBl = [[sb(f"VBl{d}{i}", [64, 4, BL], GDT) for i in range(2)] for d in range(2)]
        KVf = [sb(f"KVf{i}", [64, 4, BL]) for i in range(2)]
        Sl = sb("Sl", [64, 8, 64], GDT)
        GBk = [[sb(f"GBk{d}{i}", [64, NCB, 16]) for i in range(2)] for d in range(2)]
        St = sb("St", [64, 8, 64])
        Sb = sb("Sb", [64, 8, 64], GDT)
        names = ["G8", "B8", "GG", "EE", "DL", "EGD", "GLf"]
        smbn = ("G8h", "G8l", "NGh", "NGl")
        f32n = ("DT", "BS", "N0", "Nn", "Y", "U", "T1", "O1", "Oo", "S1")
        b16n = ("GBh", "GBl", "NTh", "NTl", "Nb", "NTt", "QKM", "RK", "KD", "RV", "Yb", "Pa", "PTa", "Pb", "PTb",
                "WT", "VN", "KD2")
        NSLOT = 2
        sm2 = [{n_: sb(f"{n_}s{t}", [64, 16]) for n_ in names} for t in range(NSLOT)]
        smb2 = [{n_: sb(f"{n_}s{t}", [64, 16], GDT) for n_ in smbn} for t in range(NSLOT)]
        big2 = []
        for t in range(NSLOT):
            b_ = {n_: sb(f"{n_}s{t}", [64, 8, 64]) for n_ in f32n}
            b_.update({n_: sb(f"{n_}s{t}", [64, 8, 64], GDT) for n_ in b16n})
            big2.append(b_)
        local_names = set(names) | set(smbn) | set(f32n) | set(b16n)
        Stmp = sb("Stmp", [64, 8, 64])
        all_ps = self.ps

        def make_rename(slot):
            def rn(k):
                if isinstance(k, str):
                    if k in local_names:
                        return f"{k}@{slot}"
                    if k.startswith("ps") and k[2:].isdigit():
                        return f"ps{(int(k[2:]) % 4) + 4 * slot}"
                return k
            return rn

        ILG = int(os.environ.get("GDN_ILG", "4"))

        def chunks(a):
            out, cur_ = [], []
            for it_ in a:
                if len(cur_) >= ILG and not it_[5]:
                    out.append(cur_)
                    cur_ = []
                cur_.append(it_)
            if cur_:
                out.append(cur_)
            return out

        def interleave(a, b):
            ca, cb = chunks(a), chunks(b)
            out = []
            for i in range(max(len(ca), len(cb))):
                if i < len(ca):
                    out.extend(ca[i])
                if i < len(cb):
                    out.extend(cb[i])
            return out

        def bcs(t, c0):
            return V(t[:], c0, [[1, 8], [0, 64]])

        def mm8(out_bank, outk, lhs, lhsk, rhs, rhsk, lhs_fn=None, rhs_fn=None):
            for ch in range(8):
                a_ = lhs_fn(ch) if lhs_fn else lhs[:, ch, :]
                b_ = rhs_fn(ch) if rhs_fn else rhs[:, ch, :]
                ks_ = (list(lhsk) if isinstance(lhsk, (list, tuple)) else [lhsk]) + \
                      (list(rhsk) if isinstance(rhsk, (list, tuple)) else [rhsk])
                S.mm(out_bank[0:64, ch * 64:(ch + 1) * 64], a_, b_, ks_, [outk])

        def P3(bank):
            return V(bank[0:64, :], 0, [[64, 8], [1, 64]])

        for (s0, T, is_sample) in cfg.seqs:
            N = T // CH
            if dbg_sel == "nomain" or (dbg_sel == "p" and is_sample) or (dbg_sel == "s" and not is_sample):
                continue
            if is_sample:
                S.dma("sp", St[:], dr["state_gdn"][l].rearrange("d h k v -> k (d h) v"), [], ["St"])
            else:
                S.memset("pool", St[:], 0.0, ["St"])
            S.copy("act", Sb[:], St[:], ["St"], ["Sb"])
            S.tt("dve", Stmp[:], St[:], Sb[:], ALU.subtract, ["St", "Sb"], ["Stmp"])
            S.copy("pool", Sl[:], Stmp[:], ["Stmp"], ["Sl"])
            cur = [None, None]
            held = []
            for s in range(N):
                slot = s % NSLOT
                sm, smb, big = sm2[slot], smb2[slot], big2[slot]
                ps = [all_ps[(i % 4) + 4 * slot] for i in range(8)]
                S.begin_capture(make_rename(slot))
                chunk = (s, N - 1 - s)
                bi = (s // NCB) % 2
                S.cap_i = 2
                for d in range(2):
                    blk = chunk[d] // NCB
                    if cur[d] != blk:
                        cur[d] = blk
                        tb0 = s0 + blk * BL
                        S.dma("pool", QB[d][bi][:],
                              dr["GN"][0:2, :, tb0:tb0 + BL].rearrange("a (b p) t -> p (a b) t", p=64),
                              [("GN", s0)], [QB[d][bi].name])
                        for ki, (dst, dstl, base) in enumerate(((KB, KBl, 2), (VB, VBl, 4))):
                            f_ = KVf[ki]
                            S.dma("sp", f_[:],
                                  dr["GN"][base:base + 2, :, tb0:tb0 + BL].rearrange("a (b p) t -> p (a b) t", p=64),
                                  [("GN", s0)], [f_.name])
                            S.copy("act", dst[d][bi][:], f_[:], [f_.name], [dst[d][bi].name])
                            S.tt("dve", f_[:], f_[:], dst[d][bi][:], ALU.subtract, [f_.name, dst[d][bi].name], [f_.name])
                            S.copy("pool", dstl[d][bi][:], f_[:], [f_.name], [dstl[d][bi].name])
                        S.dma("sp", GBk[d][bi][:], dr["TMG"][tb0:tb0 + BL, :].rearrange("(n c) f -> c n f", c=CH),
                              [("TMG", tb0 - tb0 % 128), ("TMG", tb0 - tb0 % 128 + 128)], [GBk[d][bi].name])
                S.cap_i = 0
                qb = [QB[d][bi] for d in range(2)]
                kb = [KB[d][bi] for d in range(2)]
                vb = [VB[d][bi] for d in range(2)]
                gbk = [GBk[d][bi] for d in range(2)]
                co = [(chunk[d] % NCB) * CH for d in range(2)]
                cn = [chunk[d] % NCB for d in range(2)]
                kT = lambda ch: kb[ch // 4][:, ch % 4, co[ch // 4]:co[ch // 4] + CH]
                qT = lambda ch: qb[ch // 4][:, ch % 4, co[ch // 4]:co[ch // 4] + CH]
                vT = lambda ch: vb[ch // 4][:, ch % 4, co[ch // 4]:co[ch // 4] + CH]
                kbl = [KBl[d][bi] for d in range(2)]
                vbl = [VBl[d][bi] for d in range(2)]
                kTl = lambda ch: kbl[ch // 4][:, ch % 4, co[ch // 4]:co[ch // 4] + CH]
                vTl = lambda ch: vbl[ch // 4][:, ch % 4, co[ch // 4]:co[ch // 4] + CH]
                kbk = [kb[0].name, kb[1].name]
                G8, B8, GG, EE, DL, EGD, GLf = (sm[n_] for n_ in names)
                G8h, G8l, NGh, NGl = (smb[n_] for n_ in ("G8h", "G8l", "NGh", "NGl"))
                for d in range(2):
                    S.copy("pool", G8[:, d * 4:(d + 1) * 4], gbk[d][:, cn[d], d * 4:(d + 1) * 4], [gbk[d].name], ["G8"])
                    S.copy("pool", B8[:, d * 4:(d + 1) * 4], gbk[d][:, cn[d], 8 + d * 4:8 + (d + 1) * 4],
                           [gbk[d].name], ["B8"])
                S.copy("act", G8h[:, 0:8], G8[:, 0:8], ["G8"], ["G8h"])
                S.tt("dve", GLf[:, 0:8], G8[:, 0:8], G8h[:, 0:8], ALU.subtract, ["G8", "G8h"], ["GLf"])
                S.copy("act", G8l[:, 0:8], GLf[:, 0:8], ["GLf"], ["G8l"])
                S.ts("pool", NGh[:, 0:8], G8h[:, 0:8], -1.0, None, ALU.mult, None, ["G8h"], ["NGh"])
                S.ts("pool", NGl[:, 0:8], G8l[:, 0:8], -1.0, None, ALU.mult, None, ["G8l"], ["NGl"])
                pg, pgk = ps[0], "ps0"
                gl_ = ((G8h, "G8h"), (G8l, "G8l")) if HL else ((G8h, "G8h"),)
                ng_ = len(gl_) - 1
                for gi, (gt, gk_) in enumerate(gl_):
                    S.mm(pg[0:64, 0:4], tri[:, 0, :], gt[:, 0:4], ["tri", gk_], [pgk], start=(gi == 0), stop=(gi == ng_))
                for gi, (gt, gk_) in enumerate(gl_):
                    S.mm(pg[0:64, 4:8], tri[:, 1, :], gt[:, 4:8], ["tri", gk_], [pgk], start=(gi == 0), stop=(gi == ng_))
                for gi, (gt, gk_) in enumerate(gl_):
                    S.mm(pg[0:64, 8:16], ones64[:], gt[:, 0:8], ["ones64", gk_], [pgk], start=(gi == 0), stop=(gi == ng_))
                S.copy("act", GG[:], pg[0:64, 0:16], [pgk], ["GG"])
                S.act(EE[:], GG[:], AF.Exp, ["GG"], ["EE"])
                S.tt("dve", DL[:, 0:8], GG[:, 8:16], GG[:, 0:8], ALU.subtract, ["GG"], ["DL"])
                S.act(EGD[:, 0:8], DL[:, 0:8], AF.Exp, ["DL"], ["EGD"])
                GBh, GBl, NTh, NTl, DT = big["GBh"], big["GBl"], big["NTh"], big["NTl"], big["DT"]
                S.copy("dve", GBh[:], bcs(G8h, 0), ["G8h"], ["GBh"])
                S.copy("pool", GBl[:], bcs(G8l, 0), ["G8l"], ["GBl"])
                S.tt("dve", NTh[:], TRI8[:], bcs(NGh, 0), ALU.mult, ["TRI8", "NGh"], ["NTh"])
                S.tt("pool", NTl[:], TRI8[:], bcs(NGl, 0), ALU.mult, ["TRI8", "NGl"], ["NTl"])
                pd, pdk = ps[1], "ps1"
                for ch in range(8):
                    d = ch // 4
                    o_ = pd[0:64, ch * 64:(ch + 1) * 64]
                    S.mm(o_, GBh[:, ch, :], tri[:, d, :], ["GBh", "tri"], [pdk], start=True, stop=False)
                    if HL:
                        S.mm(o_, GBl[:, ch, :], tri[:, d, :], ["GBl", "tri"], [pdk], start=False, stop=False)
                    S.mm(o_, NTh[:, ch, :], ones64[:], ["NTh", "ones64"], [pdk], start=False, stop=False)
                    if HL:
                        S.mm(o_, NTl[:, ch, :], ones64[:], ["NTl", "ones64"], [pdk], start=False, stop=False)
                    S.mm(o_, idn, mb[:, d, :], ["ident_b", "ident_f", "mb"], [pdk], start=False, stop=True)
                S.act(DT[:], P3(pd), AF.Exp, [pdk], ["DT"])
                pkk, pkkk = ps[2], "ps2"
                pqk, pqkk = ps[3], "ps3"
                mm8(pkk, pkkk, None, kbk, None, kbk, lhs_fn=kT, rhs_fn=kT)
                mm8(pqk, pqkk, None, kbk, None, [qb[0].name, qb[1].name], lhs_fn=kT, rhs_fn=qT)
                for d in range(2):
                    pass
                BS, N0, Nn, Nb, QKM = big["BS"], big["N0"], big["Nn"], big["Nb"], big["QKM"]
                S.tt("pool", BS[:], ST8[:], bcs(B8, 0), ALU.mult, ["ST8", "B8"], ["BS"])
                S.tt("dve", N0[:], P3(pkk), DT[:], ALU.mult, [pkkk, "DT"], ["N0"])
                S.tt("pool", Nn[:], N0[:], BS[:], ALU.mult, ["N0", "BS"], ["Nn"])
                S.copy("act", Nb[:], Nn[:], ["Nn"], ["Nb"])
                S.tt("dve", QKM[:], P3(pqk), DT[:], ALU.mult, [pqkk, "DT"], ["QKM"])
                pkt, pktk = ps[4], "ps4"
                pvt, pvtk = ps[5], "ps5"
                mm8(pkt, pktk, None, kbk, None, "ident_b", lhs_fn=kT, rhs_fn=lambda ch: idn)
                for ch in range(8):
                    o_ = pvt[0:64, ch * 64:(ch + 1) * 64]
                    S.mm(o_, vT(ch), idn, [vb[0].name, vb[1].name, "ident_b", "ident_f"], [pvtk], start=True, stop=not HL)
                    if HL:
                        S.mm(o_, vTl(ch), idn, [vbl[0].name, vbl[1].name, "ident_b", "ident_f"], [pvtk], start=False, stop=True)
                KD, RV = big["KD"], big["U"]
                S.tt("dve", KD[:], P3(pkt), bcs(EGD, 0), ALU.mult, [pktk, "EGD"], ["KD"])
                S.copy("act", RV[:], P3(pvt), [pvtk], ["U"])
                Y, Yb, NTt = big["Y"], big["Yb"], big["NTt"]
                Yl = big["RK"]
                sp_i = [0]

                def split(src_ap, srck, hi, hik, lo, lok):
                    tmpf = big["O1"] if sp_i[0] % 2 == 0 else big["Oo"]
                    tk = "O1" if sp_i[0] % 2 == 0 else "Oo"
                    sp_i[0] += 1
                    S.copy("act", hi[:], src_ap, [srck], [hik])
                    S.tt("dve", lo[:], src_ap, hi[:], ALU.subtract, [srck, hik], [lok])

                def mm3(bank, bk, ah, ahk, al, alk, bh, bhk, bl, blk, full=True):
                    for ch in range(8):
                        o_ = bank[0:64, ch * 64:(ch + 1) * 64]
                        S.mm(o_, ah[:, ch, :], bh[:, ch, :], [ahk, bhk], [bk], start=True, stop=not full)
                        if full:
                            S.mm(o_, ah[:, ch, :], bl[:, ch, :], [ahk, blk], [bk], start=False, stop=False)
                            S.mm(o_, al[:, ch, :], bh[:, ch, :], [alk, bhk], [bk], start=False, stop=True)

                Nl, NTl_ = big["GBh"], big["GBl"]
                split(Nn[:], "Nn", Nb, "Nb", Nl, "GBh")
                pn, pnk = ps[6], "ps6"
                for ch in range(8):
                    o_ = pn[0:64, ch * 64:(ch + 1) * 64]
                    S.mm(o_, Nb[:, ch, :], idn, ["Nb", "ident_b"], [pnk], start=True, stop=False)
                    S.mm(o_, Nl[:, ch, :], idn, ["GBh", "ident_b"], [pnk], start=False, stop=True)
                split(P3(pn), pnk, NTt, "NTt", NTl_, "GBl")
                S.tt("dve", Y[:], ID8[:], Nn[:], ALU.subtract, ["ID8", "Nn"], ["Y"])
                split(Y[:], "Y", Yb, "Yb", Yl, "RK")
                cur_ = (Nb, "Nb", Nl, "GBh", NTt, "NTt", NTl_, "GBl")
                alt = [(big["Pa"], "Pa", big["NTh"], "NTh", big["PTa"], "PTa", big["NTl"], "NTl"),
                       (big["Pb"], "Pb", big["KD2"], "KD2", big["PTb"], "PTb", big["RV"], "RV")]
                for m_ in range(5):
                    Ph, Phk, Pl, Plk, PTh, PThk, PTl, PTlk = cur_
                    nPh, nPhk, nPl, nPlk, nPTh, nPThk, nPTl, nPTlk = alt[m_ % 2]
                    fullp = m_ < 4
                    fulln = m_ < 3
                    p2t, p2tk = ps[7], "ps7"
                    mm3(p2t, p2tk, Ph, Phk, Pl, Plk, PTh, PThk, PTl, PTlk, full=fullp)
                    if fulln:
                        split(P3(p2t), p2tk, nPTh, nPThk, nPTl, nPTlk)
                    else:
                        S.copy("act", nPTh[:], P3(p2t), [p2tk], [nPThk])
                    if m_ < 4:
                        p2, p2k = ps[6], "ps6"
                        mm3(p2, p2k, PTh, PThk, PTl, PTlk, Ph, Phk, Pl, Plk, full=fullp)
                        if fulln:
                            split(P3(p2), p2k, nPh, nPhk, nPl, nPlk)
                        else:
                            S.copy("dve", nPh[:], P3(p2), [p2k], [nPhk])
                    py, pyk = ps[0], "ps0"
                    mm3(py, pyk, nPTh, nPThk, nPTl, nPTlk, Yb, "Yb", Yl, "RK", full=fulln)
                    S.tt("dve", Y[:], Y[:], P3(py), ALU.add, ["Y", pyk], ["Y"])
                    if fulln and m_ < 4:
                        split(Y[:], "Y", Yb, "Yb", Yl, "RK")
                    else:
                        S.copy("act", Yb[:], Y[:], ["Y"], ["Yb"])
                    cur_ = alt[m_ % 2]
                S.mark()
                T1, VN, O1, Oo, S1 = big["T1"], big["VN"], big["O1"], big["Oo"], big["S1"]
                Rb = big["WT"]
                p1, p1k = ps[3], "ps3"
                for ch in range(8):
                    o_ = p1[0:64, ch * 64:(ch + 1) * 64]
                    S.mm(o_, kT(ch), Sb[:, ch, :], [kbk[0], kbk[1], "Sb"], [p1k], start=True, stop=not HL)
                    if HL:
                        S.mm(o_, kT(ch), Sl[:, ch, :], [kbk[0], kbk[1], "Sl"], [p1k], start=False, stop=False)
                        S.mm(o_, kTl(ch), Sb[:, ch, :], [kbl[0].name, kbl[1].name, "Sb"], [p1k], start=False, stop=True)
                S.tt("dve", T1[:], P3(p1), bcs(EE, 0), ALU.mult, [p1k, "EE"], ["T1"])
                S.tt("pool", Rb[:], RV[:], T1[:], ALU.subtract, ["U", "T1"], ["WT"])
                px, pxk = ps[1], "ps1"
                mm8(px, pxk, Yb, "Yb", Rb, "WT")
                S.tt("dve", VN[:], P3(px), bcs(B8, 0), ALU.mult, [pxk, "B8"], ["VN"])
                po1, po1k = ps[4], "ps4"
                po2, po2k = ps[5], "ps5"
                psu, psuk = ps[6], "ps6"
                mm8(po1, po1k, None, [qb[0].name, qb[1].name], Sb, "Sb", lhs_fn=qT)
                mm8(po2, po2k, QKM, "QKM", VN, "VN")
                mm8(psu, psuk, KD, "KD", VN, "VN")
                S.tt("dve", O1[:], P3(po1), bcs(EE, 0), ALU.mult, [po1k, "EE"], ["O1"])
                S.tt("dve", Oo[:], O1[:], P3(po2), ALU.add, ["O1", po2k], ["Oo"])
                S.tt("pool", S1[:], St[:], bcs(EE, 8), ALU.mult, ["St", "EE"], ["S1"])
                S.tt("dve", St[:], S1[:], P3(psu), ALU.add, ["S1", psuk], ["St"])
                S.copy("act", Sb[:], St[:], ["St"], ["Sb"])
                S.tt("dve", S1[:], St[:], Sb[:], ALU.subtract, ["St", "Sb"], ["S1"])
                S.copy("pool", Sl[:], S1[:], ["S1"], ["Sl"])
                for d in range(2):
                    tk0 = s0 + chunk[d] * CH
                    S.dma("sp", dr["OF" if d == 0 else "OB"][tk0:tk0 + CH, :],
                          V(Oo[:], d * 256, [[1, 256]]), ["Oo"], [("O", d, tk0 - tk0 % 128)])
                held.append(S.end_capture())
                if len(held) == NSLOT or s == N - 1:
                    for h_ in held:
                        S.commit(h_[2])
                    preps = held[0][0]
                    for h_ in held[1:]:
                        preps = (preps + h_[0]) if os.environ.get("GDN_NOIL") else interleave(preps, h_[0])
                    S.commit(preps)
                    for h_ in held:
                        S.commit(h_[1])
                    held = []
            if not is_sample:
                b_ = s0 // T_P
                S.dma("sp", dr["new_gdn"][b_, l].rearrange("d h k v -> k (d h) v"), St[:], ["St"], [])
    S.barrier()


MK.phase_gdn = _phase_gdn


def _phase_C1(self, l):
    S, dr, cfg, ps = self.S, self.dr, self.cfg, self.ps
    with ExitStack() as ph:
        sb = lambda name, shape, dt=F32: self.sb(ph, name, shape, dt)
        wbr = sb("wbr", [128, 4, 2, D], BF16)
        wmg = sb("wmg", [128, NKC, 4 * D], BF16)
        wout = sb("wout", [128, NKC, D], BF16)
        for m in range(4):
            S.dma("pool", wbr[:, m, :, :], dr["w_branch"][l, m].rearrange("(kc p) n -> p kc n", p=128), [], [("wbr", m)])
        for kc in range(NKC):
            S.dma("pool", wmg[:, kc, :], dr["w_merge"][l, kc * 128:(kc + 1) * 128, :], [], [("wmg", kc)])
        S.dma("pool", wout[:], dr["w_out"][l].rearrange("(kc p) n -> p kc n", p=128), [], ["wout"])
        wbrk = [("wbr", m) for m in range(4)]
        wmgk = [("wmg", kc) for kc in range(NKC)]
        bm = sb("bm", [1, 4 * D])
        S.dma("sp", bm[:], dr["b_merge"][l:l + 1, :], [], ["bm"])
        onesr = sb("onesr", [1, 128], BF16)
        S.memset("pool", onesr[:], 1.0, ["onesr"])
        bmh = sb("bmh", [1, 4 * D], BF16)
        bml = sb("bml", [1, 4 * D], BF16)
        S.copy("act", bmh[:], bm[:], ["bm"], ["bmh"])
        S.tt("dve", bm[:], bm[:], bmh[:], ALU.subtract, ["bm", "bmh"], ["bm"])
        S.copy("act", bml[:], bm[:], ["bm"], ["bml"])
        GA1 = [self.load_mod(ph, l, g, 2, f"GA1_{g}") for g in range(2)]
        gng = sb("gng", [128, 4, 64])
        for hh in range(4):
            S.dma("sp", gng[:, hh, :], dr["gdn_norm_g"][l:l + 1, :].partition_broadcast(128), [], ["gng"])
        xt = [sb(f"cxt{i}", [128, D]) for i in range(2)]
        hT = [sb(f"chT{i}", [128, NKC, 128], BF16) for i in range(2)]
        brT = [sb(f"brT{i}", [128, 4, 2, 128], BF16) for i in range(2)]
        oft = [sb(f"oft{i}", [128, 256]) for i in range(2)]
        obt = [sb(f"obt{i}", [128, 256]) for i in range(2)]
        gzt = [sb(f"gzt{i}", [128, 256]) for i in range(2)]
        osq = sb("osq", [128, 256])
        ss4 = sb("ss4", [128, 4])
        ycb = sb("ycb", [128, 256], BF16)
        gsb = sb("gsb", [128, D])
        merged = sb("merged", [128, D])
        tmp = sb("ctmp", [128, D])
        mbf = sb("mbf", [128, D], BF16)
        mT = sb("mT", [128, NKC, 128], BF16)
        x1 = [sb(f"x1_{i}", [128, D]) for i in range(2)]
        ntile = cfg.nt // 128
        for ti in range(ntile):
            t0 = ti * 128
            g = self.group_of(t0)
            b = ti % 2
            x, h, br, of_, ob_, gz = xt[b], hT[b], brT[b], oft[b], obt[b], gzt[b]
            S.dma("sp", x[:], self.xsrc(l, t0, 128), [("X", l, t0)], [x.name])
            S.dma("sp", h[:], dr["HT"][:, :, t0:t0 + 128].rearrange("k p t -> p k t"), [("HT", t0 - t0 % 512), ("HT", t0 - t0 % 256)], [h.name])
            for m, nm in ((0, "YA"), (1, "YB"), (3, "YD")):
                S.dma("sp", br[:, m, :, :], dr[nm][:, :, t0:t0 + 128].rearrange("a p t -> p a t"), [(nm,)], [br.name])
            S.dma("sp", of_[:], dr["OF"][t0:t0 + 128, :], [("O", 0, t0)], [of_.name])
            S.dma("sp", ob_[:], dr["OB"][t0:t0 + 128, :], [("O", 1, t0)], [ob_.name])
            S.dma("sp", gz[:], dr["TMZ"][t0:t0 + 128, :], [("TMZ", t0)], [gz.name])
            S.tt("dve", of_[:], of_[:], ob_[:], ALU.add, [of_.name, ob_.name], [of_.name])
            S.act(osq[:], of_[:], AF.Square, [of_.name], ["osq"])
            S.add("dve", lambda h_: h_.reduce_sum(ss4[:], V(osq[:], 0, [[64, 4], [1, 64]]), AX.X), ["osq"], ["ss4"])
            S.act(ss4[:], ss4[:], AF.Ln, ["ss4"], ["ss4"], scale=1.0 / HD, bias=self.eps_t[:, 0:1])
            S.act(ss4[:], ss4[:], AF.Exp, ["ss4"], ["ss4"], scale=-0.5)
            S.tt("dve", V(of_[:], 0, [[64, 4], [1, 64]]), V(of_[:], 0, [[64, 4], [1, 64]]), V(ss4[:], 0, [[1, 4], [0, 64]]),
                 ALU.mult, [of_.name, "ss4"], [of_.name])
            S.tt("dve", of_[:], of_[:], V(gng[:], 0, [[1, 256]]), ALU.mult, [of_.name, "gng"], [of_.name])
            S.act(gz[:], gz[:], AF.Silu, [gz.name], [gz.name])
            S.tt("dve", ycb[:], of_[:], gz[:], ALU.mult, [of_.name, gz.name], ["ycb"])
            ptb = ps[0][:].bitcast(BF16)
            for c2 in range(2):
                S.tr(ptb[:, c2 * 128:(c2 + 1) * 128], ycb[:, c2 * 128:(c2 + 1) * 128], self.ident_b[:],
                     ["ycb", "ident_b"], ["ps0"])
            S.copy("act", br[:, 2, :, :], V(ptb, 0, [[128, 2], [1, 128]]), ["ps0"], [br.name])
            for m in range(4):
                pp = (0, 1) if m % 2 == 0 else (4, 5)
                pg = (2, 3) if m % 2 == 0 else (6, 7)
                for hf_ in range(2):
                    n0 = hf_ * 512
                    for kc in range(2):
                        S.mm(ps[pp[hf_]][:], br[:, m, kc, :], wbr[:, m, kc, n0:n0 + 512], [br.name, wbrk[m]],
                             [f"ps{pp[hf_]}"], start=(kc == 0), stop=(kc == 1))
                    for kc in range(NKC):
                        S.mm(ps[pg[hf_]][:], h[:, kc, :], wmg[:, kc, m * D + n0:m * D + n0 + 512], [h.name, wmgk[kc]],
                             [f"ps{pg[hf_]}"], start=(kc == 0), stop=False)
                    S.mm(ps[pg[hf_]][:], onesr[:], bmh[:, m * D + n0:m * D + n0 + 512], ["onesr", "bmh"],
                         [f"ps{pg[hf_]}"], start=False, stop=False)
                    S.mm(ps[pg[hf_]][:], onesr[:], bml[:, m * D + n0:m * D + n0 + 512], ["onesr", "bml"],
                         [f"ps{pg[hf_]}"], start=False, stop=True)
                    S.act(gsb[:, n0:n0 + 512], ps[pg[hf_]][:], AF.Sigmoid, [f"ps{pg[hf_]}"], [("gsb", hf_)])
                    if m == 0:
                        S.tt("dve", merged[:, n0:n0 + 512], gsb[:, n0:n0 + 512], ps[pp[hf_]][:], ALU.mult,
                             [("gsb", hf_), f"ps{pp[hf_]}"], [("merged", hf_)])
                    else:
                        S.tt("dve", tmp[:, n0:n0 + 512], gsb[:, n0:n0 + 512], ps[pp[hf_]][:], ALU.mult,
                             [("gsb", hf_), f"ps{pp[hf_]}"], [("ctmp", hf_)])
                        S.tt("pool", merged[:, n0:n0 + 512], merged[:, n0:n0 + 512], tmp[:, n0:n0 + 512], ALU.add,
                             [("merged", hf_), ("ctmp", hf_)], [("merged", hf_)])
            S.copy("act", mbf[:], merged[:], [("merged", 0), ("merged", 1)], ["mbf"])
            ptb = ps[0][:].bitcast(BF16)
            for kc in range(NKC):
                S.tr(ptb[:, kc * 128:(kc + 1) * 128], mbf[:, kc * 128:(kc + 1) * 128], self.ident_b[:],
                     ["mbf", "ident_b"], ["ps0"])
            S.copy("act", mT[:], V(ptb, 0, [[128, NKC], [1, 128]]), ["ps0"], ["mT"])
            if cfg.debug and "MT" in cfg.debug:
                S.dma("act", dr["MT"][:, :, t0:t0 + 128].rearrange("k p t -> p k t"), mT[:], ["mT"], [])
            xo = x1[b]
            for hf_ in range(2):
                n0 = hf_ * 512
                pb, pk = ps[1 + hf_], f"ps{1 + hf_}"
                for kc in range(NKC):
                    S.mm(pb[:], mT[:, kc, :], wout[:, kc, n0:n0 + 512], ["mT", "wout"], [pk],
                         start=(kc == 0), stop=(kc == NKC - 1))
                S.tt("dve", tmp[:, n0:n0 + 512], pb[:], GA1[g][:, n0:n0 + 512], ALU.mult, [pk, GA1[g].name],
                     [("ctmp", hf_)])
                S.tt("pool", xo[:, n0:n0 + 512], x[:, n0:n0 + 512], tmp[:, n0:n0 + 512], ALU.add,
                     [x.name, ("ctmp", hf_)], [xo.name])
            S.dma("act", dr["X1"][t0:t0 + 128, :], xo[:], [xo.name], [("X1", t0)])
    S.barrier()


def _phase_C2(self, l):
    S, dr, cfg, ps = self.S, self.dr, self.cfg, self.ps
    last = (l == cfg.depth - 1)
    with ExitStack() as ph:
        sb = lambda name, shape, dt=F32: self.sb(ph, name, shape, dt)
        w1 = sb("w1", [128, NKC, DFF], BF16)
        w2 = sb("w2", [128, DFF // 128, D], BF16)
        for kc in range(NKC):
            S.dma("pool", w1[:, kc, :], dr["mlp_w1"][l, kc * 128:(kc + 1) * 128, :], [], [("w1", kc)])
        for q4 in range(4):
            S.dma("pool", w2[:, q4 * 8:(q4 + 1) * 8, :],
                  dr["mlp_w2"][l, q4 * 1024:(q4 + 1) * 1024, :].rearrange("(f p) n -> p f n", p=128), [], [("w2", q4)])
        w1k = [("w1", kc) for kc in range(NKC)]
        G2 = sb("G2", [128, D])
        SH2 = sb("SH2", [128, D])
        GA2 = sb("GA2", [128, D])
        x1 = sb("x1s", [128, 2, D])
        hb = sb("h2b", [128, D], BF16)
        hT = sb("h2T", [128, NKC, 256], BF16)
        hid = [sb(f"hid{i}", [128, 256], BF16) for i in range(3)]
        rl = [sb(f"rl{i}", [128, 256]) for i in range(3)]
        scr = (sb("ss2", [128, 1]), sb("lnm2", [128, 1]), sb("rstd2", [128, 1]), sb("junk2", [128, D], BF16),
               sb("tmp2", [128, D]))
        xo = [sb(f"xo{i}", [128, D]) for i in range(2)]
        curg = None
        nst = cfg.nt // 256
        nh = 0
        for si in range(nst):
            T0 = si * 256
            g = self.group_of(T0)
            if g != curg:
                curg = g
                for t, idx in ((G2, 4), (SH2, 3), (GA2, 5)):
                    S.dma("sp", t[:], dr["MODS"][l, g, idx:idx + 1, :].partition_broadcast(128), [("MODS", l)], [t.name])
            for st in range(2):
                t0 = T0 + st * 128
                S.dma("sp", x1[:, st, :], dr["X1"][t0:t0 + 128, :], [("X1", t0)], [("x1s", st)])
                self.emit_norm_mod(x1[:, st, :], ("x1s", st), G2, SH2, hb[:], "h2b", scr, l)
                ptb = ps[7][:].bitcast(BF16)
                for kc in range(NKC):
                    S.tr(ptb[:, kc * 128:(kc + 1) * 128], hb[:, kc * 128:(kc + 1) * 128], self.ident_b[:],
                         ["h2b", "ident_b"], ["ps7"])
                S.copy("act", V(hT[:], st * 128, [[256, NKC], [1, 128]]), V(ptb, 0, [[128, NKC], [1, 128]]),
                       ["ps7"], ["h2T"])
            NF = DFF // 128

            def emit_w1(fc):
                pb, pk = ps[4 + fc % 3], f"ps{4 + fc % 3}"
                for kc in range(NKC):
                    S.mm(pb[:, 0:256], w1[:, kc, fc * 128:(fc + 1) * 128], hT[:, kc, :], ["h2T", w1k[kc]], [pk],
                         start=(kc == 0), stop=(kc == NKC - 1))
                hd = hid[fc % 3]
                r32 = rl[fc % 3]
                S.act(r32[:], pb[:, 0:256], AF.Relu, [pk], [r32.name])
                S.tt("dve" if fc % 2 == 0 else "pool", hd[:], r32[:], r32[:], ALU.mult, [r32.name], [hd.name])

            def emit_w2(fc):
                hd = hid[fc % 3]
                for st in range(2):
                    for hf_ in range(2):
                        pi = st * 2 + hf_
                        S.mm(ps[pi][:], hd[:, st * 128:(st + 1) * 128], w2[:, fc, hf_ * 512:(hf_ + 1) * 512],
                             [hd.name, ("w2", fc // 8)], [f"ps{pi}"], start=(fc == 0), stop=(fc == NF - 1))

            emit_w1(0)
            for fc in range(NF):
                if fc + 1 < NF:
                    emit_w1(fc + 1)
                emit_w2(fc)
            for st in range(2):
                t0 = T0 + st * 128
                o = xo[st]
                for hf_ in range(2):
                    n0 = hf_ * 512
                    pi = st * 2 + hf_
                    S.tt("dve", scr[4][:, n0:n0 + 512], ps[pi][:], GA2[:, n0:n0 + 512], ALU.mult, [f"ps{pi}", GA2.name],
                         [("tmp2h", hf_)])
                    S.tt("pool", o[:, n0:n0 + 512], x1[:, st, n0:n0 + 512], scr[4][:, n0:n0 + 512], ALU.add,
                         [("x1s", st), ("tmp2h", hf_)], [o.name])
                if not last:
                    S.dma("act", dr["XS"][t0:t0 + 128, :], o[:], [o.name], [("X", l + 1, t0)])
                elif t0 < cfg.np_tok:
                    S.dma("act", dr["y_prompt"][t0:t0 + 128, :], o[:], [o.name], [])
                else:
                    ts_ = t0 - cfg.np_tok
                    S.dma("act", dr["y_sample"][ts_:ts_ + 128, :], o[:], [o.name], [])
    S.barrier()


MK.phase_C1 = _phase_C1
MK.phase_C2 = _phase_C2


def build_program(cfg):
    mk = MK(cfg)
    mk.declare()
    mk.phase_mod()
    for l in range(cfg.depth):
        mk.phase_A(l)
        mk.phase_att(l, "A")
        mk.phase_att(l, "D")
        mk.phase_lru(l)
        mk.phase_gdn(l)
        mk.phase_C1(l)
        mk.phase_C2(l)
    mk.S.emit()
    return mk


_CACHE = {}


def kernel(**inputs):
    n_cores = 8
    B = inputs["x_prompt"].shape[0]
    cfg = Cfg(n_pseq=B // n_cores, t_s=inputs["x_sample"].shape[1], depth=DEPTH)
    if "mk" not in _CACHE:
        _CACHE["mk"] = build_program(cfg)
    mk = _CACHE["mk"]
    consts = const_inputs(cfg)
    maps = make_in_maps(inputs, cfg, n_cores, consts)
    res = run_bass_kernel_spmd(mk.nc, maps, core_ids=list(range(n_cores)))
    R = res.results
    nps = cfg.n_pseq
    nsamp = inputs["x_sample"].shape[0]
    grp = n_cores // nsamp
    y_prompt = np.concatenate([np.asarray(r["y_prompt"]).reshape(nps, T_P, D) for r in R], axis=0)
    y_sample = np.stack([np.asarray(R[s * grp]["y_sample"]) for s in range(nsamp)], axis=0)
    outs = [y_prompt.astype(np.float32), y_sample.astype(np.float32)]
    for n in ("new_a_k", "new_a_v", "new_d_k", "new_d_v"):
        outs.append(np.concatenate([np.asarray(r[n]).reshape(nps, DEPTH, T_P, 2, HD) for r in R], axis=0).astype(np.float32))
    outs.append(np.concatenate([np.asarray(r["new_lru"]) for r in R], axis=0).astype(np.float32))
    outs.append(np.concatenate([np.asarray(r["new_gdn"]) for r in R], axis=0).astype(np.float32))
    return tuple(outs)
```
